# Optimizing a Trainium2 kernel written in Bass

```python
import jax, jax.numpy as jnp
from jax import lax
import numpy as np

D_MODEL = 1024
BATCH = 32
SEQ = 2048
DEPTH = 1
DEC_BATCH = 4
DEC_SEQ = 8192
PAST_LEN = 128

F32 = jnp.float32
RMS_EPS = 1e-6
N_MEM = 256
MLA_HEADS = 8
MLA_NOPE = 64
MLA_ROPE = 32
MLA_QK = MLA_NOPE + MLA_ROPE
MLA_V = 64
Q_LORA = 384
KV_LORA = 256
ROPE_THETA = 10000.0
Q_BLOCK = 128
RW_HEADS = 8
RW_HEAD = 64
RW_DIM = RW_HEADS * RW_HEAD
DECAY_LORA = 64
AAA_LORA = 64
GATE_LORA = 128
RW_COLS = 3 * RW_DIM + 2 * DECAY_LORA + 2 * AAA_LORA + GATE_LORA
LNX_EPS = 64e-5
X_HEADS = 4
X_HEAD = 128
X_DIM = X_HEADS * X_HEAD
N_BRANCH = 3
MLA_COLS = Q_LORA + KV_LORA + MLA_ROPE
IN_COLS = MLA_COLS + RW_COLS + X_DIM + N_BRANCH * D_MODEL
D_FF = 2816

kernel_name = 'hybrid_mla_rwkv7_memxattn_convffn_encoder'


def rms_norm(x, g, eps=RMS_EPS):
    xf = x.astype(F32)
    y = xf * lax.rsqrt(jnp.mean(xf * xf, axis=-1, keepdims=True) + eps)
    return (y * g.astype(F32)).astype(x.dtype)


def shift_prev(x):
    return jnp.pad(x, ((0, 0), (1, 0), (0, 0)))[:, :-1]


def shift_next(x):
    return jnp.pad(x, ((0, 0), (0, 1), (0, 0)))[:, 1:]


def rope(x, T):
    half = MLA_ROPE // 2
    inv = jnp.power(ROPE_THETA, -jnp.arange(half, dtype=F32) / half)
    ang = jnp.arange(T, dtype=F32)[:, None] * inv[None, :]
    cos = jnp.cos(ang)[None, :, None, :]
    sin = jnp.sin(ang)[None, :, None, :]
    x1 = x[..., :half].astype(F32)
    x2 = x[..., half:].astype(F32)
    return jnp.concatenate([x1 * cos - x2 * sin, x2 * cos + x1 * sin], axis=-1).astype(x.dtype)


def blocked_attention(q, k, v, scale):
    B, T, H, Dq = q.shape
    nb = T // Q_BLOCK
    qb = jnp.moveaxis(q.reshape(B, nb, Q_BLOCK, H, Dq), 1, 0)

    def one(qblk):
        s = jnp.einsum('bqhd,bkhd->bhqk', qblk, k).astype(F32) * scale
        p = jax.nn.softmax(s, axis=-1)
        return jnp.einsum('bhqk,bkhd->bqhd', p.astype(v.dtype), v)

    o = lax.map(one, qb)
    return jnp.moveaxis(o, 0, 1).reshape(B, T, H, v.shape[-1])


def wkv7_scan(r, decay, k, v, kk, b, reverse):
    B, T, H, N = r.shape
    xs = tuple(jnp.moveaxis(a, 1, 0) for a in (r, decay, k, v, kk, b))

    def step(S, inp):
        r_t, w_t, k_t, v_t, kk_t, b_t = inp
        sa = jnp.einsum('bhvk,bhk->bhv', S, kk_t)
        S = S * w_t[:, :, None, :] - sa[..., None] * b_t[:, :, None, :] + v_t[..., None] * k_t[:, :, None, :]
        y = jnp.einsum('bhvk,bhk->bhv', S, r_t)
        return S, y

    S0 = jnp.zeros((B, H, N, N), F32)
    _, ys = lax.scan(step, S0, xs, reverse=reverse)
    return jnp.moveaxis(ys, 0, 1)


def rwkv7_direction(r, k, v, kk, wl, al, w0, w2, a0, a2, k_a, r_k, reverse):
    B, T, _ = r.shape
    hs = lambda t: t.reshape(B, T, RW_HEADS, RW_HEAD)
    w = -jax.nn.softplus(-(w0.astype(F32) + jnp.tanh(wl) @ w2.astype(F32))) - 0.5
    decay = jnp.exp(-jnp.exp(w))
    a = jax.nn.sigmoid(a0.astype(F32) + al @ a2.astype(F32))
    kd = k * (1.0 + (a - 1.0) * k_a.astype(F32))
    rh, kdh, vh = hs(r), hs(kd), hs(v)
    y = wkv7_scan(rh, hs(decay), kdh, vh, kk, kk * hs(a), reverse)
    bonus = jnp.sum(rh * kdh * r_k.astype(F32), axis=-1, keepdims=True) * vh
    return y + bonus


def encoder_layer(x, mem, p, l):
    B, T, _ = x.shape
    h = rms_norm(x, p['norm_mix_g'][l])
    proj = h @ p['w_in'][l]
    idx = [int(i) for i in np.cumsum([Q_LORA, KV_LORA, MLA_ROPE, RW_COLS, X_DIM])]
    c_q, c_kv, k_r, rw, xq, gate_logits = jnp.split(proj, idx, axis=-1)

    H = MLA_HEADS
    q = (rms_norm(c_q, p['q_norm_g'][l]) @ p['w_uq'][l]).reshape(B, T, H, MLA_QK)
    kv = (rms_norm(c_kv, p['kv_norm_g'][l]) @ p['w_ukv'][l]).reshape(B, T, H, MLA_NOPE + MLA_V)
    k_nope, v_a = kv[..., :MLA_NOPE], kv[..., MLA_NOPE:]
    k = jnp.concatenate([k_nope, jnp.broadcast_to(k_r[:, :, None, :], (B, T, H, MLA_ROPE))], axis=-1)
    q = rms_norm(q, p['mla_qn_g'][l])
    k = rms_norm(k, p['mla_kn_g'][l])
    q = jnp.concatenate([q[..., :MLA_NOPE], rope(q[..., MLA_NOPE:], T)], axis=-1)
    k = jnp.concatenate([k[..., :MLA_NOPE], rope(k[..., MLA_NOPE:], T)], axis=-1)
    o_a = blocked_attention(q, k, v_a, MLA_QK ** -0.5).reshape(B, T, H * MLA_V) @ p['w_o_a'][l]

    rwf = rw.astype(F32)
    rwf = rwf + p['mu_prev'][l].astype(F32) * (shift_prev(rwf) - rwf) + p['mu_next'][l].astype(F32) * (shift_next(rwf) - rwf)
    idx2 = [int(i) for i in np.cumsum([RW_DIM, RW_DIM, RW_DIM, DECAY_LORA, DECAY_LORA, AAA_LORA, AAA_LORA])]
    r7, k7, v7, wl_f, wl_b, al_f, al_b, gl = jnp.split(rwf, idx2, axis=-1)
    kk = (k7 * p['k_k'][l].astype(F32)).reshape(B, T, RW_HEADS, RW_HEAD)
    kk = kk / jnp.maximum(jnp.sqrt(jnp.sum(kk * kk, axis=-1, keepdims=True)), 1e-12)
    y_f = rwkv7_direction(r7, k7, v7, kk, wl_f, al_f, p['w0_f'][l], p['w2_f'][l], p['a0_f'][l], p['a2_f'][l], p['k_a'][l], p['r_k'][l], False)
    y_b = rwkv7_direction(r7, k7, v7, kk, wl_b, al_b, p['w0_b'][l], p['w2_b'][l], p['a0_b'][l], p['a2_b'][l], p['k_a'][l], p['r_k'][l], True)
    y7 = y_f + y_b
    mu = jnp.mean(y7, axis=-1, keepdims=True)
    var = jnp.mean(jnp.square(y7 - mu), axis=-1, keepdims=True)
    y7 = ((y7 - mu) * lax.rsqrt(var + LNX_EPS)).reshape(B, T, RW_DIM)
    y7 = y7 * p['lnx_g'][l].astype(F32) + p['lnx_b'][l].astype(F32)
    g7 = jax.nn.sigmoid(gl) @ p['g2'][l].astype(F32)
    o_b = (y7 * g7).astype(x.dtype) @ p['w_o_b'][l]

    m = rms_norm(mem, p['mem_norm_g'][l])
    mkv = (m @ p['w_mkv'][l]).reshape(B, N_MEM, X_HEADS, 2 * X_HEAD)
    mk, mv = mkv[..., :X_HEAD], mkv[..., X_HEAD:]
    xqh = rms_norm(xq.reshape(B, T, X_HEADS, X_HEAD), p['x_qn_g'][l])
    mk = rms_norm(mk, p['x_kn_g'][l])
    s = jnp.einsum('bqhd,bkhd->bhqk', xqh, mk).astype(F32) * (X_HEAD ** -0.5)
    pr = jax.nn.softmax(s, axis=-1)
    o_c = jnp.einsum('bhqk,bkhd->bqhd', pr.astype(mv.dtype), mv).reshape(B, T, X_DIM) @ p['w_o_c'][l]

    gates = jax.nn.sigmoid(gate_logits.astype(F32)).reshape(B, T, N_BRANCH, D_MODEL).astype(x.dtype)
    merged = gates[:, :, 0] * o_a + gates[:, :, 1] * o_b + gates[:, :, 2] * o_c
    x = x + merged @ p['w_out'][l]

    h2 = rms_norm(x, p['norm_ffn_g'][l])
    up = h2 @ p['w_up'][l]
    u_gate, u_val = up[..., :D_FF], up[..., D_FF:]
    cw = p['conv_w'][l]
    c = cw[0] * shift_prev(u_gate) + cw[1] * u_gate + cw[2] * shift_next(u_gate) + p['conv_b'][l]
    act = jax.nn.gelu(c, approximate=False) * u_val
    return x + act @ p['w_down'][l]


def setup_inputs(seed: int = 0) -> dict:
    key = jax.random.key(seed)
    ks = iter(jax.random.split(key, 64))

    def nrm(shape, scale):
        return jax.random.normal(next(ks), shape, F32) * scale

    def gain(shape):
        return 1.0 + nrm(shape, 0.1)

    L = DEPTH
    return {
        'x_prompt': nrm((BATCH, SEQ, D_MODEL), 1.0),
        'x_sample': nrm((DEC_BATCH, DEC_SEQ, D_MODEL), 1.0),
        'mem_prompt': nrm((BATCH, N_MEM, D_MODEL), 1.0),
        'mem_sample': nrm((DEC_BATCH, N_MEM, D_MODEL), 1.0),
        'norm_mix_g': gain((L, D_MODEL)),
        'w_in': nrm((L, D_MODEL, IN_COLS), D_MODEL ** -0.5),
        'q_norm_g': gain((L, Q_LORA)),
        'w_uq': nrm((L, Q_LORA, MLA_HEADS * MLA_QK), Q_LORA ** -0.5),
        'kv_norm_g': gain((L, KV_LORA)),
        'w_ukv': nrm((L, KV_LORA, MLA_HEADS * (MLA_NOPE + MLA_V)), KV_LORA ** -0.5),
        'mla_qn_g': gain((L, MLA_QK)),
        'mla_kn_g': gain((L, MLA_QK)),
        'w_o_a': nrm((L, MLA_HEADS * MLA_V, D_MODEL), (MLA_HEADS * MLA_V) ** -0.5),
        'mu_prev': 0.3 + nrm((L, RW_COLS), 0.1),
        'mu_next': 0.3 + nrm((L, RW_COLS), 0.1),
        'w0_f': -2.0 + nrm((L, RW_DIM), 1.0),
        'w2_f': nrm((L, DECAY_LORA, RW_DIM), 0.5 * DECAY_LORA ** -0.5),
        'a0_f': nrm((L, RW_DIM), 0.1),
        'a2_f': nrm((L, AAA_LORA, RW_DIM), 0.5 * AAA_LORA ** -0.5),
        'w0_b': -2.0 + nrm((L, RW_DIM), 1.0),
        'w2_b': nrm((L, DECAY_LORA, RW_DIM), 0.5 * DECAY_LORA ** -0.5),
        'a0_b': nrm((L, RW_DIM), 0.1),
        'a2_b': nrm((L, AAA_LORA, RW_DIM), 0.5 * AAA_LORA ** -0.5),
        'g2': nrm((L, GATE_LORA, RW_DIM), GATE_LORA ** -0.5),
        'k_k': 0.85 + nrm((L, RW_DIM), 0.05),
        'k_a': 1.0 + nrm((L, RW_DIM), 0.1),
        'r_k': nrm((L, RW_HEADS, RW_HEAD), 0.1),
        'lnx_g': gain((L, RW_DIM)),
        'lnx_b': nrm((L, RW_DIM), 0.02),
        'w_o_b': nrm((L, RW_DIM, D_MODEL), RW_DIM ** -0.5),
        'mem_norm_g': gain((L, D_MODEL)),
        'w_mkv': nrm((L, D_MODEL, 2 * X_DIM), D_MODEL ** -0.5),
        'x_qn_g': gain((L, X_HEAD)),
        'x_kn_g': gain((L, X_HEAD)),
        'w_o_c': nrm((L, X_DIM, D_MODEL), X_DIM ** -0.5),
        'w_out': nrm((L, D_MODEL, D_MODEL), D_MODEL ** -0.5),
        'norm_ffn_g': gain((L, D_MODEL)),
        'w_up': nrm((L, D_MODEL, 2 * D_FF), D_MODEL ** -0.5),
        'conv_w': nrm((L, 3, D_FF), 3 ** -0.5),
        'conv_b': nrm((L, D_FF), 0.02),
        'w_down': nrm((L, D_FF, D_MODEL), D_FF ** -0.5),
    }


def reference(x_prompt, x_sample, mem_prompt, mem_sample, norm_mix_g, w_in, q_norm_g, w_uq, kv_norm_g, w_ukv,
              mla_qn_g, mla_kn_g, w_o_a, mu_prev, mu_next, w0_f, w2_f, a0_f, a2_f, w0_b, w2_b, a0_b, a2_b,
              g2, k_k, k_a, r_k, lnx_g, lnx_b, w_o_b, mem_norm_g, w_mkv, x_qn_g, x_kn_g, w_o_c, w_out,
              norm_ffn_g, w_up, conv_w, conv_b, w_down):
    p = dict(norm_mix_g=norm_mix_g, w_in=w_in, q_norm_g=q_norm_g, w_uq=w_uq, kv_norm_g=kv_norm_g, w_ukv=w_ukv,
             mla_qn_g=mla_qn_g, mla_kn_g=mla_kn_g, w_o_a=w_o_a, mu_prev=mu_prev, mu_next=mu_next,
             w0_f=w0_f, w2_f=w2_f, a0_f=a0_f, a2_f=a2_f, w0_b=w0_b, w2_b=w2_b, a0_b=a0_b, a2_b=a2_b,
             g2=g2, k_k=k_k, k_a=k_a, r_k=r_k, lnx_g=lnx_g, lnx_b=lnx_b, w_o_b=w_o_b,
             mem_norm_g=mem_norm_g, w_mkv=w_mkv, x_qn_g=x_qn_g, x_kn_g=x_kn_g, w_o_c=w_o_c, w_out=w_out,
             norm_ffn_g=norm_ffn_g, w_up=w_up, conv_w=conv_w, conv_b=conv_b, w_down=w_down)
    y_prompt = x_prompt
    y_sample = x_sample
    for l in range(DEPTH):
        y_prompt = encoder_layer(y_prompt, mem_prompt, p, l)
        y_sample = encoder_layer(y_sample, mem_sample, p, l)
    return (y_prompt, y_sample)
```

```python
import contextlib
import os
KVAR = os.environ.get('KVAR', '')
import numpy as np
import concourse.bass as bass
import concourse.mybir as mybir
from concourse.bass_utils import run_bass_kernel_spmd

F32 = mybir.dt.float32
BF16 = mybir.dt.bfloat16
AF = mybir.ActivationFunctionType
ALU = mybir.AluOpType
AX = mybir.AxisListType


class Buf:
    __slots__ = ("t", "lw", "rd", "name", "excl")

    def __init__(self, t, name, excl=False):
        self.t = t
        self.name = name
        self.excl = excl
        self.lw = None
        self.rd = {}


class KB:
    ENG = ("pe", "act", "dve", "pool", "sp")
    NDMA = 24

    def __init__(self, nc):
        self.nc = nc
        self.es = contextlib.ExitStack()
        self.E = {"pe": nc.tensor, "act": nc.scalar, "dve": nc.vector, "pool": nc.gpsimd, "sp": nc.sync}
        self.sems = {}
        self.cnt = {}
        for e in self.ENG:
            self.sems[e] = self.es.enter_context(nc.semaphore("s_" + e))
            self.cnt[e] = 0
        self.dsem = []
        for i in range(self.NDMA):
            key = "d%d" % i
            self.sems[key] = self.es.enter_context(nc.semaphore("s_" + key))
            self.cnt[key] = 0
            self.dsem.append(key)
        self.dnext = 0
        self.waited = {e: {} for e in self.ENG}
        self.drams = {}
        self.out_events = []
        self.n_inst = 0

    def sb(self, name, shape, dt):
        self.uid = getattr(self, "uid", 0) + 1
        name = "%s_%d" % (name, self.uid)
        t = self.es.enter_context(self.nc.sbuf_tensor(name, list(shape), dt))
        return Buf(t, name)

    def ps(self, name, shape, dt):
        t = self.es.enter_context(self.nc.psum_tensor(name, list(shape), dt))
        return Buf(t, name)

    def dram(self, name):
        if name not in self.drams:
            self.drams[name] = Buf(None, name)
        return self.drams[name]

    def _wait(self, eng, deps):
        w = self.waited[eng]
        best = {}
        for d in deps:
            if d is None:
                continue
            key, val = d
            if eng == "pe" and key == "pe":
                continue
            if w.get(key, 0) >= val:
                continue
            if best.get(key, 0) < val:
                best[key] = val
        for key, val in best.items():
            w[key] = val
            sem = self.sems[key]
            self.E[eng].wait_ge(sem, val)

    def _deps(self, r, w):
        deps = []
        for b in r:
            deps.append(b.lw)
        for b in w:
            deps.append(b.lw)
            for key, val in b.rd.items():
                deps.append((key, val))
        return deps

    def _commit(self, ev, r, w):
        for b in w:
            b.lw = ev
            b.rd = {}
        for b in r:
            if b.rd.get(ev[0], 0) < ev[1]:
                b.rd[ev[0]] = ev[1]

    def op(self, eng, fn, r=(), w=(), inc=True):
        if eng != "pe":
            ex = [b for b in r if b.excl and b not in w]
            if ex:
                w = list(w) + ex
        self._wait(eng, self._deps(r, w))
        if not inc:
            assert eng == "pe"
            ev = (eng, self.cnt[eng] + 1)
            fn(self.E[eng])
            self._commit(ev, r, w)
            self.n_inst += 1
            return ev
        self.cnt[eng] += 1
        ev = (eng, self.cnt[eng])
        sem = self.sems[eng]
        fn(self.E[eng]).then_inc(sem, 1)
        self._commit(ev, r, w)
        self.n_inst += 1
        return ev

    def dma(self, eng, out, in_, r=(), w=(), is_out=False):
        key = self.dsem[self.dnext]
        self.dnext = (self.dnext + 1) % self.NDMA
        deps = self._deps(r, w)
        if self.cnt[key] > 0:
            deps.append((key, self.cnt[key]))
        self._wait(eng, deps)
        self.cnt[key] += 16
        ev = (key, self.cnt[key])
        sem = self.sems[key]
        self.E[eng].dma_start(out=out, in_=in_).then_inc(sem, 16)
        self._commit(ev, r, w)
        if is_out:
            self.out_events.append(ev)
        self.n_inst += 1
        return ev

    def act(self, out, in_, func, r=(), w=(), **kw):
        return self.op("act", lambda e: e.activation(out=out, in_=in_, func=func, **kw), r=r, w=w)

    def mm(self, out, lhsT, rhs, start, stop, r=(), w=(), inc=True):
        return self.op("pe", lambda e: e.matmul(out, lhsT, rhs, start=start, stop=stop), r=r, w=w, inc=inc)

    def tr(self, out, in_, ident, r=(), w=(), inc=True):
        return self.op("pe", lambda e: e.transpose(out, in_, ident.t[:]), r=list(r) + [ident], w=w, inc=inc)

    def make_ident(self, ident, n=128):
        self.op("pool", lambda e: e.memset(ident.t[:], 1.0), w=[ident])
        self.op("pool", lambda e: e.affine_select(
            out=ident.t[:], in_=ident.t[:], pattern=[[-1, n]], compare_op=ALU.is_equal,
            fill=0.0, base=0, channel_multiplier=1), r=[ident], w=[ident])

    def finish(self):
        nc = self.nc
        final = {}
        for key, val in self.out_events:
            final[key] = max(final.get(key, 0), val)
        for key in self.dsem:
            if self.cnt[key] > 0:
                final[key] = max(final.get(key, 0), self.cnt[key])
        for e in self.ENG:
            if self.cnt[e] > 0:
                final[e] = self.cnt[e]
        for key, val in final.items():
            self.E["sp"].wait_ge(self.sems[key], val)
        self.es.close()


    def ring(self, name, shape, dt, n):
        return Ring([self.sb("%s%d" % (name, i), shape, dt) for i in range(n)])

    def barrier(self):
        for e in self.ENG:
            deps = [(key, c) for key, c in self.cnt.items() if c > 0]
            w = self.waited[e]
            for key, val in deps:
                if key == e and e != "sp":
                    pass
                if w.get(key, 0) >= val:
                    continue
                w[key] = val
                sem = self.sems[key]
                self.E[e].wait_ge(sem, val)

    @contextlib.contextmanager
    def scope(self):
        old = self.es
        self.es = contextlib.ExitStack()
        try:
            yield
        finally:
            self.barrier()
            self.es.close()
            self.es = old


class Ring:
    def __init__(self, bufs):
        self.bufs = bufs
        self.i = 0

    def next(self):
        b = self.bufs[self.i]
        self.i = (self.i + 1) % len(self.bufs)
        return b


D = 1024
NMEM = 256
IN_COLS = 6176
DFF = 2816
NFC = 22
VEC_SPEC = [("g_mix", 8), ("g_q", 3), ("g_kv", 2), ("gq_r", 1), ("gq_sw", 1), ("gk_r", 1), ("gk_sw", 1),
            ("mu_p", 15), ("mu_n", 15), ("w0_f", 4), ("w0_b", 4), ("a0_f", 4), ("a0_b", 4), ("k_k", 4),
            ("k_a", 4), ("r_k", 4), ("g_mem", 8), ("g_xq", 1), ("g_xk", 1), ("g_ffn", 8),
            ("cw0", NFC), ("cw1", NFC), ("cw2", NFC), ("cb", NFC)]
VEC_OFF = {}
_o = 0
for _n, _c in VEC_SPEC:
    VEC_OFF[_n] = (_o, _c)
    _o += _c
NVEC = _o
DECAY_C = -0.6065306597126334
EPS = 1e-6


class StopBuild(Exception):
    pass


def build_program(units, tmax, debug=False, stop=None):
    try:
        return _build_program(units, tmax, debug, stop)
    except StopBuild as ex:
        nc, k = ex.args
        k.finish()
        return nc, k, {}


def _build_program(units, tmax, debug=False, stop=None):
    nc = bass.Bass("TRN2", target_bir_lowering=False)
    uspec = [u if isinstance(u, tuple) else (u, u, u, u // 512) for u in units]
    units = [u[0] for u in uspec]
    OWN = [u[1] for u in uspec]
    MIX = [u[2] for u in uspec]
    NQ = [u[3] for u in uspec]
    NU = len(units)
    TT = sum(units)
    tok0 = [sum(units[:i]) for i in range(NU)]
    own0 = [sum(OWN[:i]) for i in range(NU)]
    TOWN = sum(OWN)
    tlens = sorted(set(units))

    def din(name, shape, dt=F32):
        return nc.dram_tensor(name, list(shape), dt, kind="ExternalInput").ap()

    def dscr(name, shape, dt=F32):
        return nc.dram_tensor(name, list(shape), dt, kind="ExternalOutput" if debug else "Internal").ap()

    xs = din("xs", [TT, D])
    mems = din("mems", [NU * NMEM, D])
    ropec = {t_: din("ropec%d" % t_, [32, t_]) for t_ in tlens}
    ropes = {t_: din("ropes%d" % t_, [32, t_]) for t_ in tlens}
    vecs = din("vecs", [128, NVEC])
    rowb = din("rowb", [2, 128, 512])
    w_in = din("w_in", [D, IN_COLS + 32])
    w_uq = din("w_uq", [384, 1024])
    w_ukvk = din("w_ukvk", [256, 768])
    w_ukvv = din("w_ukvv", [256, 512])
    w_mk = din("w_mk", [D, 512])
    w_mv = din("w_mv", [D, 512])
    w2 = din("w2", [128, 512])
    a2 = din("a2", [128, 512])
    g2 = din("g2", [128, 512])
    w_oa = din("w_oa", [512, D])
    w_ob = din("w_ob", [512, D])
    w_oc = din("w_oc", [512, D])
    w_out = din("w_out", [D, D])
    w_up = din("w_up", [D, 2 * DFF])
    w_dn = din("w_dn", [DFF, D])
    ys = nc.dram_tensor("ys", [TOWN, D], F32, kind="ExternalOutput").ap()

    xT_s = dscr("xT_s", [D, TT])
    rw_s = dscr("rw_s", [1920, TT])
    gate_s = dscr("gate_s", [3072, TT], BF16)
    qT_s = dscr("qT_s", [768, TT], BF16)
    kT_s = dscr("kT_s", [768, TT], BF16)
    v_s = dscr("v_s", [TT, 512], BF16)
    ocT_s = dscr("ocT_s", [512, TT], BF16)
    oa_s = dscr("oa_s", [TT, 512])
    yf_s = dscr("yf_s", [TT, 512])
    x1T_s = dscr("x1T_s", [D, TT])

    k = KB(nc)
    DR = k.dram

    dbg_list = []

    def dbg(name, buf, ap):
        if not debug or name in dbg_list:
            return
        dbg_list.append(name)
        shp = list(ap.shape)
        dt_ = nc.dram_tensor("dbg_" + name, shp, ap.dtype, kind="ExternalOutput").ap()
        k.dma("sp", dt_, ap, r=[buf], w=[DR("dbg_" + name)], is_out=True)

    def chk(name):
        if stop == name:
            raise StopBuild(nc, k)

    pst = k.ps("pst", [128, 8, 512], F32)
    banks = [Buf(pst.t, "bank%d" % i, excl=True) for i in range(8)]

    def bk(i):
        return banks[i].t[:, i, :]

    ident = k.sb("ident", [128, 128], F32)
    k.make_ident(ident)
    identb = k.sb("identb", [128, 128], BF16)
    k.op("dve", lambda e: e.tensor_copy(identb.t[:], ident.t[:]), r=[ident], w=[identb])
    onesb = k.sb("onesb", [128, 128], BF16)
    k.op("pool", lambda e: e.memset(onesb.t[:], 1.0), w=[onesb])
    VEC = k.sb("VEC", [128, NVEC], F32)
    k.dma("sp", VEC.t[:], vecs, w=[VEC])

    def V(name, j=0, n=1, rows=slice(0, 128)):
        o, c = VEC_OFF[name]
        return VEC.t[rows, o + j:o + j + n]

    bank_rr = {}

    def nb(lo=0, hi=8):
        i = bank_rr.get((lo, hi), hi - 1)
        i = lo + ((i + 1 - lo) % (hi - lo))
        bank_rr[(lo, hi)] = i
        return i

    def rstd(out_ap, in_ap, scale, eps, r, w):
        k.act(out_ap, in_ap, AF.Sqrt, r=r, w=w, bias=eps, scale=scale)
        k.op("dve", lambda e: e.reciprocal(out_ap, out_ap), r=w, w=w)

    stage = [None]

    def load_w(dst, dst_ap_fn, src, ncols, gain=None, rows=128, eng_i=[0]):
        CH = 1024
        for c0 in range(0, ncols, CH):
            n = min(CH, ncols - c0)
            st = stage[0].next()
            k.dma("sp", st.t[0:rows, 0:n], src[:, c0:c0 + n], w=[st])
            d = dst_ap_fn(c0, n)
            if gain is not None:
                k.op("dve", lambda e: e.tensor_scalar(d, st.t[0:rows, 0:n], gain, None, ALU.mult), r=[st, VEC], w=[dst])
            else:
                eng_i[0] ^= 1
                if eng_i[0]:
                    k.op("pool", lambda e: e.tensor_copy(d, st.t[0:rows, 0:n]), r=[st], w=[dst])
                else:
                    k.op("act", lambda e: e.copy(d, st.t[0:rows, 0:n]), r=[st], w=[dst])

    def proj(Wt, hT, m0, M, n, kcs=8):
        b = nb()
        for kc in range(kcs):
            k.mm(banks[b].t[0:M, b, 0:n], Wt.t[:, kc, m0:m0 + M], hT.t[:, kc, 0:n], kc == 0, kc == kcs - 1,
                 r=[Wt, hT], w=[banks[b]])
        return b

    def make_hT(xTf, n, xsq, hT, RSTD):
        k.op("pool", lambda e: e.tensor_tensor(xsq.t[:, :, 0:n], xTf.t[:, :, 0:n], xTf.t[:, :, 0:n], ALU.mult), r=[xTf], w=[xsq])
        b = nb()
        for kc in range(8):
            k.mm(banks[b].t[:, b, 0:n], onesb.t[:], xsq.t[:, kc, 0:n], kc == 0, kc == 7, r=[onesb, xsq], w=[banks[b]])
        rstd(RSTD.t[:, 0:n], banks[b].t[:, b, 0:n], 1.0 / D, EPS, [banks[b]], [RSTD])
        k.op("dve", lambda e: e.tensor_tensor(hT.t[:, :, 0:n], xTf.t[:, :, 0:n],
                                              RSTD.t[:, 0:n].unsqueeze(1).to_broadcast([128, 8, n]), ALU.mult),
             r=[xTf, RSTD], w=[hT])

    with k.scope():
        stage[0] = k.ring("stg", [128, 1024], F32, 2)
        NA = 1920 + 3072
        WA = k.sb("WA", [128, 8, NA], BF16)
        for kc in range(8):
            load_w(WA, lambda c0, n, kc=kc: WA.t[:, kc, c0:c0 + n], w_in[kc * 128:(kc + 1) * 128, 672:2592], 1920, gain=V("g_mix", kc))
            load_w(WA, lambda c0, n, kc=kc: WA.t[:, kc, 1920 + c0:1920 + c0 + n], w_in[kc * 128:(kc + 1) * 128, 3104:6176], 3072, gain=V("g_mix", kc))
        xin = k.ring("xin", [128, D], F32, 4)
        xTf = k.sb("xTf", [128, 8, 512], F32)
        xsq = k.sb("xsq", [128, 8, 512], BF16)
        hT = k.sb("hT", [128, 8, 512], BF16)
        RSTD = k.sb("RSTD", [128, 512], F32)
        RWS = k.sb("RWS", [128, 15, 512], F32)
        GS = k.sb("GS", [128, 24, 512], BF16)
        for u in range(NU):
            for blk in range(units[u] // 512):
                t0 = tok0[u] + blk * 512
                xt = []
                for j in range(4):
                    x_ = xin.next()
                    k.dma("sp", x_.t[:], xs[t0 + j * 128:t0 + (j + 1) * 128, :], w=[x_])
                    xt.append(x_)
                for kc in range(8):
                    b = nb()
                    for j in range(4):
                        k.tr(banks[b].t[:, b, j * 128:(j + 1) * 128], xt[j].t[:, kc * 128:(kc + 1) * 128], ident, r=[xt[j]], w=[banks[b]])
                    if kc % 2 == 0:
                        k.op("act", lambda e, b=b, kc=kc: e.copy(xTf.t[:, kc, :], banks[b].t[:, b, :]), r=[banks[b]], w=[xTf])
                    else:
                        k.op("dve", lambda e, b=b, kc=kc: e.tensor_copy(xTf.t[:, kc, :], banks[b].t[:, b, :]), r=[banks[b]], w=[xTf])
                k.dma("sp", xT_s[:, t0:t0 + 512].rearrange("(c p) t -> p c t", p=128), xTf.t[:], r=[xTf], w=[DR("xT")])
                make_hT(xTf, 512, xsq, hT, RSTD)
                for j in range(15):
                    b = proj(WA, hT, j * 128, 128, 512)
                    if j % 2 == 0:
                        k.op("act", lambda e, b=b, j=j: e.copy(RWS.t[:, j, :], banks[b].t[:, b, :]), r=[banks[b]], w=[RWS])
                    else:
                        k.op("dve", lambda e, b=b, j=j: e.tensor_copy(RWS.t[:, j, :], banks[b].t[:, b, :]), r=[banks[b]], w=[RWS])
                k.dma("sp", rw_s[:, t0:t0 + 512].rearrange("(c p) t -> p c t", p=128), RWS.t[:], r=[RWS], w=[DR("rw")])
                if blk < NQ[u]:
                    for j in range(24):
                        b = proj(WA, hT, 1920 + j * 128, 128, 512)
                        k.act(GS.t[:, j, :], banks[b].t[:, b, :], AF.Sigmoid, r=[banks[b]], w=[GS])
                    k.dma("sp", gate_s[:, t0:t0 + 512].rearrange("(c p) t -> p c t", p=128), GS.t[:], r=[GS], w=[DR("gate")])

    chk("p1a")
    def headnorm_rope(prologue, g_r, g_sw, COS, SIN, dst, T_):
        SQ, RH, QN, SW, T1, T2, QB = T_
        src96, src_r, sw_ap, sw_r = prologue()
        yield
        k.act(SQ.t[0:96, :], src96, AF.Square, r=src_r, w=[SQ])
        yield
        b2 = nb()
        k.mm(banks[b2].t[0:96, b2, :], onesb.t[0:96, 0:96], SQ.t[0:96, :], True, True, r=[onesb, SQ], w=[banks[b2]])
        yield
        k.act(RH.t[0:96, :], banks[b2].t[0:96, b2, :], AF.Sqrt, r=[banks[b2]], w=[RH], bias=EPS, scale=1.0 / 96)
        yield
        k.op("dve", lambda e: e.reciprocal(RH.t[0:96, :], RH.t[0:96, :]), r=[RH], w=[RH])
        yield
        k.op("dve", lambda e: e.scalar_tensor_tensor(QN.t[0:96, :], src96, g_r, RH.t[0:96, :], ALU.mult, ALU.mult),
             r=list(src_r) + [RH, VEC], w=[QN])
        k.op("dve", lambda e: e.scalar_tensor_tensor(SW.t[0:32, :], sw_ap, g_sw, RH.t[0:32, :], ALU.mult, ALU.mult),
             r=list(sw_r) + [RH, VEC], w=[SW])
        yield
        k.op("pool", lambda e: e.tensor_tensor(T1.t[0:32, :], QN.t[0:32, :], COS.t[:, :], ALU.mult), r=[QN, COS], w=[T1])
        k.op("pool", lambda e: e.tensor_tensor(T2.t[0:32, :], SW.t[0:32, :], SIN.t[:, :], ALU.mult), r=[SW, SIN], w=[T2])
        k.act(QB.t[0:96, :], QN.t[0:96, :], AF.Copy, r=[QN], w=[QB])
        yield
        k.op("pool", lambda e: e.tensor_tensor(QB.t[0:32, :], T1.t[0:32, :], T2.t[0:32, :], ALU.add), r=[T1, T2, QB], w=[QB])
        yield
        k.dma("sp", dst, QB.t[0:96, :], r=[QB], w=[DR("qk")])

    def lockstep(gens):
        gens = list(gens)
        while gens:
            for g_ in list(gens):
                try:
                    next(g_)
                except StopIteration:
                    gens.remove(g_)

    with k.scope():
        stage[0] = k.ring("stg", [128, 1024], F32, 2)
        NB1 = 672 + 32 + 512
        WB = k.sb("WB", [128, 8, NB1], BF16)
        for kc in range(8):
            rs = slice(kc * 128, (kc + 1) * 128)
            load_w(WB, lambda c0, n, kc=kc: WB.t[:, kc, c0:c0 + n], w_in[rs, 0:672], 672, gain=V("g_mix", kc))
            load_w(WB, lambda c0, n, kc=kc: WB.t[:, kc, 672 + c0:672 + c0 + n], w_in[rs, 6176:6208], 32, gain=V("g_mix", kc))
            load_w(WB, lambda c0, n, kc=kc: WB.t[:, kc, 704 + c0:704 + c0 + n], w_in[rs, 2592:3104], 512, gain=V("g_mix", kc))
        Wuq = k.sb("Wuq", [128, 3, 1024], BF16)
        for j in range(3):
            load_w(Wuq, lambda c0, n, j=j: Wuq.t[:, j, c0:c0 + n], w_uq[j * 128:(j + 1) * 128, :], 1024)
        Wkk = k.sb("Wkk", [128, 2, 768], BF16)
        Wkv = k.sb("Wkv", [128, 2, 512], BF16)
        for j in range(2):
            load_w(Wkk, lambda c0, n, j=j: Wkk.t[:, j, c0:c0 + n], w_ukvk[j * 128:(j + 1) * 128, :], 768)
            load_w(Wkv, lambda c0, n, j=j: Wkv.t[:, j, c0:c0 + n], w_ukvv[j * 128:(j + 1) * 128, :], 512)
        Wmk = k.sb("Wmk", [128, 8, 512], BF16)
        Wmv = k.sb("Wmv", [128, 8, 512], BF16)
        for kc in range(8):
            rs = slice(kc * 128, (kc + 1) * 128)
            load_w(Wmk, lambda c0, n, kc=kc: Wmk.t[:, kc, c0:c0 + n], w_mk[rs, :], 512, gain=V("g_mem", kc))
            load_w(Wmv, lambda c0, n, kc=kc: Wmv.t[:, kc, c0:c0 + n], w_mv[rs, :], 512, gain=V("g_mem", kc))
        xTf = k.sb("xTf", [128, 8, 512], F32)
        xsq = k.sb("xsq", [128, 8, 512], BF16)
        hT = k.sb("hT", [128, 8, 512], BF16)
        RSTD = k.sb("RSTD", [128, 512], F32)
        CQ = k.sb("CQ", [128, 3, 512], F32)
        CQs = k.sb("CQs", [128, 3, 512], BF16)
        CQN = k.sb("CQN", [128, 3, 512], BF16)
        CKV = k.sb("CKV", [128, 2, 512], F32)
        CKVs = k.sb("CKVs", [128, 2, 512], BF16)
        CKVN = k.sb("CKVN", [128, 2, 512], BF16)
        RQ = k.sb("RQ", [128, 512], F32)
        KRb = k.sb("KRb", [32, 512], BF16)
        KRSW = k.sb("KRSW", [32, 512], F32)
        QSW = k.sb("QSW", [32, 512], F32)
        COS = k.sb("COS", [32, 512], F32)
        SIN = k.sb("SIN", [32, 512], F32)
        TQ = [k.ring("hn%d" % i, [96, 512], BF16 if i in (0, 6) else F32, 2) for i in range(7)]
        VB = k.ring("VB", [128, 512], BF16, 2)
        XQ = k.sb("XQ", [128, 512], F32)
        XQs = k.sb("XQs", [128, 512], BF16)
        XQN = k.sb("XQN", [128, 512], BF16)
        RHx = k.sb("RHx", [128, 512], F32)
        PTx = k.ring("PTx", [128, 512], BF16, 2)
        RD = k.sb("RD", [128, 512], F32)
        OC = k.sb("OC", [128, 4, 512], BF16)
        memt = k.ring("memt", [128, D], F32, 2)
        junk = k.sb("junk", [128, D], F32)
        ssm = k.sb("ssm", [128, 2], F32)
        MEMT = k.sb("MEMT", [128, 8, 256], F32)
        MEMTb = k.sb("MEMTb", [128, 8, 256], BF16)
        msq = k.sb("msq", [128, 8, 256], BF16)
        RSM = k.sb("RSM", [128, 256], F32)
        MKr = k.sb("MKr", [128, 256], F32)
        MKs = k.sb("MKs", [128, 256], BF16)
        RHm = k.sb("RHm", [128, 256], F32)
        MK = k.sb("MK", [128, 4, 256], BF16)
        MV = k.sb("MV", [128, 2, 512], BF16)
        sel32 = k.sb("sel32", [32, 96], BF16)
        k.op("pool", lambda e: e.memset(sel32.t[:], 0.0), w=[sel32])
        k.op("pool", lambda e: e.tensor_copy(sel32.t[0:32, 0:32], ident.t[0:32, 0:32]), r=[ident, sel32], w=[sel32])

        for u in range(NU):
            ml = []
            for mt in range(2):
                m_ = memt.next()
                k.dma("sp", m_.t[:], mems[u * NMEM + mt * 128:u * NMEM + (mt + 1) * 128, :], w=[m_])
                k.act(junk.t[:], m_.t[:], AF.Square, r=[m_], w=[junk, ssm], accum_out=ssm.t[:, mt:mt + 1])
                ml.append(m_)
            rstd(ssm.t[:, 0:2], ssm.t[:, 0:2], 1.0 / D, EPS, [ssm], [ssm])
            for kc in range(8):
                b = nb()
                for mt in range(2):
                    k.tr(banks[b].t[:, b, mt * 128:(mt + 1) * 128], ml[mt].t[:, kc * 128:(kc + 1) * 128], ident, r=[ml[mt]], w=[banks[b]])
                k.op("act", lambda e, b=b, kc=kc: e.copy(MEMT.t[:, kc, :], banks[b].t[:, b, 0:256]), r=[banks[b]], w=[MEMT])
            k.op("dve", lambda e: e.tensor_copy(MEMTb.t[:], MEMT.t[:]), r=[MEMT], w=[MEMTb])
            k.op("pool", lambda e: e.tensor_tensor(msq.t[:], MEMT.t[:], MEMT.t[:], ALU.mult), r=[MEMT], w=[msq])
            b = nb()
            for kc in range(8):
                k.mm(banks[b].t[:, b, 0:256], onesb.t[:], msq.t[:, kc, :], kc == 0, kc == 7, r=[onesb, msq], w=[banks[b]])
            rstd(RSM.t[:], banks[b].t[:, b, 0:256], 1.0 / D, EPS, [banks[b]], [RSM])
            for h in range(4):
                b = nb()
                for kc in range(8):
                    k.mm(banks[b].t[:, b, 0:256], Wmk.t[:, kc, h * 128:(h + 1) * 128], MEMTb.t[:, kc, :], kc == 0, kc == 7,
                         r=[Wmk, MEMTb], w=[banks[b]])
                k.op("dve", lambda e, b=b: e.tensor_tensor(MKr.t[:], banks[b].t[:, b, 0:256], RSM.t[:], ALU.mult), r=[banks[b], RSM], w=[MKr])
                k.op("pool", lambda e: e.tensor_tensor(MKs.t[:], MKr.t[:], MKr.t[:], ALU.mult), r=[MKr], w=[MKs])
                b2 = nb()
                k.mm(banks[b2].t[:, b2, 0:256], onesb.t[:], MKs.t[:], True, True, r=[onesb, MKs], w=[banks[b2]])
                rstd(RHm.t[:], banks[b2].t[:, b2, 0:256], 1.0 / 128, EPS, [banks[b2]], [RHm])
                k.op("dve", lambda e, h=h: e.scalar_tensor_tensor(MK.t[:, h, :], MKr.t[:], V("g_xk"), RHm.t[:], ALU.mult, ALU.mult),
                     r=[MKr, RHm, VEC], w=[MK])
            for mt in range(2):
                b = nb()
                for kc in range(8):
                    k.mm(banks[b].t[:, b, :], MEMTb.t[:, kc, mt * 128:(mt + 1) * 128], Wmv.t[:, kc, :], kc == 0, kc == 7,
                         r=[Wmv, MEMTb], w=[banks[b]])
                k.op("dve", lambda e, b=b, mt=mt: e.tensor_scalar(MV.t[:, mt, :], banks[b].t[:, b, :], ssm.t[:, mt:mt + 1], None, ALU.mult),
                     r=[banks[b], ssm], w=[MV])

            for blk in range(units[u] // 512):
                t0 = tok0[u] + blk * 512
                p0 = blk * 512
                k.dma("sp", xTf.t[:], xT_s[:, t0:t0 + 512].rearrange("(c p) t -> p c t", p=128), r=[DR("xT")], w=[xTf])
                k.dma("sp", COS.t[:], ropec[units[u]][:, p0:p0 + 512], w=[COS])
                k.dma("sp", SIN.t[:], ropes[units[u]][:, p0:p0 + 512], w=[SIN])
                isq = blk < NQ[u]
                make_hT(xTf, 512, xsq, hT, RSTD)
                for j in range(3 if isq else 0):
                    b = proj(WB, hT, j * 128, 128, 512)
                    k.op("act", lambda e, b=b, j=j: e.copy(CQ.t[:, j, :], banks[b].t[:, b, :]), r=[banks[b]], w=[CQ])
                if isq:
                    k.op("pool", lambda e: e.tensor_tensor(CQs.t[:], CQ.t[:], CQ.t[:], ALU.mult), r=[CQ], w=[CQs])
                    b = nb()
                    for j in range(3):
                        k.mm(banks[b].t[:, b, :], onesb.t[:], CQs.t[:, j, :], j == 0, j == 2, r=[onesb, CQs], w=[banks[b]])
                    rstd(RQ.t[:], banks[b].t[:, b, :], 1.0 / 384, EPS, [banks[b]], [RQ])
                    for j in range(3):
                        k.op("dve", lambda e, j=j: e.scalar_tensor_tensor(CQN.t[:, j, :], CQ.t[:, j, :], V("g_q", j), RQ.t[:], ALU.mult, ALU.mult),
                             r=[CQ, RQ, VEC], w=[CQN])
                for j in range(2):
                    b = proj(WB, hT, 384 + j * 128, 128, 512)
                    k.op("act", lambda e, b=b, j=j: e.copy(CKV.t[:, j, :], banks[b].t[:, b, :]), r=[banks[b]], w=[CKV])
                k.op("pool", lambda e: e.tensor_tensor(CKVs.t[:], CKV.t[:], CKV.t[:], ALU.mult), r=[CKV], w=[CKVs])
                b = nb()
                for j in range(2):
                    k.mm(banks[b].t[:, b, :], onesb.t[:], CKVs.t[:, j, :], j == 0, j == 1, r=[onesb, CKVs], w=[banks[b]])
                rstd(RQ.t[:], banks[b].t[:, b, :], 1.0 / 256, EPS, [banks[b]], [RQ])
                for j in range(2):
                    k.op("dve", lambda e, j=j: e.scalar_tensor_tensor(CKVN.t[:, j, :], CKV.t[:, j, :], V("g_kv", j), RQ.t[:], ALU.mult, ALU.mult),
                         r=[CKV, RQ, VEC], w=[CKVN])
                b = proj(WB, hT, 640, 32, 512)
                k.op("act", lambda e, b=b: e.copy(KRb.t[:], banks[b].t[0:32, b, :]), r=[banks[b]], w=[KRb])
                b = proj(WB, hT, 672, 32, 512)
                k.op("act", lambda e, b=b: e.copy(KRSW.t[:], banks[b].t[0:32, b, :]), r=[banks[b]], w=[KRSW])
                for h in range(8):
                    gens = []
                    if isq:
                        def pro_q(h=h):
                            bq = nb()
                            for j in range(3):
                                k.mm(banks[bq].t[0:96, bq, :], Wuq.t[:, j, h * 96:(h + 1) * 96], CQN.t[:, j, :], j == 0, j == 2, r=[Wuq, CQN], w=[banks[bq]])
                            bs = nb()
                            for j in range(3):
                                k.mm(banks[bs].t[0:32, bs, :], Wuq.t[:, j, 768 + h * 32:768 + (h + 1) * 32], CQN.t[:, j, :], j == 0, j == 2, r=[Wuq, CQN], w=[banks[bs]])
                            return banks[bq].t[0:96, bq, :], [banks[bq]], banks[bs].t[0:32, bs, :], [banks[bs]]
                        gens.append(headnorm_rope(pro_q, V("gq_r", rows=slice(0, 96)), V("gq_sw", rows=slice(0, 32)), COS, SIN,
                                                  qT_s[h * 96:(h + 1) * 96, t0:t0 + 512], [r_.next() for r_ in TQ]))

                    def pro_k(h=h):
                        bq = nb()
                        for j in range(2):
                            k.mm(banks[bq].t[0:96, bq, :], Wkk.t[:, j, h * 96:(h + 1) * 96], CKVN.t[:, j, :], j == 0, False, r=[Wkk, CKVN], w=[banks[bq]])
                        k.mm(banks[bq].t[0:96, bq, :], sel32.t[0:32, 0:96], KRb.t[0:32, :], False, True, r=[sel32, KRb], w=[banks[bq]])
                        return banks[bq].t[0:96, bq, :], [banks[bq]], KRSW.t[0:32, :], [KRSW]
                    gens.append(headnorm_rope(pro_k, V("gk_r", rows=slice(0, 96)), V("gk_sw", rows=slice(0, 32)), COS, SIN,
                                              kT_s[h * 96:(h + 1) * 96, t0:t0 + 512], [r_.next() for r_ in TQ]))
                    lockstep(gens)
                for j in range(4):
                    b = nb()
                    for kc in range(2):
                        k.mm(banks[b].t[:, b, :], CKVN.t[:, kc, j * 128:(j + 1) * 128], Wkv.t[:, kc, :], kc == 0, kc == 1, r=[CKVN, Wkv], w=[banks[b]])
                    vb = VB.next()
                    k.op("act", lambda e, b=b, vb=vb: e.copy(vb.t[:], banks[b].t[:, b, :]), r=[banks[b]], w=[vb])
                    k.dma("sp", v_s[t0 + j * 128:t0 + (j + 1) * 128, :], vb.t[:], r=[vb], w=[DR("v")])
                for h in range(4 if isq else 0):
                    b = proj(WB, hT, 704 + h * 128, 128, 512)
                    k.op("act", lambda e, b=b: e.copy(XQ.t[:], banks[b].t[:, b, :]), r=[banks[b]], w=[XQ])
                    k.op("pool", lambda e: e.tensor_tensor(XQs.t[:], XQ.t[:], XQ.t[:], ALU.mult), r=[XQ], w=[XQs])
                    b2 = nb()
                    k.mm(banks[b2].t[:, b2, :], onesb.t[:], XQs.t[:], True, True, r=[onesb, XQs], w=[banks[b2]])
                    rstd(RHx.t[:], banks[b2].t[:, b2, :], 1.0 / 128, EPS, [banks[b2]], [RHx])
                    k.op("dve", lambda e: e.scalar_tensor_tensor(XQN.t[:], XQ.t[:], V("g_xq"), RHx.t[:], ALU.mult, ALU.mult),
                         r=[XQ, RHx, VEC], w=[XQN])
                    bo = nb(0, 2)
                    bd = 2 + bo
                    for mt in range(2):
                        bs = nb(4, 8)
                        k.mm(banks[bs].t[:, bs, :], MK.t[:, h, mt * 128:(mt + 1) * 128], XQN.t[:], True, True, r=[MK, XQN], w=[banks[bs]])
                        pt = PTx.next()
                        k.act(pt.t[:], banks[bs].t[:, bs, :], AF.Exp, r=[banks[bs]], w=[pt], scale=128 ** -0.5)
                        k.mm(banks[bo].t[:, bo, :], MV.t[:, mt, h * 128:(h + 1) * 128], pt.t[:], mt == 0, mt == 1, r=[MV, pt], w=[banks[bo]])
                        k.mm(banks[bd].t[:, bd, :], onesb.t[:], pt.t[:], mt == 0, mt == 1, r=[onesb, pt], w=[banks[bd]])
                    k.op("dve", lambda e, bd=bd: e.reciprocal(RD.t[:], banks[bd].t[:, bd, :]), r=[banks[bd]], w=[RD])
                    k.op("dve", lambda e, bo=bo, h=h: e.tensor_tensor(OC.t[:, h, :], banks[bo].t[:, bo, :], RD.t[:], ALU.mult), r=[banks[bo], RD], w=[OC])
                if isq:
                    k.dma("sp", ocT_s[:, t0:t0 + 512].rearrange("(c p) t -> p c t", p=128), OC.t[:], r=[OC], w=[DR("oc")])

    chk("p1b")
    with k.scope():
        KT = k.ring("KT", [96, tmax], BF16, 2)
        VA = k.ring("VA", [128, tmax // 128, 65], BF16, 2)
        for va in VA.bufs:
            k.op("pool", lambda e, va=va: e.memset(va.t[:], 1.0), w=[va])
        QT = k.ring("QT", [96, 512], BF16, 3)
        PT = k.ring("PT", [128, 512], BF16, 4)
        RDa = k.ring("RDa", [128, 4, 1], F32, 2)
        OA = k.ring("OA", [128, 4, 64], F32, 2)
        for u in range(NU):
            T = units[u]
            nkt = T // 128
            for h in range(8):
                kt_ = KT.next()
                va = VA.next()
                k.dma("sp", kt_.t[:, 0:T], kT_s[h * 96:(h + 1) * 96, tok0[u]:tok0[u] + T], r=[DR("qk")], w=[kt_])
                k.dma("sp", va.t[:, 0:nkt, 0:64],
                      v_s[tok0[u]:tok0[u] + T, h * 64:(h + 1) * 64].rearrange("(n p) c -> p n c", p=128), r=[DR("v")], w=[va])
                for qb in range(NQ[u]):
                    t0 = tok0[u] + qb * 512
                    qt = QT.next()
                    k.dma("sp", qt.t[:], qT_s[h * 96:(h + 1) * 96, t0:t0 + 512], r=[DR("qk")], w=[qt])
                    bo = nb(6, 8)
                    AHEAD = 2
                    bsq = []

                    def issue_s(kt2):
                        bs_ = nb(0, 6)
                        k.mm(banks[bs_].t[:, bs_, :], kt_.t[:, kt2 * 128:(kt2 + 1) * 128], qt.t[:], True, True, r=[kt_, qt], w=[banks[bs_]])
                        bsq.append(bs_)

                    for kt2 in range(min(AHEAD, nkt)):
                        issue_s(kt2)
                    for kt in range(nkt):
                        if kt + AHEAD < nkt:
                            issue_s(kt + AHEAD)
                        bs = bsq.pop(0)
                        pt = PT.next()
                        k.act(pt.t[:], banks[bs].t[:, bs, :], AF.Exp, r=[banks[bs]], w=[pt], scale=96 ** -0.5)
                        for j in range(4):
                            k.op("pe", lambda e, bo=bo, j=j, pt=pt, va=va, kt=kt: e.matmul(
                                banks[bo].t[:, bo, j * 65:(j + 1) * 65], pt.t[:, j * 128:(j + 1) * 128], va.t[:, kt, :],
                                start=(kt == 0 and j == 0), stop=(kt == nkt - 1), skip_group_check=True), r=[pt, va], w=[banks[bo]])
                    ov = banks[bo].t[:, bo, 0:260].rearrange("p (j c) -> p j c", c=65)
                    rd = RDa.next()
                    oa = OA.next()
                    k.op("dve", lambda e, rd=rd, ov=ov: e.reciprocal(rd.t[:], ov[:, :, 64:65]), r=[banks[bo]], w=[rd])
                    k.op("dve", lambda e, rd=rd, ov=ov, oa=oa: e.tensor_tensor(oa.t[:], ov[:, :, 0:64], rd.t[:].to_broadcast([128, 4, 64]), ALU.mult),
                         r=[banks[bo], rd], w=[oa])
                    k.dma("sp", oa_s[t0:t0 + 512, h * 64:(h + 1) * 64].rearrange("(j p) c -> p j c", p=128), oa.t[:], r=[oa], w=[DR("oa")])

    chk("p2")
    def bc3(ap2, n):
        return ap2.unsqueeze(2).to_broadcast([128, ap2.shape[1], n])

    with k.scope():
        W2 = k.sb("W2", [128, 512], BF16)
        A2 = k.sb("A2", [128, 512], BF16)
        G2 = k.sb("G2", [128, 512], BF16)
        Woa = k.sb("Woa", [128, 4, D], BF16)
        Wob = k.sb("Wob", [128, 4, D], BF16)
        Woc = k.sb("Woc", [128, 4, D], BF16)
        Wo = k.sb("Wo", [128, 8, D], BF16)
        stage_scope = k.scope()
        stage_scope.__enter__()
        stage[0] = k.ring("stg", [128, 1024], F32, 2)
        load_w(W2, lambda c0, n: W2.t[:, c0:c0 + n], w2, 512)
        load_w(A2, lambda c0, n: A2.t[:, c0:c0 + n], a2, 512)
        load_w(G2, lambda c0, n: G2.t[:, c0:c0 + n], g2, 512)
        for j in range(4):
            load_w(Woa, lambda c0, n, j=j: Woa.t[:, j, c0:c0 + n], w_oa[j * 128:(j + 1) * 128, :], D)
            load_w(Wob, lambda c0, n, j=j: Wob.t[:, j, c0:c0 + n], w_ob[j * 128:(j + 1) * 128, :], D)
            load_w(Woc, lambda c0, n, j=j: Woc.t[:, j, c0:c0 + n], w_oc[j * 128:(j + 1) * 128, :], D)
        for j in range(8):
            load_w(Wo, lambda c0, n, j=j: Wo.t[:, j, c0:c0 + n], w_out[j * 128:(j + 1) * 128, :], D)
        stage_scope.__exit__(None, None, None)
        LNG = k.sb("LNG", [128, 512], F32)
        LNB = k.sb("LNB", [128, 512], F32)
        k.dma("sp", LNG.t[:], rowb[0], w=[LNG])
        k.dma("sp", LNB.t[:], rowb[1], w=[LNB])
        C0 = k.sb("C0", [128, 15], F32)
        k.op("dve", lambda e: e.tensor_tensor(C0.t[:], V("mu_p", 0, 15), V("mu_n", 0, 15), ALU.add), r=[VEC], w=[C0])
        k.op("dve", lambda e: e.tensor_scalar(C0.t[:], C0.t[:], -1.0, 1.0, ALU.mult, ALU.add), r=[C0], w=[C0])
        OMK = k.sb("OMK", [128, 4], F32)
        k.op("dve", lambda e: e.tensor_scalar(OMK.t[:], V("k_a", 0, 4), -1.0, 1.0, ALU.mult, ALU.add), r=[VEC], w=[OMK])
        def X4(nm):
            if nm == "omk":
                return bc3(OMK.t[:, :], 128)
            return bc3(V(nm, 0, 4), 128)

        BD = k.sb("BD", [128, 128], BF16)
        k.op("pool", lambda e: e.memset(BD.t[:], 0.0), w=[BD])
        k.op("pool", lambda e: e.memset(BD.t[0:64, 0:64], 1.0), r=[BD], w=[BD])
        k.op("pool", lambda e: e.memset(BD.t[64:128, 64:128], 1.0), r=[BD], w=[BD])
        HSEL = k.sb("HSEL", [128, 2], BF16)
        k.op("pool", lambda e: e.memset(HSEL.t[:], 0.0), w=[HSEL])
        k.op("pool", lambda e: e.memset(HSEL.t[0:64, 0:1], 1.0), r=[HSEL], w=[HSEL])
        k.op("pool", lambda e: e.memset(HSEL.t[64:128, 1:2], 1.0), r=[HSEL], w=[HSEL])
        MASK = []
        MASKT = []
        for d in range(2):
            m_ = k.sb("MASK%d" % d, [128, 256], F32)
            mt_ = k.sb("MASKT%d" % d, [128, 128], F32)
            k.op("pool", lambda e, m_=m_: e.memset(m_.t[:], 1.0), w=[m_])
            k.op("pool", lambda e, mt_=mt_: e.memset(mt_.t[:], 1.0), w=[mt_])
            if d == 0:
                k.op("pool", lambda e, m_=m_: e.affine_select(out=m_.t[:, 0:128], in_=m_.t[:, 0:128], pattern=[[1, 128]], compare_op=ALU.is_gt, fill=0.0, base=0, channel_multiplier=-1), r=[m_], w=[m_])
                k.op("pool", lambda e, m_=m_: e.affine_select(out=m_.t[:, 128:256], in_=m_.t[:, 128:256], pattern=[[1, 128]], compare_op=ALU.is_ge, fill=0.0, base=0, channel_multiplier=-1), r=[m_], w=[m_])
                k.op("pool", lambda e, mt_=mt_: e.affine_select(out=mt_.t[:], in_=mt_.t[:], pattern=[[-1, 128]], compare_op=ALU.is_gt, fill=0.0, base=0, channel_multiplier=1), r=[mt_], w=[mt_])
            else:
                k.op("pool", lambda e, m_=m_: e.affine_select(out=m_.t[:, 0:128], in_=m_.t[:, 0:128], pattern=[[-1, 128]], compare_op=ALU.is_gt, fill=0.0, base=0, channel_multiplier=1), r=[m_], w=[m_])
                k.op("pool", lambda e, m_=m_: e.affine_select(out=m_.t[:, 128:256], in_=m_.t[:, 128:256], pattern=[[-1, 128]], compare_op=ALU.is_ge, fill=0.0, base=0, channel_multiplier=1), r=[m_], w=[m_])
                k.op("pool", lambda e, mt_=mt_: e.affine_select(out=mt_.t[:], in_=mt_.t[:], pattern=[[1, 128]], compare_op=ALU.is_gt, fill=0.0, base=0, channel_multiplier=-1), r=[mt_], w=[mt_])
            MASK.append(m_)
            MASKT.append(mt_)
        RESET = k.sb("RESET", [128, 4, 128], F32)
        k.op("pool", lambda e: e.memset(RESET.t[:], 1.0), w=[RESET])
        k.op("pool", lambda e: e.memset(RESET.t[:, :, 0:1], 0.0), r=[RESET], w=[RESET])

        def f32t(name, shape=(128, 4, 128)):
            return k.sb(name, list(shape), F32)

        RWW = k.sb("RWW", [128, 15, 130], F32)
        RWF = f32t("RWF", (128, 15, 128))
        KQ, RN, KK, ZS, SG, AA, LF = [f32t(n) for n in ("KQ", "RN", "KK", "ZS", "SG", "AA", "LF")]
        E1, E2, E3, E4, TT_, KD, BB, KT32, NBH, KHH = [f32t(n) for n in ("E1", "E2", "E3", "E4", "TT", "KD", "BB", "KT32", "NBH", "KHH")]
        AS_ = ZS
        LI = ZS
        LE = KQ
        D4 = RN
        RKD = TT_
        GC = k.sb("GC", [128, 4], F32)
        DG = f32t("DG", (128, 4, 64))
        DGhr = k.ring("DGh", [128, 4, 64], BF16, 2)
        DGlr = k.ring("DGl", [128, 4, 64], BF16, 2)
        SQb = k.sb("SQb", [128, 4, 128], BF16)
        TW = k.sb("TW", [128, 128], BF16)
        ALb = k.sb("ALb", [128, 128], BF16)
        SGLr = k.ring("SGL", [128, 128], BF16, 2)
        NBb = k.sb("NBb", [128, 4, 128], BF16)
        KTb = k.sb("KTb", [128, 4, 128], BF16)
        KTILb = k.sb("KTILb", [128, 4, 128], BF16)
        RKDbr = k.ring("RKDb", [128, 4, 128], BF16, 2)
        CATr = k.ring("CAT", [128, 4, 256], BF16, 2)
        KTMr, NBHMr, KHMr, VMbr = [k.ring(n, [128, 512], BF16, 2) for n in ("KTM", "NBHM", "KHM", "VMb")]
        VM32r = k.ring("VM32", [128, 512], F32, 2)
        ABrr = k.ring("ABr", [128, 8, 256], BF16, 2)
        AKrr = k.ring("AKr", [128, 8, 256], BF16, 2)
        Y0r = k.ring("Y0", [128, 8, 128], BF16, 2)
        XA = k.ring("XA", [128, 8, 128], BF16, 2)
        YA = k.ring("YA", [128, 8, 128], BF16, 2)
        TTr = k.ring("TTr", [128, 8, 128], BF16, 2)
        KHAT, AVb, UH = [k.sb(n, [128, 512], BF16) for n in ("KHAT", "AVb", "UH")]
        YH = k.sb("YH", [128, 512], F32)
        ZH = k.sb("ZH", [64, 512], F32)
        MT = k.sb("MT", [64, 512], F32)
        RHT = k.sb("RHT", [64, 8, 128], BF16)
        PS_ = k.ring("PS", [64, 512], F32, 2)
        Pb = k.sb("Pb", [64, 512], BF16)
        YO = k.sb("YO", [128, 512], F32)
        CB = k.sb("CB", [128, 8], F32)
        YF = k.sb("YF", [128, 512], F32)
        st8 = [k.sb("st8_%d" % i, [128, 8], F32) for i in range(3)]
        CEN = k.sb("CEN", [128, 512], F32)
        SQ5 = k.sb("SQ5", [128, 512], F32)
        BON = SQ5
        YG = CEN
        YGT = k.sb("YGT", [128, 4, 128], BF16)
        OAt = YF
        OAT = k.sb("OAT", [128, 4, 128], BF16)
        OCT = k.sb("OCT", [128, 4, 128], BF16)
        GT = k.sb("GT", [128, 24, 128], BF16)
        XT = k.sb("XT", [128, 8, 128], F32)
        PROD = k.sb("PROD", [128, 3, 128], F32)
        MS = k.sb("MS", [128, 128], F32)
        MRG = k.sb("MRG", [128, 8, 128], BF16)
        X1 = k.sb("X1", [128, 8, 128], F32)

        def flat(t_):
            return t_.t[:].rearrange("p a b -> p (a b)")

        def rwkv_chunk(u, c, d, st, state_only=False, first=False):
            T = units[u]
            t0 = tok0[u] + c * 128
            CAT, KTM, NBHM, KHM, VMb, VM32 = CATr.next(), KTMr.next(), NBHMr.next(), KHMr.next(), VMbr.next(), VM32r.next()
            ABr, AKr, Y0, DGh, DGl, RKDb, SGL = ABrr.next(), AKrr.next(), Y0r.next(), DGhr.next(), DGlr.next(), RKDbr.next(), SGLr.next()
            lo = 1 if c == 0 else 0
            hi = 129 if c == T // 128 - 1 else 130
            if lo == 1:
                k.op("pool", lambda e: e.memset(RWW.t[:, :, 0:1], 0.0), w=[RWW])
            if hi == 129:
                k.op("pool", lambda e: e.memset(RWW.t[:, :, 129:130], 0.0), w=[RWW])
            k.dma("sp", RWW.t[:, :, lo:hi], rw_s[:, t0 - 1 + lo:t0 - 1 + hi].rearrange("(c p) t -> p c t", p=128), r=[DR("rw")], w=[RWW])
            k.op("dve", lambda e: e.tensor_tensor(RWF.t[:], RWW.t[:, :, 1:129], bc3(C0.t[:, :], 128), ALU.mult), r=[RWW, C0], w=[RWF])
            for gi, (tmp, j0, j1) in enumerate(((E1, 0, 4), (E2, 4, 8), (E3, 8, 12), (E4, 12, 15))):
                n_ = j1 - j0
                k.op("pool", lambda e, tmp=tmp, j0=j0, j1=j1, n_=n_: e.tensor_tensor(tmp.t[:, 0:n_, :], RWW.t[:, j0:j1, 0:128], bc3(V("mu_p", j0, n_), 128), ALU.mult), r=[RWW, VEC], w=[tmp])
                k.op("dve", lambda e, tmp=tmp, j0=j0, j1=j1, n_=n_: e.tensor_tensor(RWF.t[:, j0:j1, :], RWF.t[:, j0:j1, :], tmp.t[:, 0:n_, :], ALU.add), r=[RWF, tmp], w=[RWF])
                k.op("pool", lambda e, tmp=tmp, j0=j0, j1=j1, n_=n_: e.tensor_tensor(tmp.t[:, 0:n_, :], RWW.t[:, j0:j1, 2:130], bc3(V("mu_n", j0, n_), 128), ALU.mult), r=[RWW, VEC, tmp], w=[tmp])
                k.op("dve", lambda e, tmp=tmp, j0=j0, j1=j1, n_=n_: e.tensor_tensor(RWF.t[:, j0:j1, :], RWF.t[:, j0:j1, :], tmp.t[:, 0:n_, :], ALU.add), r=[RWF, tmp], w=[RWF])
            yield
            chk("r1")
            r_ = RWF.t[:, 0:4, :]
            kx = RWF.t[:, 4:8, :]
            vx = RWF.t[:, 8:12, :]
            k.op("pool", lambda e: e.tensor_tensor(KQ.t[:], kx, X4("k_k"), ALU.mult), r=[RWF, VEC], w=[KQ])
            k.act(SQb.t[:], KQ.t[:], AF.Square, r=[KQ], w=[SQb])
            b = nb()
            k.mm(banks[b].t[:, b, :], BD.t[:], flat(SQb), True, True, r=[BD, SQb], w=[banks[b]])
            k.act(flat(RN), banks[b].t[:, b, :], AF.Sqrt, r=[banks[b]], w=[RN])
            k.op("dve", lambda e: e.tensor_scalar(RN.t[:], RN.t[:], 1e-12, None, ALU.max), r=[RN], w=[RN])
            k.op("dve", lambda e: e.reciprocal(RN.t[:], RN.t[:]), r=[RN], w=[RN])
            k.op("dve", lambda e: e.tensor_tensor(KK.t[:], KQ.t[:], RN.t[:], ALU.mult), r=[KQ, RN], w=[KK])
            yield
            chk("r2")
            k.act(TW.t[:], RWF.t[:, 12, :], AF.Tanh, r=[RWF], w=[TW])
            k.op("pool", lambda e: e.tensor_copy(ALb.t[:], RWF.t[:, 13, :]), r=[RWF], w=[ALb])
            dr = slice(0, 64) if d == 0 else slice(64, 128)
            sfx = "_f" if d == 0 else "_b"
            b = nb()
            for j in range(4):
                k.mm(banks[b].t[:, b, j * 128:(j + 1) * 128], W2.t[dr, j * 128:(j + 1) * 128], TW.t[dr, :], True, True, r=[W2, TW], w=[banks[b]])
            k.op("dve", lambda e, b=b: e.tensor_tensor(ZS.t[:], banks[b].t[:, b, :].rearrange("p (a b) -> p a b", b=128), X4("w0" + sfx), ALU.add), r=[banks[b], VEC], w=[ZS])
            k.act(SG.t[:], ZS.t[:], AF.Sigmoid, r=[ZS], w=[SG])
            b = nb()
            for j in range(4):
                k.mm(banks[b].t[:, b, j * 128:(j + 1) * 128], A2.t[dr, j * 128:(j + 1) * 128], ALb.t[dr, :], True, True, r=[A2, ALb], w=[banks[b]])
            k.op("dve", lambda e, b=b: e.tensor_tensor(AS_.t[:], banks[b].t[:, b, :].rearrange("p (a b) -> p a b", b=128), X4("a0" + sfx), ALU.add), r=[banks[b], VEC], w=[AS_])
            k.act(AA.t[:], AS_.t[:], AF.Sigmoid, r=[AS_], w=[AA])
            yield
            k.op("dve", lambda e: e.tensor_tensor_scan(flat(LF), flat(RESET), flat(SG), 0.0, ALU.mult, ALU.add), r=[RESET, SG], w=[LF])
            TOT = LF.t[:, :, 127:128]
            if d == 0:
                LIa = LF
            else:
                k.op("dve", lambda e: e.tensor_tensor(D4.t[:], SG.t[:], LF.t[:], ALU.subtract), r=[SG, LF], w=[D4])
                k.op("dve", lambda e: e.tensor_tensor(LI.t[:], D4.t[:], TOT.to_broadcast([128, 4, 128]), ALU.add), r=[D4, LF], w=[LI])
                LIa = LI
            k.op("pool", lambda e: e.tensor_tensor(LE.t[:], LIa.t[:], SG.t[:], ALU.subtract), r=[LIa, SG], w=[LE])
            k.op("pool", lambda e: e.tensor_tensor(D4.t[:], TOT.to_broadcast([128, 4, 128]), LIa.t[:], ALU.subtract), r=[LF, LIa, D4], w=[D4])
            k.act(E1.t[:], LIa.t[:], AF.Exp, r=[LIa], w=[E1], scale=DECAY_C)
            k.act(E2.t[:], LIa.t[:], AF.Exp, r=[LIa], w=[E2], scale=-DECAY_C)
            k.act(E3.t[:], LE.t[:], AF.Exp, r=[LE], w=[E3], scale=DECAY_C)
            k.act(E4.t[:], D4.t[:], AF.Exp, r=[D4], w=[E4], scale=DECAY_C)
            k.act(GC.t[:].unsqueeze(2), TOT, AF.Exp, r=[LF], w=[GC], scale=DECAY_C)
            yield
            chk("r3")
            k.op("dve", lambda e: e.tensor_tensor(TT_.t[:], AA.t[:], X4("k_a"), ALU.mult), r=[AA, VEC], w=[TT_])
            k.op("dve", lambda e: e.tensor_tensor(TT_.t[:], TT_.t[:], X4("omk"), ALU.add), r=[TT_, OMK], w=[TT_])
            k.op("pool", lambda e: e.tensor_tensor(KD.t[:], kx, TT_.t[:], ALU.mult), r=[RWF, TT_], w=[KD])
            k.op("pool", lambda e: e.tensor_tensor(BB.t[:], KK.t[:], AA.t[:], ALU.mult), r=[KK, AA], w=[BB])
            k.op("dve", lambda e: e.tensor_tensor(KT32.t[:], KK.t[:], E3.t[:], ALU.mult), r=[KK, E3], w=[KT32])
            k.act(CAT.t[:, :, 0:128], KT32.t[:], AF.Copy, r=[KT32], w=[CAT])
            k.op("pool", lambda e: e.tensor_copy(KTb.t[:], KT32.t[:]), r=[KT32], w=[KTb])
            k.op("dve", lambda e: e.scalar_tensor_tensor(NBb.t[:], BB.t[:], -1.0, E2.t[:], ALU.mult, ALU.mult), r=[BB, E2], w=[NBb])
            k.op("pool", lambda e: e.tensor_tensor(KTILb.t[:], KD.t[:], E2.t[:], ALU.mult), r=[KD, E2], w=[KTILb])
            k.op("pool", lambda e: e.tensor_tensor(CAT.t[:, :, 128:256], r_, E1.t[:], ALU.mult), r=[RWF, E1, CAT], w=[CAT])
            k.op("dve", lambda e: e.scalar_tensor_tensor(NBH.t[:], BB.t[:], -1.0, E4.t[:], ALU.mult, ALU.mult), r=[BB, E4], w=[NBH])
            k.op("pool", lambda e: e.tensor_tensor(KHH.t[:], KD.t[:], E4.t[:], ALU.mult), r=[KD, E4], w=[KHH])
            k.op("dve", lambda e: e.tensor_tensor(DG.t[0:64, :, :], ident.t[0:64, 0:64].unsqueeze(1).to_broadcast([64, 4, 64]),
                                                  GC.t[0:64, :].unsqueeze(2).to_broadcast([64, 4, 64]), ALU.mult), r=[ident, GC], w=[DG])
            k.op("dve", lambda e: e.tensor_tensor(DG.t[64:128, :, :], ident.t[64:128, 64:128].unsqueeze(1).to_broadcast([64, 4, 64]),
                                                  GC.t[64:128, :].unsqueeze(2).to_broadcast([64, 4, 64]), ALU.mult), r=[ident, GC, DG], w=[DG])
            k.op("pool", lambda e: e.tensor_copy(DGh.t[:], DG.t[:]), r=[DG], w=[DGh])
            k.op("pool", lambda e: e.tensor_tensor(DGl.t[:], DG.t[:], DGh.t[:], ALU.subtract), r=[DG, DGh], w=[DGl])
            for nm_, t_ in (("SG", SG), ("LF", LF), ("LE", LE), ("D4", D4), ("E1", E1), ("E2", E2), ("E3", E3), ("E4", E4), ("KK", KK), ("AA", AA),
                            ("KT32", KT32), ("NBH", NBH), ("KHH", KHH), ("KD", KD), ("RWF", RWF), ("DG", DG)):
                dbg(nm_, t_, t_.t[:])
            dbg("GC", GC, GC.t[:])
            k.op("pool", lambda e: e.tensor_tensor(RKD.t[:], r_, KD.t[:], ALU.mult), r=[RWF, KD], w=[RKD])
            k.op("pool", lambda e: e.tensor_tensor(RKDb.t[:], RKD.t[:], X4("r_k"), ALU.mult), r=[RKD, VEC], w=[RKDb])
            k.act(SGL.t[:], RWF.t[:, 14, :], AF.Sigmoid, r=[RWF], w=[SGL])
            yield
            chk("r4")
            for src, dst, dst32 in ((KT32.t, KTM, None), (NBH.t, NBHM, None), (KHH.t, KHM, None), (RWF.t, VMb, VM32)):
                b = nb()
                for j in range(4):
                    sap = src[:, 8 + j, :] if dst is VMb else src[:, j, :]
                    rr = [RWF] if dst is VMb else [KT32, NBH, KHH]
                    if KVAR != "notr":
                        k.tr(banks[b].t[:, b, j * 128:(j + 1) * 128], sap, ident, r=rr, w=[banks[b]], inc=(j == 3))
                if KVAR == "dvecp":
                    k.op("dve", lambda e, b=b, dst=dst: e.tensor_copy(dst.t[:], banks[b].t[:, b, :]), r=[banks[b]], w=[dst])
                elif KVAR == "actcp":
                    k.op("act", lambda e, b=b, dst=dst: e.copy(dst.t[:], banks[b].t[:, b, :]), r=[banks[b]], w=[dst])
                elif KVAR != "nocp":
                    k.act(dst.t[:], banks[b].t[:, b, :], AF.Copy, r=[banks[b]], w=[dst])
                if dst32 is not None and KVAR != "nocp":
                    k.act(dst32.t[:], banks[b].t[:, b, :], AF.Copy, r=[banks[b]], w=[dst32])
                yield
            chk("r5")
            for q in range(2):
                for par in range(2):
                    hr = slice(64 * par, 64 * par + 64)
                    b1 = nb()
                    b2 = nb()
                    for i in range(2):
                        hp = 2 * q + i
                        k.mm(banks[b1].t[:, b1, i * 256:(i + 1) * 256], NBb.t[hr, hp, :], CAT.t[hr, hp, :], True, True, r=[NBb, CAT], w=[banks[b1]])
                        k.mm(banks[b2].t[:, b2, i * 256:(i + 1) * 256], KTILb.t[hr, hp, :], CAT.t[hr, hp, :], True, True, r=[KTILb, CAT], w=[banks[b2]])
                    mk_ = MASK[d].t[:, :].unsqueeze(1).to_broadcast([128, 2, 256])
                    h0 = 4 * q + par
                    k.op("dve", lambda e, b1=b1, h0=h0, mk_=mk_: e.tensor_tensor(ABr.t[:, h0:h0 + 3:2, :], banks[b1].t[:, b1, :].rearrange("p (a b) -> p a b", b=256), mk_, ALU.mult),
                         r=[banks[b1], MASK[d]], w=[ABr])
                    k.op("dve", lambda e, b2=b2, h0=h0, mk_=mk_: e.tensor_tensor(AKr.t[:, h0:h0 + 3:2, :], banks[b2].t[:, b2, :].rearrange("p (a b) -> p a b", b=256), mk_, ALU.mult),
                         r=[banks[b2], MASK[d]], w=[AKr])
                    yield
            for par in range(2):
                hr = slice(64 * par, 64 * par + 64)
                b = nb()
                for i in range(4):
                    k.mm(banks[b].t[:, b, i * 128:(i + 1) * 128], KTb.t[hr, i, :], NBb.t[hr, i, :], True, True, r=[KTb, NBb], w=[banks[b]])
                k.op("dve", lambda e, b=b, par=par, Y0=Y0: e.tensor_tensor(Y0.t[:, par::2, :], banks[b].t[:, b, :].rearrange("p (a b) -> p a b", b=128),
                                                                    MASKT[d].t[:, :].unsqueeze(1).to_broadcast([128, 4, 128]), ALU.mult),
                     r=[banks[b], MASKT[d]], w=[Y0])
            chk("r6")
            yield "S2"
            if first:
                P_cur = PS_.next()
                k.op("pool", lambda e: e.memset(P_cur.t[:], 0.0), w=[P_cur])
            else:
                P_cur = st["P"]
            X0 = XA.next()
            k.op("pool", lambda e, X0=X0: e.tensor_copy(X0.t[:], ABr.t[:, :, 0:128]), r=[ABr], w=[X0])
            Tc = TTr.next()
            k.op("dve", lambda e, Tc=Tc: e.tensor_tensor(Tc.t[:], ABr.t[:, :, 0:128], identb.t[:, :].unsqueeze(1).to_broadcast([128, 8, 128]), ALU.add), r=[ABr, identb], w=[Tc])
            Xc, Yc = X0, Y0
            for lvl in range(6):
                Xn = XA.next()
                Yn = YA.next()
                last = lvl == 5
                for g4 in range(2):
                    bx = nb()
                    by = nb()
                    for hh in range(4):
                        h = g4 * 4 + hh
                        if not last:
                            k.mm(banks[bx].t[:, bx, hh * 128:(hh + 1) * 128], Yc.t[:, h, :], Xc.t[:, h, :], True, True, r=[Yc, Xc], w=[banks[bx]])
                        k.mm(banks[by].t[:, by, hh * 128:(hh + 1) * 128], Xc.t[:, h, :], Yc.t[:, h, :], True, True, r=[Yc, Xc], w=[banks[by]])
                    if not last:
                        k.act(Xn.t[:, 4 * g4:4 * g4 + 4, :], banks[bx].t[:, bx, :].rearrange("p (a b) -> p a b", b=128), AF.Copy, r=[banks[bx]], w=[Xn])
                    k.act(Yn.t[:, 4 * g4:4 * g4 + 4, :], banks[by].t[:, by, :].rearrange("p (a b) -> p a b", b=128), AF.Copy, r=[banks[by]], w=[Yn])
                Tn = TTr.next()
                for g4 in range(2):
                    bt = nb()
                    for hh in range(4):
                        h = g4 * 4 + hh
                        k.mm(banks[bt].t[:, bt, hh * 128:(hh + 1) * 128], Yn.t[:, h, :], Tc.t[:, h, :], True, True, r=[Yn, Tc], w=[banks[bt]])
                    k.op("dve", lambda e, bt=bt, g4=g4, Tn=Tn, Tc=Tc: e.tensor_tensor(Tn.t[:, 4 * g4:4 * g4 + 4, :], banks[bt].t[:, bt, :].rearrange("p (a b) -> p a b", b=128),
                                                                            Tc.t[:, 4 * g4:4 * g4 + 4, :], ALU.add), r=[banks[bt], Tc], w=[Tn])
                Tc, Xc, Yc = Tn, Xn, Yn
                yield
            chk("r7")
            b = nb()
            for h in range(8):
                k.mm(banks[b].t[:, b, h * 64:(h + 1) * 64], Tc.t[:, h, :], KTM.t[:, h * 64:(h + 1) * 64], True, True, r=[Tc, KTM], w=[banks[b]])
            k.act(KHAT.t[:], banks[b].t[:, b, :], AF.Copy, r=[banks[b]], w=[KHAT])
            yield
            b = nb()
            for h in range(8):
                k.mm(banks[b].t[:, b, h * 64:(h + 1) * 64], AKr.t[:, h, 0:128], VMb.t[:, h * 64:(h + 1) * 64], True, True, r=[AKr, VMb], w=[banks[b]])
            k.act(AVb.t[:], banks[b].t[:, b, :], AF.Copy, r=[banks[b]], w=[AVb])
            yield
            b = nb()
            for h in range(8):
                k.mm(banks[b].t[:, b, h * 64:(h + 1) * 64], Tc.t[:, h, :], AVb.t[:, h * 64:(h + 1) * 64], True, True, r=[Tc, AVb], w=[banks[b]])
            k.act(UH.t[:], banks[b].t[:, b, :], AF.Copy, r=[banks[b]], w=[UH])
            yield
            if not state_only:
                b = nb()
                for h in range(8):
                    k.mm(banks[b].t[:, b, h * 64:(h + 1) * 64], ABr.t[:, h, 128:256], UH.t[:, h * 64:(h + 1) * 64], True, False, r=[ABr, UH], w=[banks[b]])
                    k.mm(banks[b].t[:, b, h * 64:(h + 1) * 64], AKr.t[:, h, 128:256], VMb.t[:, h * 64:(h + 1) * 64], False, True, r=[AKr, VMb], w=[banks[b]])
                k.act(YH.t[:], banks[b].t[:, b, :], AF.Copy, r=[banks[b]], w=[YH])
            b = nb()
            for h in range(8):
                k.mm(banks[b].t[0:64, b, h * 64:(h + 1) * 64], NBHM.t[:, h * 64:(h + 1) * 64], UH.t[:, h * 64:(h + 1) * 64], True, False, r=[NBHM, UH], w=[banks[b]])
                k.mm(banks[b].t[0:64, b, h * 64:(h + 1) * 64], KHM.t[:, h * 64:(h + 1) * 64], VMb.t[:, h * 64:(h + 1) * 64], False, True, r=[KHM, VMb], w=[banks[b]])
            k.act(ZH.t[:], banks[b].t[0:64, b, :], AF.Copy, r=[banks[b]], w=[ZH])
            yield
            b = nb()
            for h in range(8):
                hr = slice(64 * (h % 2), 64 * (h % 2) + 64)
                k.mm(banks[b].t[0:64, b, h * 64:(h + 1) * 64], KHAT.t[:, h * 64:(h + 1) * 64], NBHM.t[:, h * 64:(h + 1) * 64], True, False, r=[KHAT, NBHM], w=[banks[b]])
                k.mm(banks[b].t[0:64, b, h * 64:(h + 1) * 64], identb.t[hr, hr], DGh.t[hr, h // 2, :], False, False, r=[identb, DGh], w=[banks[b]])
                k.mm(banks[b].t[0:64, b, h * 64:(h + 1) * 64], identb.t[hr, hr], DGl.t[hr, h // 2, :], False, True, r=[identb, DGl], w=[banks[b]])
            k.act(MT.t[:], banks[b].t[0:64, b, :], AF.Copy, r=[banks[b]], w=[MT])
            yield
            for g4 in range(0 if state_only else 2):
                b = nb()
                for hh in range(4):
                    h = g4 * 4 + hh
                    hr = slice(64 * (h % 2), 64 * (h % 2) + 64)
                    k.mm(banks[b].t[0:64, b, hh * 128:(hh + 1) * 128], KHAT.t[:, h * 64:(h + 1) * 64], ABr.t[:, h, 128:256], True, False, r=[KHAT, ABr], w=[banks[b]])
                    k.mm(banks[b].t[0:64, b, hh * 128:(hh + 1) * 128], identb.t[hr, hr], CAT.t[hr, h // 2, 128:256], False, True, r=[identb, CAT], w=[banks[b]])
                k.act(RHT.t[:, 4 * g4:4 * g4 + 4, :], banks[b].t[0:64, b, :].rearrange("p (a b) -> p a b", b=128), AF.Copy, r=[banks[b]], w=[RHT])
            chk("r8")
            if not state_only:
                k.act(Pb.t[:], P_cur.t[:], AF.Copy, r=[P_cur], w=[Pb])
                b = nb()
                for h in range(8):
                    k.mm(banks[b].t[:, b, h * 64:(h + 1) * 64], RHT.t[:, h, :], Pb.t[:, h * 64:(h + 1) * 64], True, True, r=[RHT, Pb], w=[banks[b]])
                k.op("dve", lambda e, b=b: e.tensor_tensor(YO.t[:], banks[b].t[:, b, :], YH.t[:], ALU.add), r=[banks[b], YH], w=[YO])
            P_new = PS_.next()
            b = nb()
            for h in range(8):
                k.mm(banks[b].t[0:64, b, h * 64:(h + 1) * 64], MT.t[:, h * 64:(h + 1) * 64], P_cur.t[:, h * 64:(h + 1) * 64], True, True, r=[MT, P_cur], w=[banks[b]])
            k.op("dve", lambda e, b=b, P_new=P_new: e.tensor_tensor(P_new.t[:], banks[b].t[0:64, b, :], ZH.t[:], ALU.add), r=[banks[b], ZH], w=[P_new])
            st["P"] = P_new
            if state_only:
                return
            yield
            b = nb()
            for j in range(4):
                k.mm(banks[b].t[:, b, 2 * j:2 * j + 2], RKDb.t[:, j, :], HSEL.t[:], True, True, r=[RKDb, HSEL], w=[banks[b]])
            k.act(CB.t[:], banks[b].t[:, b, 0:8], AF.Copy, r=[banks[b]], w=[CB])
            k.op("dve", lambda e: e.tensor_tensor(BON.t[:].rearrange("p (a b) -> p a b", b=64), VM32.t[:].rearrange("p (a b) -> p a b", b=64),
                                                  CB.t[:, :].unsqueeze(2).to_broadcast([128, 8, 64]), ALU.mult), r=[VM32, CB], w=[BON])
            k.op("pool", lambda e: e.tensor_tensor(YO.t[:], YO.t[:], BON.t[:], ALU.add), r=[YO, BON], w=[YO])
            chk("r9")
            if d == 0:
                k.dma("sp", yf_s[t0:t0 + 128, :], YO.t[:], r=[YO], w=[DR("yf")])
            else:
                merge_chunk(u, c, SGL)
                chk("m1")

        def merge_chunk(u, c, SGL):
            T = units[u]
            t0 = tok0[u] + c * 128
            k.dma("sp", YF.t[:], yf_s[t0:t0 + 128, :], r=[DR("yf")], w=[YF])
            k.op("dve", lambda e: e.tensor_tensor(YO.t[:], YO.t[:], YF.t[:], ALU.add), r=[YO, YF], w=[YO])
            y3 = YO.t[:].rearrange("p (a b) -> p a b", b=64)
            mean, var, rs_ = st8
            k.op("dve", lambda e: e.tensor_reduce(mean.t[:], y3, AX.X, ALU.add), r=[YO], w=[mean])
            k.op("dve", lambda e: e.tensor_scalar(mean.t[:], mean.t[:], 1.0 / 64, None, ALU.mult), r=[mean], w=[mean])
            c3 = CEN.t[:].rearrange("p (a b) -> p a b", b=64)
            k.op("dve", lambda e: e.tensor_tensor(c3, y3, mean.t[:, :].unsqueeze(2).to_broadcast([128, 8, 64]), ALU.subtract), r=[YO, mean], w=[CEN])
            k.op("pool", lambda e: e.tensor_tensor(SQ5.t[:], CEN.t[:], CEN.t[:], ALU.mult), r=[CEN], w=[SQ5])
            k.op("dve", lambda e: e.tensor_reduce(var.t[:], SQ5.t[:].rearrange("p (a b) -> p a b", b=64), AX.X, ALU.add), r=[SQ5], w=[var])
            rstd(rs_.t[:], var.t[:], 1.0 / 64, 64e-5, [var], [rs_])
            k.op("dve", lambda e: e.tensor_tensor(c3, c3, rs_.t[:, :].unsqueeze(2).to_broadcast([128, 8, 64]), ALU.mult), r=[CEN, rs_], w=[CEN])
            k.op("pool", lambda e: e.tensor_tensor(CEN.t[:], CEN.t[:], LNG.t[:], ALU.mult), r=[CEN, LNG], w=[CEN])
            k.op("pool", lambda e: e.tensor_tensor(CEN.t[:], CEN.t[:], LNB.t[:], ALU.add), r=[CEN, LNB], w=[CEN])
            b = nb()
            k.mm(banks[b].t[:, b, :], SGL.t[:], G2.t[:], True, True, r=[SGL, G2], w=[banks[b]])
            k.op("dve", lambda e, b=b: e.tensor_tensor(YG.t[:], CEN.t[:], banks[b].t[:, b, :], ALU.mult), r=[CEN, banks[b]], w=[YG])
            b = nb()
            for j in range(4):
                k.tr(banks[b].t[:, b, j * 128:(j + 1) * 128], YG.t[:, j * 128:(j + 1) * 128], ident, r=[YG], w=[banks[b]])
            k.act(YGT.t[:], banks[b].t[:, b, :].rearrange("p (a b) -> p a b", b=128), AF.Copy, r=[banks[b]], w=[YGT])
            k.dma("sp", OAt.t[:], oa_s[t0:t0 + 128, :], r=[DR("oa")], w=[OAt])
            b = nb()
            for j in range(4):
                k.tr(banks[b].t[:, b, j * 128:(j + 1) * 128], OAt.t[:, j * 128:(j + 1) * 128], ident, r=[OAt], w=[banks[b]])
            k.act(OAT.t[:], banks[b].t[:, b, :].rearrange("p (a b) -> p a b", b=128), AF.Copy, r=[banks[b]], w=[OAT])
            k.dma("sp", OCT.t[:], ocT_s[:, t0:t0 + 128].rearrange("(c p) t -> p c t", p=128), r=[DR("oc")], w=[OCT])
            k.dma("sp", GT.t[:], gate_s[:, t0:t0 + 128].rearrange("(c p) t -> p c t", p=128), r=[DR("gate")], w=[GT])
            k.dma("sp", XT.t[:], xT_s[:, t0:t0 + 128].rearrange("(c p) t -> p c t", p=128), r=[DR("xT")], w=[XT])
            for dc in range(8):
                b = nb()
                for i, (Wt, At) in enumerate(((Woa, OAT), (Wob, YGT), (Woc, OCT))):
                    for kc in range(4):
                        k.mm(banks[b].t[:, b, i * 128:(i + 1) * 128], Wt.t[:, kc, dc * 128:(dc + 1) * 128], At.t[:, kc, :], kc == 0, kc == 3, r=[Wt, At], w=[banks[b]])
                k.op("dve", lambda e, b=b, dc=dc: e.tensor_tensor(PROD.t[:], banks[b].t[:, b, 0:384].rearrange("p (a b) -> p a b", b=128), GT.t[:, dc::8, :], ALU.mult),
                     r=[banks[b], GT], w=[PROD])
                k.op("pool", lambda e: e.tensor_tensor(MS.t[:], PROD.t[:, 0, :], PROD.t[:, 1, :], ALU.add), r=[PROD], w=[MS])
                k.op("pool", lambda e, dc=dc: e.tensor_tensor(MRG.t[:, dc, :], MS.t[:], PROD.t[:, 2, :], ALU.add), r=[MS, PROD], w=[MRG])
            for dc in range(8):
                b = nb()
                for kc in range(8):
                    k.mm(banks[b].t[:, b, 0:128], Wo.t[:, kc, dc * 128:(dc + 1) * 128], MRG.t[:, kc, :], kc == 0, kc == 7, r=[Wo, MRG], w=[banks[b]])
                k.op("dve", lambda e, b=b, dc=dc: e.tensor_tensor(X1.t[:, dc, :], banks[b].t[:, b, 0:128], XT.t[:, dc, :], ALU.add), r=[banks[b], XT], w=[X1])
            k.dma("sp", x1T_s[:, t0:t0 + 128].rearrange("(c p) t -> p c t", p=128), X1.t[:], r=[X1], w=[DR("x1T")])

        tasks = []
        st = {"P": None}
        for u in range(NU):
            ncn = units[u] // 128
            nmix = MIX[u] // 128
            for d in range(2):
                order = list(range(nmix)) if d == 0 else list(range(ncn - 1, -1, -1))
                for i, c in enumerate(order):
                    tasks.append((u, c, d, c >= nmix, i == 0))
        prev = None
        for tk in tasks + [None]:
            cur = rwkv_chunk(tk[0], tk[1], tk[2], st, state_only=tk[3], first=tk[4]) if tk is not None else None
            cur_s1_done = cur is None
            prev_done = prev is None
            while not (cur_s1_done and prev_done):
                if not prev_done:
                    try:
                        next(prev)
                    except StopIteration:
                        prev_done = True
                if not cur_s1_done:
                    if next(cur) == "S2":
                        cur_s1_done = True
            prev = cur

    chk("p45")
    with k.scope():
        Wup = k.sb("Wup", [128, 8, 2 * DFF], BF16)
        Wdn = k.sb("Wdn", [128, NFC, D], BF16)
        stage_scope = k.scope()
        stage_scope.__enter__()
        stage[0] = k.ring("stg", [128, 1024], F32, 2)
        for kc in range(8):
            load_w(Wup, lambda c0, n, kc=kc: Wup.t[:, kc, c0:c0 + n], w_up[kc * 128:(kc + 1) * 128, :], 2 * DFF, gain=V("g_ffn", kc))
        for fc in range(NFC):
            load_w(Wdn, lambda c0, n, fc=fc: Wdn.t[:, fc, c0:c0 + n], w_dn[fc * 128:(fc + 1) * 128, :], D)
        stage_scope.__exit__(None, None, None)
        NBK = 256
        XW = k.sb("XW", [128, 8, NBK + 2], F32)
        SQ7 = k.sb("SQ7", [128, 8, NBK + 2], BF16)
        H2 = k.sb("H2", [128, 8, NBK + 2], BF16)
        RS7 = k.sb("RS7", [128, NBK + 2], F32)
        ACTT = k.sb("ACTT", [128, NFC, NBK], BF16)
        CT = k.ring("CT", [128, NBK], F32, 2)
        GG = k.ring("GG", [128, NBK], F32, 2)
        YOut = k.ring("YOut", [128, D], F32, 2)
        for u in range(NU):
            T = units[u]
            for blk in range(OWN[u] // NBK):
                t0 = tok0[u] + blk * NBK
                o0 = own0[u] + blk * NBK
                lo = 1 if blk == 0 else 0
                hi = NBK + 1 if blk == T // NBK - 1 else NBK + 2
                if lo == 1:
                    k.op("pool", lambda e: e.memset(XW.t[:, :, 0:1], 0.0), w=[XW])
                if hi == NBK + 1:
                    k.op("pool", lambda e: e.memset(XW.t[:, :, NBK + 1:NBK + 2], 0.0), w=[XW])
                k.dma("sp", XW.t[:, :, lo:hi], x1T_s[:, t0 - 1 + lo:t0 - 1 + hi].rearrange("(c p) t -> p c t", p=128), r=[DR("x1T")], w=[XW])
                k.op("pool", lambda e: e.tensor_tensor(SQ7.t[:], XW.t[:], XW.t[:], ALU.mult), r=[XW], w=[SQ7])
                b = nb()
                for kc in range(8):
                    k.mm(banks[b].t[:, b, 0:NBK + 2], onesb.t[:], SQ7.t[:, kc, :], kc == 0, kc == 7, r=[onesb, SQ7], w=[banks[b]])
                rstd(RS7.t[:], banks[b].t[:, b, 0:NBK + 2], 1.0 / D, EPS, [banks[b]], [RS7])
                k.op("dve", lambda e: e.tensor_tensor(H2.t[:], XW.t[:], RS7.t[:, :].unsqueeze(1).to_broadcast([128, 8, NBK + 2]), ALU.mult), r=[XW, RS7], w=[H2])
                for fc in range(NFC):
                    bg = nb()
                    for kc in range(8):
                        k.mm(banks[bg].t[:, bg, 0:NBK + 2], Wup.t[:, kc, fc * 128:(fc + 1) * 128], H2.t[:, kc, :], kc == 0, kc == 7, r=[Wup, H2], w=[banks[bg]])
                    bv = nb()
                    for kc in range(8):
                        k.mm(banks[bv].t[:, bv, 0:NBK], Wup.t[:, kc, DFF + fc * 128:DFF + (fc + 1) * 128], H2.t[:, kc, 1:NBK + 1], kc == 0, kc == 7, r=[Wup, H2], w=[banks[bv]])
                    ct = CT.next()
                    gg = GG.next()
                    k.act(ct.t[:], banks[bg].t[:, bg, 1:NBK + 1], AF.Identity, r=[banks[bg], VEC], w=[ct], scale=V("cw1", fc), bias=V("cb", fc))
                    k.op("dve", lambda e, bg=bg, ct=ct, fc=fc: e.scalar_tensor_tensor(ct.t[:], banks[bg].t[:, bg, 0:NBK], V("cw0", fc), ct.t[:], ALU.mult, ALU.add), r=[banks[bg], ct, VEC], w=[ct])
                    k.op("dve", lambda e, bg=bg, ct=ct, fc=fc: e.scalar_tensor_tensor(ct.t[:], banks[bg].t[:, bg, 2:NBK + 2], V("cw2", fc), ct.t[:], ALU.mult, ALU.add), r=[banks[bg], ct, VEC], w=[ct])
                    k.act(gg.t[:], ct.t[:], AF.Gelu, r=[ct], w=[gg])
                    k.op("dve", lambda e, bv=bv, gg=gg, fc=fc: e.tensor_tensor(ACTT.t[:, fc, :], gg.t[:], banks[bv].t[:, bv, 0:NBK], ALU.mult), r=[gg, banks[bv]], w=[ACTT])
                for dc in range(8):
                    b = nb()
                    for fc in range(NFC):
                        k.mm(banks[b].t[:, b, 0:NBK], Wdn.t[:, fc, dc * 128:(dc + 1) * 128], ACTT.t[:, fc, :], fc == 0, fc == NFC - 1, r=[Wdn, ACTT], w=[banks[b]])
                    k.op("dve", lambda e, b=b, dc=dc: e.tensor_tensor(XW.t[:, dc, 1:NBK + 1], banks[b].t[:, b, 0:NBK], XW.t[:, dc, 1:NBK + 1], ALU.add), r=[banks[b], XW], w=[XW])
                for j in range(NBK // 128):
                    yo = YOut.next()
                    for half in range(2):
                        b = nb()
                        for q in range(4):
                            dc = half * 4 + q
                            k.tr(banks[b].t[:, b, q * 128:(q + 1) * 128], XW.t[:, dc, 1 + j * 128:1 + (j + 1) * 128], ident, r=[XW], w=[banks[b]])
                        if half == 0:
                            k.act(yo.t[:, 0:512], banks[b].t[:, b, :], AF.Copy, r=[banks[b]], w=[yo])
                        else:
                            k.op("dve", lambda e, b=b, yo=yo: e.tensor_copy(yo.t[:, 512:1024], banks[b].t[:, b, :]), r=[banks[b]], w=[yo])
                    k.dma("sp", ys[o0 + j * 128:o0 + (j + 1) * 128, :], yo.t[:], r=[yo], w=[DR("ys")], is_out=True)
    k.finish()
    return nc, k, locals()


def _cols(v, n):
    v = np.asarray(v, np.float32).reshape(-1)
    if v.size < n * 128:
        v = np.concatenate([v, np.zeros(n * 128 - v.size, np.float32)])
    return v.reshape(n, 128).T


_SWAP = {"mu_prev": "mu_next", "mu_next": "mu_prev", "w0_f": "w0_b", "w0_b": "w0_f", "a0_f": "a0_b", "a0_b": "a0_f",
         "w2_f": "w2_b", "w2_b": "w2_f", "a2_f": "a2_b", "a2_b": "a2_f"}


def host_pack(inp, tlens, rev=False):
    if isinstance(tlens, int):
        tlens = [tlens]

    def g(n):
        if rev and n in _SWAP:
            n = _SWAP[n]
        a = np.asarray(inp[n], np.float32)[0]
        if rev and n == "conv_w":
            a = a[::-1]
        return a
    vec = {}
    vec["g_mix"] = _cols(g("norm_mix_g"), 8)
    vec["g_q"] = _cols(g("q_norm_g"), 3)
    vec["g_kv"] = _cols(g("kv_norm_g"), 2)
    qn, kn = g("mla_qn_g"), g("mla_kn_g")
    rf = np.concatenate([np.arange(64, 96), np.arange(0, 64)])
    sw = np.concatenate([np.arange(80, 96), np.arange(64, 80)])
    vec["gq_r"] = _cols(qn[rf], 1)
    vec["gq_sw"] = _cols(qn[sw], 1)
    vec["gk_r"] = _cols(kn[rf], 1)
    vec["gk_sw"] = _cols(kn[sw], 1)
    rwperm = np.arange(1920)
    if rev:
        rwperm = np.concatenate([np.arange(0, 1536), np.arange(1600, 1664), np.arange(1536, 1600),
                                 np.arange(1728, 1792), np.arange(1664, 1728), np.arange(1792, 1920)])
    vec["mu_p"] = _cols(g("mu_prev")[rwperm], 15)
    vec["mu_n"] = _cols(g("mu_next")[rwperm], 15)
    for n in ("w0_f", "w0_b", "a0_f", "a0_b", "k_k", "k_a", "r_k"):
        vec[n] = _cols(g(n), 4)
    vec["g_mem"] = _cols(g("mem_norm_g"), 8)
    vec["g_xq"] = _cols(g("x_qn_g"), 1)
    vec["g_xk"] = _cols(g("x_kn_g"), 1)
    vec["g_ffn"] = _cols(g("norm_ffn_g"), 8)
    cw = g("conv_w")
    vec["cw0"] = _cols(cw[0], NFC)
    vec["cw1"] = _cols(cw[1], NFC)
    vec["cw2"] = _cols(cw[2], NFC)
    vec["cb"] = _cols(g("conv_b"), NFC)
    vecs = np.concatenate([vec[n] for n, _ in VEC_SPEC], axis=1).astype(np.float32)
    assert vecs.shape == (128, NVEC)
    w_in = g("w_in")
    w_in = np.concatenate([w_in[:, :672], w_in[:, 672:2592][:, rwperm], w_in[:, 2592:]], axis=1)
    w_in_ext = np.concatenate([w_in, w_in[:, 656:672], w_in[:, 640:656]], axis=1)
    w_uq = g("w_uq").reshape(384, 8, 96)
    w_uq_p = np.concatenate([w_uq[:, :, rf].reshape(384, 768), w_uq[:, :, sw].reshape(384, 256)], axis=1)
    w_ukv = g("w_ukv").reshape(256, 8, 128)
    w_ukvk = np.concatenate([np.zeros((256, 8, 32), np.float32), w_ukv[:, :, 0:64]], axis=2).reshape(256, 768)
    w_ukvv = w_ukv[:, :, 64:128].reshape(256, 512)
    w_mkv = g("w_mkv").reshape(D, 4, 256)
    half = 16
    inv = np.power(np.float32(10000.0), -np.arange(half, dtype=np.float32) / half).astype(np.float32)
    shared = {}
    for t_ in tlens:
        pos = np.arange(t_, dtype=np.float32)
        if rev:
            pos = pos[::-1]
        ang = pos[None, :] * inv[:, None]
        cos, sin = np.cos(ang).astype(np.float32), np.sin(ang).astype(np.float32)
        shared["ropec%d" % t_] = np.concatenate([cos, cos], 0)
        shared["ropes%d" % t_] = np.concatenate([-sin, sin], 0)
    shared.update({
        "vecs": vecs,
        "rowb": np.stack([np.tile(g("lnx_g")[None, :], (128, 1)), np.tile(g("lnx_b")[None, :], (128, 1))]).astype(np.float32),
        "w_in": w_in_ext, "w_uq": w_uq_p, "w_ukvk": w_ukvk, "w_ukvv": w_ukvv,
        "w_mk": w_mkv[:, :, 0:128].reshape(D, 512), "w_mv": w_mkv[:, :, 128:256].reshape(D, 512),
        "w2": np.concatenate([g("w2_f"), g("w2_b")], 0), "a2": np.concatenate([g("a2_f"), g("a2_b")], 0), "g2": g("g2"),
        "w_oa": g("w_o_a"), "w_ob": g("w_o_b"), "w_oc": g("w_o_c"), "w_out": g("w_out"),
        "w_up": g("w_up"), "w_dn": g("w_down"),
    })
    return {kk: np.ascontiguousarray(v, dtype=np.float32) for kk, v in shared.items()}


N_CORES = 8
UNITS = [2048, 2048, 2048, 2048, (8192, 4096, 4224, 9)]
_PROG = {}


def kernel(**inputs):
    xp = np.asarray(inputs["x_prompt"], np.float32)
    xsm = np.asarray(inputs["x_sample"], np.float32)
    mp = np.asarray(inputs["mem_prompt"], np.float32)
    msm = np.asarray(inputs["mem_sample"], np.float32)
    if "nc" not in _PROG:
        _PROG["nc"] = build_program(UNITS, 8192)[0]
    nc = _PROG["nc"]
    packs = [host_pack(inputs, [2048, 8192], rev=False), host_pack(inputs, [2048, 8192], rev=True)]
    in_maps = []
    for c in range(N_CORES):
        s_, rev = c // 2, c % 2 == 1
        seqs = [xp[4 * c + i] for i in range(4)] + [xsm[s_]]
        if rev:
            seqs = [a[::-1] for a in seqs]
        mems = np.concatenate([mp[4 * c + i] for i in range(4)] + [msm[s_]], axis=0)
        m = dict(packs[1 if rev else 0])
        m["xs"] = np.ascontiguousarray(np.concatenate(seqs, axis=0))
        m["mems"] = np.ascontiguousarray(mems)
        in_maps.append(m)
    res = run_bass_kernel_spmd(nc, in_maps, core_ids=list(range(N_CORES)))
    y_prompt = np.empty_like(xp)
    y_sample = np.empty_like(xsm)
    for c in range(N_CORES):
        s_, rev = c // 2, c % 2 == 1
        ys = np.asarray(res.results[c]["ys"], np.float32)
        for i in range(4):
            blk = ys[i * 2048:(i + 1) * 2048]
            y_prompt[4 * c + i] = blk[::-1] if rev else blk
        half = ys[8192:8192 + 4096]
        if rev:
            y_sample[s_, 4096:] = half[::-1]
        else:
            y_sample[s_, :4096] = half
    return (y_prompt, y_sample)
```

```python
import contextlib
import os
KVAR = os.environ.get('KVAR', '')
import numpy as np
import concourse.bass as bass
import concourse.mybir as mybir
from concourse.bass_utils import run_bass_kernel_spmd

F32 = mybir.dt.float32
BF16 = mybir.dt.bfloat16
AF = mybir.ActivationFunctionType
ALU = mybir.AluOpType
AX = mybir.AxisListType


class Buf:
    __slots__ = ("t", "lw", "rd", "name", "excl")

    def __init__(self, t, name, excl=False):
        self.t = t
        self.name = name
        self.excl = excl
        self.lw = None
        self.rd = {}


class KB:
    ENG = ("pe", "act", "dve", "pool", "sp")
    NDMA = 24

    def __init__(self, nc):
        self.nc = nc
        self.es = contextlib.ExitStack()
        self.E = {"pe": nc.tensor, "act": nc.scalar, "dve": nc.vector, "pool": nc.gpsimd, "sp": nc.sync}
        self.sems = {}
        self.cnt = {}
        for e in self.ENG:
            self.sems[e] = self.es.enter_context(nc.semaphore("s_" + e))
            self.cnt[e] = 0
        self.dsem = []
        for i in range(self.NDMA):
            key = "d%d" % i
            self.sems[key] = self.es.enter_context(nc.semaphore("s_" + key))
            self.cnt[key] = 0
            self.dsem.append(key)
        self.dnext = 0
        self.psem = []
        for i in range(12):
            key = "q%d" % i
            self.sems[key] = self.es.enter_context(nc.semaphore("s_" + key))
            self.cnt[key] = 0
            self.psem.append(key)
        self.pnext = 0
        self.waited = {e: {} for e in self.ENG}
        self.drams = {}
        self.out_events = []
        self.n_inst = 0

    def sb(self, name, shape, dt):
        self.uid = getattr(self, "uid", 0) + 1
        name = "%s_%d" % (name, self.uid)
        t = self.es.enter_context(self.nc.sbuf_tensor(name, list(shape), dt))
        return Buf(t, name)

    def ps(self, name, shape, dt):
        t = self.es.enter_context(self.nc.psum_tensor(name, list(shape), dt))
        return Buf(t, name)

    def dram(self, name):
        if name not in self.drams:
            self.drams[name] = Buf(None, name)
        return self.drams[name]

    def _wait(self, eng, deps):
        w = self.waited[eng]
        best = {}
        for d in deps:
            if d is None:
                continue
            key, val = d
            if eng == "pe" and key == "pe":
                continue
            if w.get(key, 0) >= val:
                continue
            if best.get(key, 0) < val:
                best[key] = val
        for key, val in best.items():
            w[key] = val
            sem = self.sems[key]
            self.E[eng].wait_ge(sem, val)

    def _deps(self, r, w):
        deps = []
        for b in r:
            deps.append(b.lw)
        for b in w:
            deps.append(b.lw)
            for key, val in b.rd.items():
                deps.append((key, val))
        return deps

    def _commit(self, ev, r, w):
        for b in w:
            b.lw = ev
            b.rd = {}
        for b in r:
            if b.rd.get(ev[0], 0) < ev[1]:
                b.rd[ev[0]] = ev[1]

    def op(self, eng, fn, r=(), w=(), inc=True):
        if eng != "pe":
            ex = [b for b in r if b.excl and b not in w]
            if ex:
                w = list(w) + ex
        self._wait(eng, self._deps(r, w))
        if not inc:
            assert eng == "pe"
            ev = (eng, self.cnt[eng] + 1)
            fn(self.E[eng])
            self._commit(ev, r, w)
            self.n_inst += 1
            return ev
        self.cnt[eng] += 1
        ev = (eng, self.cnt[eng])
        sem = self.sems[eng]
        fn(self.E[eng]).then_inc(sem, 1)
        self._commit(ev, r, w)
        self.n_inst += 1
        return ev

    def dma(self, eng, out, in_, r=(), w=(), is_out=False):
        if eng == "pool":
            key = self.psem[self.pnext]
            self.pnext = (self.pnext + 1) % len(self.psem)
        else:
            key = self.dsem[self.dnext]
            self.dnext = (self.dnext + 1) % self.NDMA
        deps = self._deps(r, w)
        if self.cnt[key] > 0:
            deps.append((key, self.cnt[key]))
        self._wait(eng, deps)
        self.cnt[key] += 16
        ev = (key, self.cnt[key])
        sem = self.sems[key]
        self.E[eng].dma_start(out=out, in_=in_).then_inc(sem, 16)
        self._commit(ev, r, w)
        if is_out:
            self.out_events.append(ev)
        self.n_inst += 1
        return ev

    def act(self, out, in_, func, r=(), w=(), **kw):
        return self.op("act", lambda e: e.activation(out=out, in_=in_, func=func, **kw), r=r, w=w)

    def mm(self, out, lhsT, rhs, start, stop, r=(), w=(), inc=True):
        return self.op("pe", lambda e: e.matmul(out, lhsT, rhs, start=start, stop=stop), r=r, w=w, inc=inc)

    def tr(self, out, in_, ident, r=(), w=(), inc=True):
        return self.op("pe", lambda e: e.transpose(out, in_, ident.t[:]), r=list(r) + [ident], w=w, inc=inc)

    def make_ident(self, ident, n=128):
        self.op("pool", lambda e: e.memset(ident.t[:], 1.0), w=[ident])
        self.op("pool", lambda e: e.affine_select(
            out=ident.t[:], in_=ident.t[:], pattern=[[-1, n]], compare_op=ALU.is_equal,
            fill=0.0, base=0, channel_multiplier=1), r=[ident], w=[ident])

    def finish(self):
        nc = self.nc
        final = {}
        for key, val in self.out_events:
            final[key] = max(final.get(key, 0), val)
        for key in self.dsem + self.psem:
            if self.cnt[key] > 0:
                final[key] = max(final.get(key, 0), self.cnt[key])
        for e in self.ENG:
            if self.cnt[e] > 0:
                final[e] = self.cnt[e]
        for key, val in final.items():
            self.E["sp"].wait_ge(self.sems[key], val)
        self.es.close()


    def ring(self, name, shape, dt, n):
        return Ring([self.sb("%s%d" % (name, i), shape, dt) for i in range(n)])

    def barrier(self):
        for e in self.ENG:
            deps = [(key, c) for key, c in self.cnt.items() if c > 0]
            w = self.waited[e]
            for key, val in deps:
                if key == e and e != "sp":
                    pass
                if w.get(key, 0) >= val:
                    continue
                w[key] = val
                sem = self.sems[key]
                self.E[e].wait_ge(sem, val)

    @contextlib.contextmanager
    def scope(self):
        old = self.es
        self.es = contextlib.ExitStack()
        try:
            yield
        finally:
            self.barrier()
            self.es.close()
            self.es = old


class Ring:
    def __init__(self, bufs):
        self.bufs = bufs
        self.i = 0

    def next(self):
        b = self.bufs[self.i]
        self.i = (self.i + 1) % len(self.bufs)
        return b


D = 1024
NMEM = 256
IN_COLS = 6176
DFF = 2816
NFC = 22
VEC_SPEC = [("g_mix", 8), ("g_q", 3), ("g_kv", 2), ("gq_r", 1), ("gq_sw", 1), ("gk_r", 1), ("gk_sw", 1),
            ("mu_p", 15), ("mu_n", 15), ("w0_f", 4), ("w0_b", 4), ("a0_f", 4), ("a0_b", 4), ("k_k", 4),
            ("k_a", 4), ("r_k", 4), ("g_mem", 8), ("g_xq", 1), ("g_xk", 1), ("g_ffn", 8),
            ("cw0", NFC), ("cw1", NFC), ("cw2", NFC), ("cb", NFC)]
VEC_OFF = {}
_o = 0
for _n, _c in VEC_SPEC:
    VEC_OFF[_n] = (_o, _c)
    _o += _c
NVEC = _o
DECAY_C = -0.6065306597126334
EPS = 1e-6


class StopBuild(Exception):
    pass


def build_program(units, tmax, debug=False, stop=None):
    try:
        return _build_program(units, tmax, debug, stop)
    except StopBuild as ex:
        nc, k = ex.args
        k.finish()
        return nc, k, {}


def _build_program(units, tmax, debug=False, stop=None):
    nc = bass.Bass("TRN2", target_bir_lowering=False)
    uspec = [u if isinstance(u, tuple) else (u, u, u, u // 512) for u in units]
    units = [u[0] for u in uspec]
    OWN = [u[1] for u in uspec]
    MIX = [u[2] for u in uspec]
    NQ = [u[3] for u in uspec]
    NU = len(units)
    TT = sum(units)
    tok0 = [sum(units[:i]) for i in range(NU)]
    own0 = [sum(OWN[:i]) for i in range(NU)]
    TOWN = sum(OWN)
    tlens = sorted(set(units))

    def din(name, shape, dt=F32):
        return nc.dram_tensor(name, list(shape), dt, kind="ExternalInput").ap()

    def dscr(name, shape, dt=F32):
        return nc.dram_tensor(name, list(shape), dt, kind="ExternalOutput" if debug else "Internal").ap()

    xs = din("xs", [TT, D])
    mems = din("mems", [NU * NMEM, D])
    ropec = {t_: din("ropec%d" % t_, [32, t_]) for t_ in tlens}
    ropes = {t_: din("ropes%d" % t_, [32, t_]) for t_ in tlens}
    vecs = din("vecs", [128, NVEC])
    rowb = din("rowb", [2, 128, 512])
    w_in = din("w_in", [D, IN_COLS + 32])
    w_uq = din("w_uq", [384, 1024])
    w_ukvk = din("w_ukvk", [256, 768])
    w_ukvv = din("w_ukvv", [256, 512])
    w_mk = din("w_mk", [D, 512])
    w_mv = din("w_mv", [D, 512])
    w2 = din("w2", [128, 512])
    a2 = din("a2", [128, 512])
    g2 = din("g2", [128, 512])
    w_oa = din("w_oa", [512, D])
    w_ob = din("w_ob", [512, D])
    w_oc = din("w_oc", [512, D])
    w_out = din("w_out", [D, D])
    w_up = din("w_up", [D, 2 * DFF])
    w_dn = din("w_dn", [DFF, D])
    ys = nc.dram_tensor("ys", [TOWN, D], F32, kind="ExternalOutput").ap()

    xT_s = dscr("xT_s", [D, TT])
    rw_s = dscr("rw_s", [1920, TT])
    gate_s = dscr("gate_s", [3072, TT], BF16)
    qT_s = dscr("qT_s", [768, TT], BF16)
    kT_s = dscr("kT_s", [768, TT], BF16)
    v_s = dscr("v_s", [TT, 512], BF16)
    ocT_s = dscr("ocT_s", [512, TT], BF16)
    oa_s = dscr("oa_s", [TT, 512])
    yf_s = dscr("yf_s", [TT, 512])
    x1T_s = dscr("x1T_s", [D, TT])

    k = KB(nc)
    DR = k.dram

    dbg_list = []

    def dbg(name, buf, ap):
        if not debug or name in dbg_list:
            return
        dbg_list.append(name)
        shp = list(ap.shape)
        dt_ = nc.dram_tensor("dbg_" + name, shp, ap.dtype, kind="ExternalOutput").ap()
        k.dma("sp", dt_, ap, r=[buf], w=[DR("dbg_" + name)], is_out=True)

    def chk(name):
        if stop == name:
            raise StopBuild(nc, k)

    pst = k.ps("pst", [128, 8, 512], F32)
    banks = [Buf(pst.t, "bank%d" % i, excl=True) for i in range(8)]

    def bk(i):
        return banks[i].t[:, i, :]

    ident = k.sb("ident", [128, 128], F32)
    k.make_ident(ident)
    identb = k.sb("identb", [128, 128], BF16)
    k.op("dve", lambda e: e.tensor_copy(identb.t[:], ident.t[:]), r=[ident], w=[identb])
    onesb = k.sb("onesb", [128, 128], BF16)
    k.op("pool", lambda e: e.memset(onesb.t[:], 1.0), w=[onesb])
    VEC = k.sb("VEC", [128, NVEC], F32)
    k.dma("sp", VEC.t[:], vecs, w=[VEC])

    def V(name, j=0, n=1, rows=slice(0, 128)):
        o, c = VEC_OFF[name]
        return VEC.t[rows, o + j:o + j + n]

    bank_rr = {}

    def nb(lo=0, hi=8):
        i = bank_rr.get((lo, hi), hi - 1)
        i = lo + ((i + 1 - lo) % (hi - lo))
        bank_rr[(lo, hi)] = i
        return i

    def rstd(out_ap, in_ap, scale, eps, r, w):
        k.act(out_ap, in_ap, AF.Sqrt, r=r, w=w, bias=eps, scale=scale)
        k.op("dve", lambda e: e.reciprocal(out_ap, out_ap), r=w, w=w)

    stage = [None]

    def load_w(dst, dst_ap_fn, src, ncols, gain=None, rows=128, eng_i=[0]):
        CH = 1024
        for c0 in range(0, ncols, CH):
            n = min(CH, ncols - c0)
            st = stage[0].next()
            k.dma("sp", st.t[0:rows, 0:n], src[:, c0:c0 + n], w=[st])
            d = dst_ap_fn(c0, n)
            if gain is not None:
                k.op("dve", lambda e: e.tensor_scalar(d, st.t[0:rows, 0:n], gain, None, ALU.mult), r=[st, VEC], w=[dst])
            else:
                eng_i[0] ^= 1
                if eng_i[0]:
                    k.op("pool", lambda e: e.tensor_copy(d, st.t[0:rows, 0:n]), r=[st], w=[dst])
                else:
                    k.op("act", lambda e: e.copy(d, st.t[0:rows, 0:n]), r=[st], w=[dst])

    def proj(Wt, hT, m0, M, n, kcs=8):
        b = nb()
        for kc in range(kcs):
            k.mm(banks[b].t[0:M, b, 0:n], Wt.t[:, kc, m0:m0 + M], hT.t[:, kc, 0:n], kc == 0, kc == kcs - 1,
                 r=[Wt, hT], w=[banks[b]])
        return b

    def make_hT(xTf, n, xsq, hT, RSTD):
        k.op("pool", lambda e: e.tensor_tensor(xsq.t[:, :, 0:n], xTf.t[:, :, 0:n], xTf.t[:, :, 0:n], ALU.mult), r=[xTf], w=[xsq])
        b = nb()
        for kc in range(8):
            k.mm(banks[b].t[:, b, 0:n], onesb.t[:], xsq.t[:, kc, 0:n], kc == 0, kc == 7, r=[onesb, xsq], w=[banks[b]])
        rstd(RSTD.t[:, 0:n], banks[b].t[:, b, 0:n], 1.0 / D, EPS, [banks[b]], [RSTD])
        k.op("dve", lambda e: e.tensor_tensor(hT.t[:, :, 0:n], xTf.t[:, :, 0:n],
                                              RSTD.t[:, 0:n].unsqueeze(1).to_broadcast([128, 8, n]), ALU.mult),
             r=[xTf, RSTD], w=[hT])

    with k.scope():
        stage[0] = k.ring("stg", [128, 1024], F32, 2)
        NA = 1920 + 3072
        WA = k.sb("WA", [128, 8, NA], BF16)
        for kc in range(8):
            load_w(WA, lambda c0, n, kc=kc: WA.t[:, kc, c0:c0 + n], w_in[kc * 128:(kc + 1) * 128, 672:2592], 1920, gain=V("g_mix", kc))
            load_w(WA, lambda c0, n, kc=kc: WA.t[:, kc, 1920 + c0:1920 + c0 + n], w_in[kc * 128:(kc + 1) * 128, 3104:6176], 3072, gain=V("g_mix", kc))
        xin = k.ring("xin", [128, D], F32, 4)
        xTf = k.sb("xTf", [128, 8, 512], F32)
        xsq = k.sb("xsq", [128, 8, 512], BF16)
        hT = k.sb("hT", [128, 8, 512], BF16)
        RSTD = k.sb("RSTD", [128, 512], F32)
        RWS = k.sb("RWS", [128, 15, 512], F32)
        GS = k.sb("GS", [128, 24, 512], BF16)
        for u in range(NU):
            for blk in range(units[u] // 512):
                t0 = tok0[u] + blk * 512
                xt = []
                for j in range(4):
                    x_ = xin.next()
                    k.dma("sp", x_.t[:], xs[t0 + j * 128:t0 + (j + 1) * 128, :], w=[x_])
                    xt.append(x_)
                for kc in range(8):
                    b = nb()
                    for j in range(4):
                        k.tr(banks[b].t[:, b, j * 128:(j + 1) * 128], xt[j].t[:, kc * 128:(kc + 1) * 128], ident, r=[xt[j]], w=[banks[b]])
                    if kc % 2 == 0:
                        k.op("act", lambda e, b=b, kc=kc: e.copy(xTf.t[:, kc, :], banks[b].t[:, b, :]), r=[banks[b]], w=[xTf])
                    else:
                        k.op("dve", lambda e, b=b, kc=kc: e.tensor_copy(xTf.t[:, kc, :], banks[b].t[:, b, :]), r=[banks[b]], w=[xTf])
                k.dma("pool", xT_s[:, t0:t0 + 512].rearrange("(c p) t -> p c t", p=128), xTf.t[:], r=[xTf], w=[DR("xT")])
                make_hT(xTf, 512, xsq, hT, RSTD)
                for j in range(15):
                    b = proj(WA, hT, j * 128, 128, 512)
                    if j % 2 == 0:
                        k.op("act", lambda e, b=b, j=j: e.copy(RWS.t[:, j, :], banks[b].t[:, b, :]), r=[banks[b]], w=[RWS])
                    else:
                        k.op("dve", lambda e, b=b, j=j: e.tensor_copy(RWS.t[:, j, :], banks[b].t[:, b, :]), r=[banks[b]], w=[RWS])
                k.dma("pool", rw_s[:, t0:t0 + 512].rearrange("(c p) t -> p c t", p=128), RWS.t[:], r=[RWS], w=[DR("rw")])
                if blk < NQ[u]:
                    for j in range(24):
                        b = proj(WA, hT, 1920 + j * 128, 128, 512)
                        k.act(GS.t[:, j, :], banks[b].t[:, b, :], AF.Sigmoid, r=[banks[b]], w=[GS])
                    k.dma("pool", gate_s[:, t0:t0 + 512].rearrange("(c p) t -> p c t", p=128), GS.t[:], r=[GS], w=[DR("gate")])

    chk("p1a")
    def headnorm_rope(prologue, g_r, g_sw, COS, SIN, dst, T_):
        SQ, RH, QN, SW, T1, T2, QB = T_
        src96, src_r, sw_ap, sw_r = prologue()
        yield
        k.act(SQ.t[0:96, :], src96, AF.Square, r=src_r, w=[SQ])
        yield
        b2 = nb()
        k.mm(banks[b2].t[0:96, b2, :], onesb.t[0:96, 0:96], SQ.t[0:96, :], True, True, r=[onesb, SQ], w=[banks[b2]])
        yield
        k.act(RH.t[0:96, :], banks[b2].t[0:96, b2, :], AF.Sqrt, r=[banks[b2]], w=[RH], bias=EPS, scale=1.0 / 96)
        yield
        k.op("dve", lambda e: e.reciprocal(RH.t[0:96, :], RH.t[0:96, :]), r=[RH], w=[RH])
        yield
        k.op("dve", lambda e: e.scalar_tensor_tensor(QN.t[0:96, :], src96, g_r, RH.t[0:96, :], ALU.mult, ALU.mult),
             r=list(src_r) + [RH, VEC], w=[QN])
        k.op("dve", lambda e: e.scalar_tensor_tensor(SW.t[0:32, :], sw_ap, g_sw, RH.t[0:32, :], ALU.mult, ALU.mult),
             r=list(sw_r) + [RH, VEC], w=[SW])
        yield
        k.op("pool", lambda e: e.tensor_tensor(T1.t[0:32, :], QN.t[0:32, :], COS.t[:, :], ALU.mult), r=[QN, COS], w=[T1])
        k.op("pool", lambda e: e.tensor_tensor(T2.t[0:32, :], SW.t[0:32, :], SIN.t[:, :], ALU.mult), r=[SW, SIN], w=[T2])
        k.act(QB.t[0:96, :], QN.t[0:96, :], AF.Copy, r=[QN], w=[QB])
        yield
        k.op("pool", lambda e: e.tensor_tensor(QB.t[0:32, :], T1.t[0:32, :], T2.t[0:32, :], ALU.add), r=[T1, T2, QB], w=[QB])
        yield
        k.dma("pool", dst, QB.t[0:96, :], r=[QB], w=[DR("qk")])

    def lockstep(gens):
        gens = list(gens)
        while gens:
            for g_ in list(gens):
                try:
                    next(g_)
                except StopIteration:
                    gens.remove(g_)

    with k.scope():
        stage[0] = k.ring("stg", [128, 1024], F32, 2)
        NB1 = 672 + 32 + 512
        WB = k.sb("WB", [128, 8, NB1], BF16)
        for kc in range(8):
            rs = slice(kc * 128, (kc + 1) * 128)
            load_w(WB, lambda c0, n, kc=kc: WB.t[:, kc, c0:c0 + n], w_in[rs, 0:672], 672, gain=V("g_mix", kc))
            load_w(WB, lambda c0, n, kc=kc: WB.t[:, kc, 672 + c0:672 + c0 + n], w_in[rs, 6176:6208], 32, gain=V("g_mix", kc))
            load_w(WB, lambda c0, n, kc=kc: WB.t[:, kc, 704 + c0:704 + c0 + n], w_in[rs, 2592:3104], 512, gain=V("g_mix", kc))
        Wuq = k.sb("Wuq", [128, 3, 1024], BF16)
        for j in range(3):
            load_w(Wuq, lambda c0, n, j=j: Wuq.t[:, j, c0:c0 + n], w_uq[j * 128:(j + 1) * 128, :], 1024)
        Wkk = k.sb("Wkk", [128, 2, 768], BF16)
        Wkv = k.sb("Wkv", [128, 2, 512], BF16)
        for j in range(2):
            load_w(Wkk, lambda c0, n, j=j: Wkk.t[:, j, c0:c0 + n], w_ukvk[j * 128:(j + 1) * 128, :], 768)
            load_w(Wkv, lambda c0, n, j=j: Wkv.t[:, j, c0:c0 + n], w_ukvv[j * 128:(j + 1) * 128, :], 512)
        Wmk = k.sb("Wmk", [128, 8, 512], BF16)
        Wmv = k.sb("Wmv", [128, 8, 512], BF16)
        for kc in range(8):
            rs = slice(kc * 128, (kc + 1) * 128)
            load_w(Wmk, lambda c0, n, kc=kc: Wmk.t[:, kc, c0:c0 + n], w_mk[rs, :], 512, gain=V("g_mem", kc))
            load_w(Wmv, lambda c0, n, kc=kc: Wmv.t[:, kc, c0:c0 + n], w_mv[rs, :], 512, gain=V("g_mem", kc))
        xTf = k.sb("xTf", [128, 8, 512], F32)
        xsq = k.sb("xsq", [128, 8, 512], BF16)
        hT = k.sb("hT", [128, 8, 512], BF16)
        RSTD = k.sb("RSTD", [128, 512], F32)
        CQ = k.sb("CQ", [128, 3, 512], F32)
        CQs = k.sb("CQs", [128, 3, 512], BF16)
        CQN = k.sb("CQN", [128, 3, 512], BF16)
        CKV = k.sb("CKV", [128, 2, 512], F32)
        CKVs = k.sb("CKVs", [128, 2, 512], BF16)
        CKVN = k.sb("CKVN", [128, 2, 512], BF16)
        RQ = k.sb("RQ", [128, 512], F32)
        KRb = k.sb("KRb", [32, 512], BF16)
        KRSW = k.sb("KRSW", [32, 512], F32)
        QSW = k.sb("QSW", [32, 512], F32)
        COS = k.sb("COS", [32, 512], F32)
        SIN = k.sb("SIN", [32, 512], F32)
        TQ = [k.ring("hn%d" % i, [96, 512], BF16 if i in (0, 6) else F32, 2) for i in range(7)]
        VB = k.ring("VB", [128, 512], BF16, 2)
        XQ = k.sb("XQ", [128, 512], F32)
        XQs = k.sb("XQs", [128, 512], BF16)
        XQN = k.sb("XQN", [128, 512], BF16)
        RHx = k.sb("RHx", [128, 512], F32)
        PTx = k.ring("PTx", [128, 512], BF16, 2)
        RD = k.sb("RD", [128, 512], F32)
        OC = k.sb("OC", [128, 4, 512], BF16)
        memt = k.ring("memt", [128, D], F32, 2)
        junk = k.sb("junk", [128, D], F32)
        ssm = k.sb("ssm", [128, 2], F32)
        MEMT = k.sb("MEMT", [128, 8, 256], F32)
        MEMTb = k.sb("MEMTb", [128, 8, 256], BF16)
        msq = k.sb("msq", [128, 8, 256], BF16)
        RSM = k.sb("RSM", [128, 256], F32)
        MKr = k.sb("MKr", [128, 256], F32)
        MKs = k.sb("MKs", [128, 256], BF16)
        RHm = k.sb("RHm", [128, 256], F32)
        MK = k.sb("MK", [128, 4, 256], BF16)
        MV = k.sb("MV", [128, 2, 512], BF16)
        sel32 = k.sb("sel32", [32, 96], BF16)
        k.op("pool", lambda e: e.memset(sel32.t[:], 0.0), w=[sel32])
        k.op("pool", lambda e: e.tensor_copy(sel32.t[0:32, 0:32], ident.t[0:32, 0:32]), r=[ident, sel32], w=[sel32])

        for u in range(NU):
            ml = []
            for mt in range(2):
                m_ = memt.next()
                k.dma("sp", m_.t[:], mems[u * NMEM + mt * 128:u * NMEM + (mt + 1) * 128, :], w=[m_])
                k.act(junk.t[:], m_.t[:], AF.Square, r=[m_], w=[junk, ssm], accum_out=ssm.t[:, mt:mt + 1])
                ml.append(m_)
            rstd(ssm.t[:, 0:2], ssm.t[:, 0:2], 1.0 / D, EPS, [ssm], [ssm])
            for kc in range(8):
                b = nb()
                for mt in range(2):
                    k.tr(banks[b].t[:, b, mt * 128:(mt + 1) * 128], ml[mt].t[:, kc * 128:(kc + 1) * 128], ident, r=[ml[mt]], w=[banks[b]])
                k.op("act", lambda e, b=b, kc=kc: e.copy(MEMT.t[:, kc, :], banks[b].t[:, b, 0:256]), r=[banks[b]], w=[MEMT])
            k.op("dve", lambda e: e.tensor_copy(MEMTb.t[:], MEMT.t[:]), r=[MEMT], w=[MEMTb])
            k.op("pool", lambda e: e.tensor_tensor(msq.t[:], MEMT.t[:], MEMT.t[:], ALU.mult), r=[MEMT], w=[msq])
            b = nb()
            for kc in range(8):
                k.mm(banks[b].t[:, b, 0:256], onesb.t[:], msq.t[:, kc, :], kc == 0, kc == 7, r=[onesb, msq], w=[banks[b]])
            rstd(RSM.t[:], banks[b].t[:, b, 0:256], 1.0 / D, EPS, [banks[b]], [RSM])
            for h in range(4):
                b = nb()
                for kc in range(8):
                    k.mm(banks[b].t[:, b, 0:256], Wmk.t[:, kc, h * 128:(h + 1) * 128], MEMTb.t[:, kc, :], kc == 0, kc == 7,
                         r=[Wmk, MEMTb], w=[banks[b]])
                k.op("dve", lambda e, b=b: e.tensor_tensor(MKr.t[:], banks[b].t[:, b, 0:256], RSM.t[:], ALU.mult), r=[banks[b], RSM], w=[MKr])
                k.op("pool", lambda e: e.tensor_tensor(MKs.t[:], MKr.t[:], MKr.t[:], ALU.mult), r=[MKr], w=[MKs])
                b2 = nb()
                k.mm(banks[b2].t[:, b2, 0:256], onesb.t[:], MKs.t[:], True, True, r=[onesb, MKs], w=[banks[b2]])
                rstd(RHm.t[:], banks[b2].t[:, b2, 0:256], 1.0 / 128, EPS, [banks[b2]], [RHm])
                k.op("dve", lambda e, h=h: e.scalar_tensor_tensor(MK.t[:, h, :], MKr.t[:], V("g_xk"), RHm.t[:], ALU.mult, ALU.mult),
                     r=[MKr, RHm, VEC], w=[MK])
            for mt in range(2):
                b = nb()
                for kc in range(8):
                    k.mm(banks[b].t[:, b, :], MEMTb.t[:, kc, mt * 128:(mt + 1) * 128], Wmv.t[:, kc, :], kc == 0, kc == 7,
                         r=[Wmv, MEMTb], w=[banks[b]])
                k.op("dve", lambda e, b=b, mt=mt: e.tensor_scalar(MV.t[:, mt, :], banks[b].t[:, b, :], ssm.t[:, mt:mt + 1], None, ALU.mult),
                     r=[banks[b], ssm], w=[MV])

            for blk in range(units[u] // 512):
                t0 = tok0[u] + blk * 512
                p0 = blk * 512
                k.dma("sp", xTf.t[:], xT_s[:, t0:t0 + 512].rearrange("(c p) t -> p c t", p=128), r=[DR("xT")], w=[xTf])
                k.dma("sp", COS.t[:], ropec[units[u]][:, p0:p0 + 512], w=[COS])
                k.dma("sp", SIN.t[:], ropes[units[u]][:, p0:p0 + 512], w=[SIN])
                isq = blk < NQ[u]
                make_hT(xTf, 512, xsq, hT, RSTD)
                for j in range(3 if isq else 0):
                    b = proj(WB, hT, j * 128, 128, 512)
                    k.op("act", lambda e, b=b, j=j: e.copy(CQ.t[:, j, :], banks[b].t[:, b, :]), r=[banks[b]], w=[CQ])
                if isq:
                    k.op("pool", lambda e: e.tensor_tensor(CQs.t[:], CQ.t[:], CQ.t[:], ALU.mult), r=[CQ], w=[CQs])
                    b = nb()
                    for j in range(3):
                        k.mm(banks[b].t[:, b, :], onesb.t[:], CQs.t[:, j, :], j == 0, j == 2, r=[onesb, CQs], w=[banks[b]])
                    rstd(RQ.t[:], banks[b].t[:, b, :], 1.0 / 384, EPS, [banks[b]], [RQ])
                    for j in range(3):
                        k.op("dve", lambda e, j=j: e.scalar_tensor_tensor(CQN.t[:, j, :], CQ.t[:, j, :], V("g_q", j), RQ.t[:], ALU.mult, ALU.mult),
                             r=[CQ, RQ, VEC], w=[CQN])
                for j in range(2):
                    b = proj(WB, hT, 384 + j * 128, 128, 512)
                    k.op("act", lambda e, b=b, j=j: e.copy(CKV.t[:, j, :], banks[b].t[:, b, :]), r=[banks[b]], w=[CKV])
                k.op("pool", lambda e: e.tensor_tensor(CKVs.t[:], CKV.t[:], CKV.t[:], ALU.mult), r=[CKV], w=[CKVs])
                b = nb()
                for j in range(2):
                    k.mm(banks[b].t[:, b, :], onesb.t[:], CKVs.t[:, j, :], j == 0, j == 1, r=[onesb, CKVs], w=[banks[b]])
                rstd(RQ.t[:], banks[b].t[:, b, :], 1.0 / 256, EPS, [banks[b]], [RQ])
                for j in range(2):
                    k.op("dve", lambda e, j=j: e.scalar_tensor_tensor(CKVN.t[:, j, :], CKV.t[:, j, :], V("g_kv", j), RQ.t[:], ALU.mult, ALU.mult),
                         r=[CKV, RQ, VEC], w=[CKVN])
                b = proj(WB, hT, 640, 32, 512)
                k.op("act", lambda e, b=b: e.copy(KRb.t[:], banks[b].t[0:32, b, :]), r=[banks[b]], w=[KRb])
                b = proj(WB, hT, 672, 32, 512)
                k.op("act", lambda e, b=b: e.copy(KRSW.t[:], banks[b].t[0:32, b, :]), r=[banks[b]], w=[KRSW])
                for h in range(8):
                    gens = []
                    if isq:
                        def pro_q(h=h):
                            bq = nb()
                            for j in range(3):
                                k.mm(banks[bq].t[0:96, bq, :], Wuq.t[:, j, h * 96:(h + 1) * 96], CQN.t[:, j, :], j == 0, j == 2, r=[Wuq, CQN], w=[banks[bq]])
                            bs = nb()
                            for j in range(3):
                                k.mm(banks[bs].t[0:32, bs, :], Wuq.t[:, j, 768 + h * 32:768 + (h + 1) * 32], CQN.t[:, j, :], j == 0, j == 2, r=[Wuq, CQN], w=[banks[bs]])
                            return banks[bq].t[0:96, bq, :], [banks[bq]], banks[bs].t[0:32, bs, :], [banks[bs]]
                        gens.append(headnorm_rope(pro_q, V("gq_r", rows=slice(0, 96)), V("gq_sw", rows=slice(0, 32)), COS, SIN,
                                                  qT_s[h * 96:(h + 1) * 96, t0:t0 + 512], [r_.next() for r_ in TQ]))

                    def pro_k(h=h):
                        bq = nb()
                        for j in range(2):
                            k.mm(banks[bq].t[0:96, bq, :], Wkk.t[:, j, h * 96:(h + 1) * 96], CKVN.t[:, j, :], j == 0, False, r=[Wkk, CKVN], w=[banks[bq]])
                        k.mm(banks[bq].t[0:96, bq, :], sel32.t[0:32, 0:96], KRb.t[0:32, :], False, True, r=[sel32, KRb], w=[banks[bq]])
                        return banks[bq].t[0:96, bq, :], [banks[bq]], KRSW.t[0:32, :], [KRSW]
                    gens.append(headnorm_rope(pro_k, V("gk_r", rows=slice(0, 96)), V("gk_sw", rows=slice(0, 32)), COS, SIN,
                                              kT_s[h * 96:(h + 1) * 96, t0:t0 + 512], [r_.next() for r_ in TQ]))
                    lockstep(gens)
                for j in range(4):
                    b = nb()
                    for kc in range(2):
                        k.mm(banks[b].t[:, b, :], CKVN.t[:, kc, j * 128:(j + 1) * 128], Wkv.t[:, kc, :], kc == 0, kc == 1, r=[CKVN, Wkv], w=[banks[b]])
                    vb = VB.next()
                    k.op("act", lambda e, b=b, vb=vb: e.copy(vb.t[:], banks[b].t[:, b, :]), r=[banks[b]], w=[vb])
                    k.dma("pool", v_s[t0 + j * 128:t0 + (j + 1) * 128, :], vb.t[:], r=[vb], w=[DR("v")])
                for h in range(4 if isq else 0):
                    b = proj(WB, hT, 704 + h * 128, 128, 512)
                    k.op("act", lambda e, b=b: e.copy(XQ.t[:], banks[b].t[:, b, :]), r=[banks[b]], w=[XQ])
                    k.op("pool", lambda e: e.tensor_tensor(XQs.t[:], XQ.t[:], XQ.t[:], ALU.mult), r=[XQ], w=[XQs])
                    b2 = nb()
                    k.mm(banks[b2].t[:, b2, :], onesb.t[:], XQs.t[:], True, True, r=[onesb, XQs], w=[banks[b2]])
                    rstd(RHx.t[:], banks[b2].t[:, b2, :], 1.0 / 128, EPS, [banks[b2]], [RHx])
                    k.op("dve", lambda e: e.scalar_tensor_tensor(XQN.t[:], XQ.t[:], V("g_xq"), RHx.t[:], ALU.mult, ALU.mult),
                         r=[XQ, RHx, VEC], w=[XQN])
                    bo = nb(0, 2)
                    bd = 2 + bo
                    for mt in range(2):
                        bs = nb(4, 8)
                        k.mm(banks[bs].t[:, bs, :], MK.t[:, h, mt * 128:(mt + 1) * 128], XQN.t[:], True, True, r=[MK, XQN], w=[banks[bs]])
                        pt = PTx.next()
                        k.act(pt.t[:], banks[bs].t[:, bs, :], AF.Exp, r=[banks[bs]], w=[pt], scale=128 ** -0.5)
                        k.mm(banks[bo].t[:, bo, :], MV.t[:, mt, h * 128:(h + 1) * 128], pt.t[:], mt == 0, mt == 1, r=[MV, pt], w=[banks[bo]])
                        k.mm(banks[bd].t[:, bd, :], onesb.t[:], pt.t[:], mt == 0, mt == 1, r=[onesb, pt], w=[banks[bd]])
                    k.op("dve", lambda e, bd=bd: e.reciprocal(RD.t[:], banks[bd].t[:, bd, :]), r=[banks[bd]], w=[RD])
                    k.op("dve", lambda e, bo=bo, h=h: e.tensor_tensor(OC.t[:, h, :], banks[bo].t[:, bo, :], RD.t[:], ALU.mult), r=[banks[bo], RD], w=[OC])
                if isq:
                    k.dma("pool", ocT_s[:, t0:t0 + 512].rearrange("(c p) t -> p c t", p=128), OC.t[:], r=[OC], w=[DR("oc")])

    chk("p1b")
    with k.scope():
        KT = k.ring("KT", [96, tmax], BF16, 2)
        VA = k.ring("VA", [128, tmax // 128, 65], BF16, 2)
        for va in VA.bufs:
            k.op("pool", lambda e, va=va: e.memset(va.t[:], 1.0), w=[va])
        QT = k.ring("QT", [96, 512], BF16, 3)
        PT = k.ring("PT", [128, 512], BF16, 4)
        RDa = k.ring("RDa", [128, 4, 1], F32, 2)
        OA = k.ring("OA", [128, 4, 64], F32, 2)
        for u in range(NU):
            T = units[u]
            nkt = T // 128
            for h in range(8):
                kt_ = KT.next()
                va = VA.next()
                k.dma("sp", kt_.t[:, 0:T], kT_s[h * 96:(h + 1) * 96, tok0[u]:tok0[u] + T], r=[DR("qk")], w=[kt_])
                k.dma("sp", va.t[:, 0:nkt, 0:64],
                      v_s[tok0[u]:tok0[u] + T, h * 64:(h + 1) * 64].rearrange("(n p) c -> p n c", p=128), r=[DR("v")], w=[va])
                for qb in range(NQ[u]):
                    t0 = tok0[u] + qb * 512
                    qt = QT.next()
                    k.dma("sp", qt.t[:], qT_s[h * 96:(h + 1) * 96, t0:t0 + 512], r=[DR("qk")], w=[qt])
                    bo = nb(6, 8)
                    AHEAD = 2
                    bsq = []

                    def issue_s(kt2):
                        bs_ = nb(0, 6)
                        k.mm(banks[bs_].t[:, bs_, :], kt_.t[:, kt2 * 128:(kt2 + 1) * 128], qt.t[:], True, True, r=[kt_, qt], w=[banks[bs_]])
                        bsq.append(bs_)

                    for kt2 in range(min(AHEAD, nkt)):
                        issue_s(kt2)
                    for kt in range(nkt):
                        if kt + AHEAD < nkt:
                            issue_s(kt + AHEAD)
                        bs = bsq.pop(0)
                        pt = PT.next()
                        k.act(pt.t[:], banks[bs].t[:, bs, :], AF.Exp, r=[banks[bs]], w=[pt], scale=96 ** -0.5)
                        for j in range(4):
                            k.op("pe", lambda e, bo=bo, j=j, pt=pt, va=va, kt=kt: e.matmul(
                                banks[bo].t[:, bo, j * 65:(j + 1) * 65], pt.t[:, j * 128:(j + 1) * 128], va.t[:, kt, :],
                                start=(kt == 0 and j == 0), stop=(kt == nkt - 1), skip_group_check=True), r=[pt, va], w=[banks[bo]])
                    ov = banks[bo].t[:, bo, 0:260].rearrange("p (j c) -> p j c", c=65)
                    rd = RDa.next()
                    oa = OA.next()
                    k.op("dve", lambda e, rd=rd, ov=ov: e.reciprocal(rd.t[:], ov[:, :, 64:65]), r=[banks[bo]], w=[rd])
                    k.op("dve", lambda e, rd=rd, ov=ov, oa=oa: e.tensor_tensor(oa.t[:], ov[:, :, 0:64], rd.t[:].to_broadcast([128, 4, 64]), ALU.mult),
                         r=[banks[bo], rd], w=[oa])
                    k.dma("pool", oa_s[t0:t0 + 512, h * 64:(h + 1) * 64].rearrange("(j p) c -> p j c", p=128), oa.t[:], r=[oa], w=[DR("oa")])

    chk("p2")
    def bc3(ap2, n):
        return ap2.unsqueeze(2).to_broadcast([128, ap2.shape[1], n])

    with k.scope():
        W2 = k.sb("W2", [128, 512], BF16)
        A2 = k.sb("A2", [128, 512], BF16)
        G2 = k.sb("G2", [128, 512], BF16)
        Woa = k.sb("Woa", [128, 4, D], BF16)
        Wob = k.sb("Wob", [128, 4, D], BF16)
        Woc = k.sb("Woc", [128, 4, D], BF16)
        Wo = k.sb("Wo", [128, 8, D], BF16)
        stage_scope = k.scope()
        stage_scope.__enter__()
        stage[0] = k.ring("stg", [128, 1024], F32, 2)
        load_w(W2, lambda c0, n: W2.t[:, c0:c0 + n], w2, 512)
        load_w(A2, lambda c0, n: A2.t[:, c0:c0 + n], a2, 512)
        load_w(G2, lambda c0, n: G2.t[:, c0:c0 + n], g2, 512)
        for j in range(4):
            load_w(Woa, lambda c0, n, j=j: Woa.t[:, j, c0:c0 + n], w_oa[j * 128:(j + 1) * 128, :], D)
            load_w(Wob, lambda c0, n, j=j: Wob.t[:, j, c0:c0 + n], w_ob[j * 128:(j + 1) * 128, :], D)
            load_w(Woc, lambda c0, n, j=j: Woc.t[:, j, c0:c0 + n], w_oc[j * 128:(j + 1) * 128, :], D)
        for j in range(8):
            load_w(Wo, lambda c0, n, j=j: Wo.t[:, j, c0:c0 + n], w_out[j * 128:(j + 1) * 128, :], D)
        stage_scope.__exit__(None, None, None)
        LNG = k.sb("LNG", [128, 512], F32)
        LNB = k.sb("LNB", [128, 512], F32)
        k.dma("sp", LNG.t[:], rowb[0], w=[LNG])
        k.dma("sp", LNB.t[:], rowb[1], w=[LNB])
        C0 = k.sb("C0", [128, 15], F32)
        k.op("dve", lambda e: e.tensor_tensor(C0.t[:], V("mu_p", 0, 15), V("mu_n", 0, 15), ALU.add), r=[VEC], w=[C0])
        k.op("dve", lambda e: e.tensor_scalar(C0.t[:], C0.t[:], -1.0, 1.0, ALU.mult, ALU.add), r=[C0], w=[C0])
        OMK = k.sb("OMK", [128, 4], F32)
        k.op("dve", lambda e: e.tensor_scalar(OMK.t[:], V("k_a", 0, 4), -1.0, 1.0, ALU.mult, ALU.add), r=[VEC], w=[OMK])
        def X4(nm):
            if nm == "omk":
                return bc3(OMK.t[:, :], 128)
            return bc3(V(nm, 0, 4), 128)

        BD = k.sb("BD", [128, 128], BF16)
        k.op("pool", lambda e: e.memset(BD.t[:], 0.0), w=[BD])
        k.op("pool", lambda e: e.memset(BD.t[0:64, 0:64], 1.0), r=[BD], w=[BD])
        k.op("pool", lambda e: e.memset(BD.t[64:128, 64:128], 1.0), r=[BD], w=[BD])
        HSEL = k.sb("HSEL", [128, 2], BF16)
        k.op("pool", lambda e: e.memset(HSEL.t[:], 0.0), w=[HSEL])
        k.op("pool", lambda e: e.memset(HSEL.t[0:64, 0:1], 1.0), r=[HSEL], w=[HSEL])
        k.op("pool", lambda e: e.memset(HSEL.t[64:128, 1:2], 1.0), r=[HSEL], w=[HSEL])
        MASK = []
        MASKT = []
        for d in range(2):
            m_ = k.sb("MASK%d" % d, [128, 256], F32)
            mt_ = k.sb("MASKT%d" % d, [128, 128], F32)
            k.op("pool", lambda e, m_=m_: e.memset(m_.t[:], 1.0), w=[m_])
            k.op("pool", lambda e, mt_=mt_: e.memset(mt_.t[:], 1.0), w=[mt_])
            if d == 0:
                k.op("pool", lambda e, m_=m_: e.affine_select(out=m_.t[:, 0:128], in_=m_.t[:, 0:128], pattern=[[1, 128]], compare_op=ALU.is_gt, fill=0.0, base=0, channel_multiplier=-1), r=[m_], w=[m_])
                k.op("pool", lambda e, m_=m_: e.affine_select(out=m_.t[:, 128:256], in_=m_.t[:, 128:256], pattern=[[1, 128]], compare_op=ALU.is_ge, fill=0.0, base=0, channel_multiplier=-1), r=[m_], w=[m_])
                k.op("pool", lambda e, mt_=mt_: e.affine_select(out=mt_.t[:], in_=mt_.t[:], pattern=[[-1, 128]], compare_op=ALU.is_gt, fill=0.0, base=0, channel_multiplier=1), r=[mt_], w=[mt_])
            else:
                k.op("pool", lambda e, m_=m_: e.affine_select(out=m_.t[:, 0:128], in_=m_.t[:, 0:128], pattern=[[-1, 128]], compare_op=ALU.is_gt, fill=0.0, base=0, channel_multiplier=1), r=[m_], w=[m_])
                k.op("pool", lambda e, m_=m_: e.affine_select(out=m_.t[:, 128:256], in_=m_.t[:, 128:256], pattern=[[-1, 128]], compare_op=ALU.is_ge, fill=0.0, base=0, channel_multiplier=1), r=[m_], w=[m_])
                k.op("pool", lambda e, mt_=mt_: e.affine_select(out=mt_.t[:], in_=mt_.t[:], pattern=[[1, 128]], compare_op=ALU.is_gt, fill=0.0, base=0, channel_multiplier=-1), r=[mt_], w=[mt_])
            MASK.append(m_)
            MASKT.append(mt_)
        RESET = k.sb("RESET", [128, 4, 128], F32)
        k.op("pool", lambda e: e.memset(RESET.t[:], 1.0), w=[RESET])
        k.op("pool", lambda e: e.memset(RESET.t[:, :, 0:1], 0.0), r=[RESET], w=[RESET])

        def f32t(name, shape=(128, 4, 128)):
            return k.sb(name, list(shape), F32)

        RWW = k.sb("RWW", [128, 15, 130], F32)
        RWF = f32t("RWF", (128, 15, 128))
        KQ, RN, KK, ZS, SG, AA, LF = [f32t(n) for n in ("KQ", "RN", "KK", "ZS", "SG", "AA", "LF")]
        E1, E2, E3, E4, TT_, KD, BB, KT32, NBH, KHH = [f32t(n) for n in ("E1", "E2", "E3", "E4", "TT", "KD", "BB", "KT32", "NBH", "KHH")]
        AS_ = ZS
        LI = ZS
        LE = KQ
        D4 = RN
        RKD = TT_
        GC = k.sb("GC", [128, 4], F32)
        DG = f32t("DG", (128, 4, 64))
        DGhr = k.ring("DGh", [128, 4, 64], BF16, 2)
        DGlr = k.ring("DGl", [128, 4, 64], BF16, 2)
        SQb = k.sb("SQb", [128, 4, 128], BF16)
        TW = k.sb("TW", [128, 128], BF16)
        ALb = k.sb("ALb", [128, 128], BF16)
        SGLr = k.ring("SGL", [128, 128], BF16, 2)
        NBb = k.sb("NBb", [128, 4, 128], BF16)
        KTb = k.sb("KTb", [128, 4, 128], BF16)
        KTILb = k.sb("KTILb", [128, 4, 128], BF16)
        RKDbr = k.ring("RKDb", [128, 4, 128], BF16, 2)
        CATr = k.ring("CAT", [128, 4, 256], BF16, 2)
        KTMr, NBHMr, KHMr, VMbr = [k.ring(n, [128, 512], BF16, 2) for n in ("KTM", "NBHM", "KHM", "VMb")]
        VM32r = k.ring("VM32", [128, 512], F32, 2)
        ABrr = k.ring("ABr", [128, 8, 256], BF16, 2)
        AKrr = k.ring("AKr", [128, 8, 256], BF16, 2)
        Y0r = k.ring("Y0", [128, 8, 128], BF16, 2)
        XA = k.ring("XA", [128, 8, 128], BF16, 2)
        YA = k.ring("YA", [128, 8, 128], BF16, 2)
        TTr = k.ring("TTr", [128, 8, 128], BF16, 2)
        KHAT, AVb, UH = [k.sb(n, [128, 512], BF16) for n in ("KHAT", "AVb", "UH")]
        YH = k.sb("YH", [128, 512], F32)
        ZH = k.sb("ZH", [64, 512], F32)
        MT = k.sb("MT", [64, 512], F32)
        RHT = k.sb("RHT", [64, 8, 128], BF16)
        PS_ = k.ring("PS", [64, 512], F32, 2)
        Pb = k.sb("Pb", [64, 512], BF16)
        YO = k.sb("YO", [128, 512], F32)
        CB = k.sb("CB", [128, 8], F32)
        YF = k.sb("YF", [128, 512], F32)
        st8 = [k.sb("st8_%d" % i, [128, 8], F32) for i in range(3)]
        CEN = k.sb("CEN", [128, 512], F32)
        SQ5 = k.sb("SQ5", [128, 512], F32)
        BON = SQ5
        YG = CEN
        YGT = k.sb("YGT", [128, 4, 128], BF16)
        OAt = YF
        OAT = k.sb("OAT", [128, 4, 128], BF16)
        OCT = k.sb("OCT", [128, 4, 128], BF16)
        GT = k.sb("GT", [128, 24, 128], BF16)
        XT = k.sb("XT", [128, 8, 128], F32)
        PROD = k.sb("PROD", [128, 3, 128], F32)
        MS = k.sb("MS", [128, 128], F32)
        MRG = k.sb("MRG", [128, 8, 128], BF16)
        X1 = k.sb("X1", [128, 8, 128], F32)

        def flat(t_):
            return t_.t[:].rearrange("p a b -> p (a b)")

        def rwkv_chunk(u, c, d, st, state_only=False, first=False):
            T = units[u]
            t0 = tok0[u] + c * 128
            CAT, KTM, NBHM, KHM, VMb, VM32 = CATr.next(), KTMr.next(), NBHMr.next(), KHMr.next(), VMbr.next(), VM32r.next()
            ABr, AKr, Y0, DGh, DGl, RKDb, SGL = ABrr.next(), AKrr.next(), Y0r.next(), DGhr.next(), DGlr.next(), RKDbr.next(), SGLr.next()
            lo = 1 if c == 0 else 0
            hi = 129 if c == T // 128 - 1 else 130
            if lo == 1:
                k.op("pool", lambda e: e.memset(RWW.t[:, :, 0:1], 0.0), w=[RWW])
            if hi == 129:
                k.op("pool", lambda e: e.memset(RWW.t[:, :, 129:130], 0.0), w=[RWW])
            k.dma("sp", RWW.t[:, :, lo:hi], rw_s[:, t0 - 1 + lo:t0 - 1 + hi].rearrange("(c p) t -> p c t", p=128), r=[DR("rw")], w=[RWW])
            k.op("dve", lambda e: e.tensor_tensor(RWF.t[:], RWW.t[:, :, 1:129], bc3(C0.t[:, :], 128), ALU.mult), r=[RWW, C0], w=[RWF])
            for gi, (tmp, j0, j1) in enumerate(((E1, 0, 4), (E2, 4, 8), (E3, 8, 12), (E4, 12, 15))):
                n_ = j1 - j0
                k.op("pool", lambda e, tmp=tmp, j0=j0, j1=j1, n_=n_: e.tensor_tensor(tmp.t[:, 0:n_, :], RWW.t[:, j0:j1, 0:128], bc3(V("mu_p", j0, n_), 128), ALU.mult), r=[RWW, VEC], w=[tmp])
                k.op("dve", lambda e, tmp=tmp, j0=j0, j1=j1, n_=n_: e.tensor_tensor(RWF.t[:, j0:j1, :], RWF.t[:, j0:j1, :], tmp.t[:, 0:n_, :], ALU.add), r=[RWF, tmp], w=[RWF])
                k.op("pool", lambda e, tmp=tmp, j0=j0, j1=j1, n_=n_: e.tensor_tensor(tmp.t[:, 0:n_, :], RWW.t[:, j0:j1, 2:130], bc3(V("mu_n", j0, n_), 128), ALU.mult), r=[RWW, VEC, tmp], w=[tmp])
                k.op("dve", lambda e, tmp=tmp, j0=j0, j1=j1, n_=n_: e.tensor_tensor(RWF.t[:, j0:j1, :], RWF.t[:, j0:j1, :], tmp.t[:, 0:n_, :], ALU.add), r=[RWF, tmp], w=[RWF])
            yield
            chk("r1")
            r_ = RWF.t[:, 0:4, :]
            kx = RWF.t[:, 4:8, :]
            vx = RWF.t[:, 8:12, :]
            k.op("pool", lambda e: e.tensor_tensor(KQ.t[:], kx, X4("k_k"), ALU.mult), r=[RWF, VEC], w=[KQ])
            k.act(SQb.t[:], KQ.t[:], AF.Square, r=[KQ], w=[SQb])
            b = nb()
            k.mm(banks[b].t[:, b, :], BD.t[:], flat(SQb), True, True, r=[BD, SQb], w=[banks[b]])
            k.act(flat(RN), banks[b].t[:, b, :], AF.Sqrt, r=[banks[b]], w=[RN])
            k.op("dve", lambda e: e.tensor_scalar(RN.t[:], RN.t[:], 1e-12, None, ALU.max), r=[RN], w=[RN])
            k.op("dve", lambda e: e.reciprocal(RN.t[:], RN.t[:]), r=[RN], w=[RN])
            k.op("dve", lambda e: e.tensor_tensor(KK.t[:], KQ.t[:], RN.t[:], ALU.mult), r=[KQ, RN], w=[KK])
            yield
            chk("r2")
            k.act(TW.t[:], RWF.t[:, 12, :], AF.Tanh, r=[RWF], w=[TW])
            k.op("pool", lambda e: e.tensor_copy(ALb.t[:], RWF.t[:, 13, :]), r=[RWF], w=[ALb])
            dr = slice(0, 64) if d == 0 else slice(64, 128)
            sfx = "_f" if d == 0 else "_b"
            b = nb()
            for j in range(4):
                k.mm(banks[b].t[:, b, j * 128:(j + 1) * 128], W2.t[dr, j * 128:(j + 1) * 128], TW.t[dr, :], True, True, r=[W2, TW], w=[banks[b]])
            k.op("dve", lambda e, b=b: e.tensor_tensor(ZS.t[:], banks[b].t[:, b, :].rearrange("p (a b) -> p a b", b=128), X4("w0" + sfx), ALU.add), r=[banks[b], VEC], w=[ZS])
            k.act(SG.t[:], ZS.t[:], AF.Sigmoid, r=[ZS], w=[SG])
            b = nb()
            for j in range(4):
                k.mm(banks[b].t[:, b, j * 128:(j + 1) * 128], A2.t[dr, j * 128:(j + 1) * 128], ALb.t[dr, :], True, True, r=[A2, ALb], w=[banks[b]])
            k.op("dve", lambda e, b=b: e.tensor_tensor(AS_.t[:], banks[b].t[:, b, :].rearrange("p (a b) -> p a b", b=128), X4("a0" + sfx), ALU.add), r=[banks[b], VEC], w=[AS_])
            k.act(AA.t[:], AS_.t[:], AF.Sigmoid, r=[AS_], w=[AA])
            yield
            k.op("dve", lambda e: e.tensor_tensor_scan(flat(LF), flat(RESET), flat(SG), 0.0, ALU.mult, ALU.add), r=[RESET, SG], w=[LF])
            TOT = LF.t[:, :, 127:128]
            if d == 0:
                LIa = LF
            else:
                k.op("dve", lambda e: e.tensor_tensor(D4.t[:], SG.t[:], LF.t[:], ALU.subtract), r=[SG, LF], w=[D4])
                k.op("dve", lambda e: e.tensor_tensor(LI.t[:], D4.t[:], TOT.to_broadcast([128, 4, 128]), ALU.add), r=[D4, LF], w=[LI])
                LIa = LI
            k.op("pool", lambda e: e.tensor_tensor(LE.t[:], LIa.t[:], SG.t[:], ALU.subtract), r=[LIa, SG], w=[LE])
            k.op("pool", lambda e: e.tensor_tensor(D4.t[:], TOT.to_broadcast([128, 4, 128]), LIa.t[:], ALU.subtract), r=[LF, LIa, D4], w=[D4])
            k.act(E1.t[:], LIa.t[:], AF.Exp, r=[LIa], w=[E1], scale=DECAY_C)
            k.act(E2.t[:], LIa.t[:], AF.Exp, r=[LIa], w=[E2], scale=-DECAY_C)
            k.act(E3.t[:], LE.t[:], AF.Exp, r=[LE], w=[E3], scale=DECAY_C)
            k.act(E4.t[:], D4.t[:], AF.Exp, r=[D4], w=[E4], scale=DECAY_C)
            k.act(GC.t[:].unsqueeze(2), TOT, AF.Exp, r=[LF], w=[GC], scale=DECAY_C)
            yield
            chk("r3")
            k.op("dve", lambda e: e.tensor_tensor(TT_.t[:], AA.t[:], X4("k_a"), ALU.mult), r=[AA, VEC], w=[TT_])
            k.op("dve", lambda e: e.tensor_tensor(TT_.t[:], TT_.t[:], X4("omk"), ALU.add), r=[TT_, OMK], w=[TT_])
            k.op("pool", lambda e: e.tensor_tensor(KD.t[:], kx, TT_.t[:], ALU.mult), r=[RWF, TT_], w=[KD])
            k.op("pool", lambda e: e.tensor_tensor(BB.t[:], KK.t[:], AA.t[:], ALU.mult), r=[KK, AA], w=[BB])
            k.op("dve", lambda e: e.tensor_tensor(KT32.t[:], KK.t[:], E3.t[:], ALU.mult), r=[KK, E3], w=[KT32])
            k.act(CAT.t[:, :, 0:128], KT32.t[:], AF.Copy, r=[KT32], w=[CAT])
            k.op("pool", lambda e: e.tensor_copy(KTb.t[:], KT32.t[:]), r=[KT32], w=[KTb])
            k.op("dve", lambda e: e.scalar_tensor_tensor(NBb.t[:], BB.t[:], -1.0, E2.t[:], ALU.mult, ALU.mult), r=[BB, E2], w=[NBb])
            k.op("pool", lambda e: e.tensor_tensor(KTILb.t[:], KD.t[:], E2.t[:], ALU.mult), r=[KD, E2], w=[KTILb])
            k.op("pool", lambda e: e.tensor_tensor(CAT.t[:, :, 128:256], r_, E1.t[:], ALU.mult), r=[RWF, E1, CAT], w=[CAT])
            k.op("dve", lambda e: e.scalar_tensor_tensor(NBH.t[:], BB.t[:], -1.0, E4.t[:], ALU.mult, ALU.mult), r=[BB, E4], w=[NBH])
            k.op("pool", lambda e: e.tensor_tensor(KHH.t[:], KD.t[:], E4.t[:], ALU.mult), r=[KD, E4], w=[KHH])
            k.op("dve", lambda e: e.tensor_tensor(DG.t[0:64, :, :], ident.t[0:64, 0:64].unsqueeze(1).to_broadcast([64, 4, 64]),
                                                  GC.t[0:64, :].unsqueeze(2).to_broadcast([64, 4, 64]), ALU.mult), r=[ident, GC], w=[DG])
            k.op("dve", lambda e: e.tensor_tensor(DG.t[64:128, :, :], ident.t[64:128, 64:128].unsqueeze(1).to_broadcast([64, 4, 64]),
                                                  GC.t[64:128, :].unsqueeze(2).to_broadcast([64, 4, 64]), ALU.mult), r=[ident, GC, DG], w=[DG])
            k.op("pool", lambda e: e.tensor_copy(DGh.t[:], DG.t[:]), r=[DG], w=[DGh])
            k.op("pool", lambda e: e.tensor_tensor(DGl.t[:], DG.t[:], DGh.t[:], ALU.subtract), r=[DG, DGh], w=[DGl])
            for nm_, t_ in (("SG", SG), ("LF", LF), ("LE", LE), ("D4", D4), ("E1", E1), ("E2", E2), ("E3", E3), ("E4", E4), ("KK", KK), ("AA", AA),
                            ("KT32", KT32), ("NBH", NBH), ("KHH", KHH), ("KD", KD), ("RWF", RWF), ("DG", DG)):
                dbg(nm_, t_, t_.t[:])
            dbg("GC", GC, GC.t[:])
            k.op("pool", lambda e: e.tensor_tensor(RKD.t[:], r_, KD.t[:], ALU.mult), r=[RWF, KD], w=[RKD])
            k.op("pool", lambda e: e.tensor_tensor(RKDb.t[:], RKD.t[:], X4("r_k"), ALU.mult), r=[RKD, VEC], w=[RKDb])
            k.act(SGL.t[:], RWF.t[:, 14, :], AF.Sigmoid, r=[RWF], w=[SGL])
            yield
            chk("r4")
            for src, dst, dst32 in ((KT32.t, KTM, None), (NBH.t, NBHM, None), (KHH.t, KHM, None), (RWF.t, VMb, VM32)):
                b = nb()
                for j in range(4):
                    sap = src[:, 8 + j, :] if dst is VMb else src[:, j, :]
                    rr = [RWF] if dst is VMb else [KT32, NBH, KHH]
                    if KVAR != "notr":
                        k.tr(banks[b].t[:, b, j * 128:(j + 1) * 128], sap, ident, r=rr, w=[banks[b]], inc=(j == 3))
                if KVAR == "dvecp":
                    k.op("dve", lambda e, b=b, dst=dst: e.tensor_copy(dst.t[:], banks[b].t[:, b, :]), r=[banks[b]], w=[dst])
                elif KVAR == "actcp":
                    k.op("act", lambda e, b=b, dst=dst: e.copy(dst.t[:], banks[b].t[:, b, :]), r=[banks[b]], w=[dst])
                elif KVAR != "nocp":
                    k.act(dst.t[:], banks[b].t[:, b, :], AF.Copy, r=[banks[b]], w=[dst])
                if dst32 is not None and KVAR != "nocp":
                    k.act(dst32.t[:], banks[b].t[:, b, :], AF.Copy, r=[banks[b]], w=[dst32])
                yield
            chk("r5")
            for q in range(2):
                for par in range(2):
                    hr = slice(64 * par, 64 * par + 64)
                    b1 = nb()
                    b2 = nb()
                    for i in range(2):
                        hp = 2 * q + i
                        k.mm(banks[b1].t[:, b1, i * 256:(i + 1) * 256], NBb.t[hr, hp, :], CAT.t[hr, hp, :], True, True, r=[NBb, CAT], w=[banks[b1]])
                        k.mm(banks[b2].t[:, b2, i * 256:(i + 1) * 256], KTILb.t[hr, hp, :], CAT.t[hr, hp, :], True, True, r=[KTILb, CAT], w=[banks[b2]])
                    mk_ = MASK[d].t[:, :].unsqueeze(1).to_broadcast([128, 2, 256])
                    h0 = 4 * q + par
                    k.op("dve", lambda e, b1=b1, h0=h0, mk_=mk_: e.tensor_tensor(ABr.t[:, h0:h0 + 3:2, :], banks[b1].t[:, b1, :].rearrange("p (a b) -> p a b", b=256), mk_, ALU.mult),
                         r=[banks[b1], MASK[d]], w=[ABr])
                    k.op("dve", lambda e, b2=b2, h0=h0, mk_=mk_: e.tensor_tensor(AKr.t[:, h0:h0 + 3:2, :], banks[b2].t[:, b2, :].rearrange("p (a b) -> p a b", b=256), mk_, ALU.mult),
                         r=[banks[b2], MASK[d]], w=[AKr])
                    yield
            for par in range(2):
                hr = slice(64 * par, 64 * par + 64)
                b = nb()
                for i in range(4):
                    k.mm(banks[b].t[:, b, i * 128:(i + 1) * 128], KTb.t[hr, i, :], NBb.t[hr, i, :], True, True, r=[KTb, NBb], w=[banks[b]])
                k.op("dve", lambda e, b=b, par=par, Y0=Y0: e.tensor_tensor(Y0.t[:, par::2, :], banks[b].t[:, b, :].rearrange("p (a b) -> p a b", b=128),
                                                                    MASKT[d].t[:, :].unsqueeze(1).to_broadcast([128, 4, 128]), ALU.mult),
                     r=[banks[b], MASKT[d]], w=[Y0])
            chk("r6")
            yield "S2"
            if first:
                P_cur = PS_.next()
                k.op("pool", lambda e: e.memset(P_cur.t[:], 0.0), w=[P_cur])
            else:
                P_cur = st["P"]
            X0 = XA.next()
            k.op("pool", lambda e, X0=X0: e.tensor_copy(X0.t[:], ABr.t[:, :, 0:128]), r=[ABr], w=[X0])
            Tc = TTr.next()
            k.op("dve", lambda e, Tc=Tc: e.tensor_tensor(Tc.t[:], ABr.t[:, :, 0:128], identb.t[:, :].unsqueeze(1).to_broadcast([128, 8, 128]), ALU.add), r=[ABr, identb], w=[Tc])
            Xc, Yc = X0, Y0
            for lvl in range(6):
                Xn = XA.next()
                Yn = YA.next()
                last = lvl == 5
                for g4 in range(2):
                    bx = nb()
                    by = nb()
                    for hh in range(4):
                        h = g4 * 4 + hh
                        if not last:
                            k.mm(banks[bx].t[:, bx, hh * 128:(hh + 1) * 128], Yc.t[:, h, :], Xc.t[:, h, :], True, True, r=[Yc, Xc], w=[banks[bx]])
                        k.mm(banks[by].t[:, by, hh * 128:(hh + 1) * 128], Xc.t[:, h, :], Yc.t[:, h, :], True, True, r=[Yc, Xc], w=[banks[by]])
                    if not last:
                        k.act(Xn.t[:, 4 * g4:4 * g4 + 4, :], banks[bx].t[:, bx, :].rearrange("p (a b) -> p a b", b=128), AF.Copy, r=[banks[bx]], w=[Xn])
                    k.act(Yn.t[:, 4 * g4:4 * g4 + 4, :], banks[by].t[:, by, :].rearrange("p (a b) -> p a b", b=128), AF.Copy, r=[banks[by]], w=[Yn])
                Tn = TTr.next()
                for g4 in range(2):
                    bt = nb()
                    for hh in range(4):
                        h = g4 * 4 + hh
                        k.mm(banks[bt].t[:, bt, hh * 128:(hh + 1) * 128], Yn.t[:, h, :], Tc.t[:, h, :], True, True, r=[Yn, Tc], w=[banks[bt]])
                    k.op("dve", lambda e, bt=bt, g4=g4, Tn=Tn, Tc=Tc: e.tensor_tensor(Tn.t[:, 4 * g4:4 * g4 + 4, :], banks[bt].t[:, bt, :].rearrange("p (a b) -> p a b", b=128),
                                                                            Tc.t[:, 4 * g4:4 * g4 + 4, :], ALU.add), r=[banks[bt], Tc], w=[Tn])
                Tc, Xc, Yc = Tn, Xn, Yn
                yield
            chk("r7")
            b = nb()
            for h in range(8):
                k.mm(banks[b].t[:, b, h * 64:(h + 1) * 64], Tc.t[:, h, :], KTM.t[:, h * 64:(h + 1) * 64], True, True, r=[Tc, KTM], w=[banks[b]])
            k.act(KHAT.t[:], banks[b].t[:, b, :], AF.Copy, r=[banks[b]], w=[KHAT])
            yield
            b = nb()
            for h in range(8):
                k.mm(banks[b].t[:, b, h * 64:(h + 1) * 64], AKr.t[:, h, 0:128], VMb.t[:, h * 64:(h + 1) * 64], True, True, r=[AKr, VMb], w=[banks[b]])
            k.act(AVb.t[:], banks[b].t[:, b, :], AF.Copy, r=[banks[b]], w=[AVb])
            yield
            b = nb()
            for h in range(8):
                k.mm(banks[b].t[:, b, h * 64:(h + 1) * 64], Tc.t[:, h, :], AVb.t[:, h * 64:(h + 1) * 64], True, True, r=[Tc, AVb], w=[banks[b]])
            k.act(UH.t[:], banks[b].t[:, b, :], AF.Copy, r=[banks[b]], w=[UH])
            yield
            if not state_only:
                b = nb()
                for h in range(8):
                    k.mm(banks[b].t[:, b, h * 64:(h + 1) * 64], ABr.t[:, h, 128:256], UH.t[:, h * 64:(h + 1) * 64], True, False, r=[ABr, UH], w=[banks[b]])
                    k.mm(banks[b].t[:, b, h * 64:(h + 1) * 64], AKr.t[:, h, 128:256], VMb.t[:, h * 64:(h + 1) * 64], False, True, r=[AKr, VMb], w=[banks[b]])
                k.act(YH.t[:], banks[b].t[:, b, :], AF.Copy, r=[banks[b]], w=[YH])
            b = nb()
            for h in range(8):
                k.mm(banks[b].t[0:64, b, h * 64:(h + 1) * 64], NBHM.t[:, h * 64:(h + 1) * 64], UH.t[:, h * 64:(h + 1) * 64], True, False, r=[NBHM, UH], w=[banks[b]])
                k.mm(banks[b].t[0:64, b, h * 64:(h + 1) * 64], KHM.t[:, h * 64:(h + 1) * 64], VMb.t[:, h * 64:(h + 1) * 64], False, True, r=[KHM, VMb], w=[banks[b]])
            k.act(ZH.t[:], banks[b].t[0:64, b, :], AF.Copy, r=[banks[b]], w=[ZH])
            yield
            b = nb()
            for h in range(8):
                hr = slice(64 * (h % 2), 64 * (h % 2) + 64)
                k.mm(banks[b].t[0:64, b, h * 64:(h + 1) * 64], KHAT.t[:, h * 64:(h + 1) * 64], NBHM.t[:, h * 64:(h + 1) * 64], True, False, r=[KHAT, NBHM], w=[banks[b]])
                k.mm(banks[b].t[0:64, b, h * 64:(h + 1) * 64], identb.t[hr, hr], DGh.t[hr, h // 2, :], False, False, r=[identb, DGh], w=[banks[b]])
                k.mm(banks[b].t[0:64, b, h * 64:(h + 1) * 64], identb.t[hr, hr], DGl.t[hr, h // 2, :], False, True, r=[identb, DGl], w=[banks[b]])
            k.act(MT.t[:], banks[b].t[0:64, b, :], AF.Copy, r=[banks[b]], w=[MT])
            yield
            for g4 in range(0 if state_only else 2):
                b = nb()
                for hh in range(4):
                    h = g4 * 4 + hh
                    hr = slice(64 * (h % 2), 64 * (h % 2) + 64)
                    k.mm(banks[b].t[0:64, b, hh * 128:(hh + 1) * 128], KHAT.t[:, h * 64:(h + 1) * 64], ABr.t[:, h, 128:256], True, False, r=[KHAT, ABr], w=[banks[b]])
                    k.mm(banks[b].t[0:64, b, hh * 128:(hh + 1) * 128], identb.t[hr, hr], CAT.t[hr, h // 2, 128:256], False, True, r=[identb, CAT], w=[banks[b]])
                k.act(RHT.t[:, 4 * g4:4 * g4 + 4, :], banks[b].t[0:64, b, :].rearrange("p (a b) -> p a b", b=128), AF.Copy, r=[banks[b]], w=[RHT])
            chk("r8")
            if not state_only:
                k.act(Pb.t[:], P_cur.t[:], AF.Copy, r=[P_cur], w=[Pb])
                b = nb()
                for h in range(8):
                    k.mm(banks[b].t[:, b, h * 64:(h + 1) * 64], RHT.t[:, h, :], Pb.t[:, h * 64:(h + 1) * 64], True, True, r=[RHT, Pb], w=[banks[b]])
                k.op("dve", lambda e, b=b: e.tensor_tensor(YO.t[:], banks[b].t[:, b, :], YH.t[:], ALU.add), r=[banks[b], YH], w=[YO])
            P_new = PS_.next()
            b = nb()
            for h in range(8):
                k.mm(banks[b].t[0:64, b, h * 64:(h + 1) * 64], MT.t[:, h * 64:(h + 1) * 64], P_cur.t[:, h * 64:(h + 1) * 64], True, True, r=[MT, P_cur], w=[banks[b]])
            k.op("dve", lambda e, b=b, P_new=P_new: e.tensor_tensor(P_new.t[:], banks[b].t[0:64, b, :], ZH.t[:], ALU.add), r=[banks[b], ZH], w=[P_new])
            st["P"] = P_new
            if state_only:
                return
            yield
            b = nb()
            for j in range(4):
                k.mm(banks[b].t[:, b, 2 * j:2 * j + 2], RKDb.t[:, j, :], HSEL.t[:], True, True, r=[RKDb, HSEL], w=[banks[b]])
            k.act(CB.t[:], banks[b].t[:, b, 0:8], AF.Copy, r=[banks[b]], w=[CB])
            k.op("dve", lambda e: e.tensor_tensor(BON.t[:].rearrange("p (a b) -> p a b", b=64), VM32.t[:].rearrange("p (a b) -> p a b", b=64),
                                                  CB.t[:, :].unsqueeze(2).to_broadcast([128, 8, 64]), ALU.mult), r=[VM32, CB], w=[BON])
            k.op("pool", lambda e: e.tensor_tensor(YO.t[:], YO.t[:], BON.t[:], ALU.add), r=[YO, BON], w=[YO])
            chk("r9")
            if d == 0:
                k.dma("pool", yf_s[t0:t0 + 128, :], YO.t[:], r=[YO], w=[DR("yf")])
            else:
                merge_chunk(u, c, SGL)
                chk("m1")

        def merge_chunk(u, c, SGL):
            T = units[u]
            t0 = tok0[u] + c * 128
            k.dma("sp", YF.t[:], yf_s[t0:t0 + 128, :], r=[DR("yf")], w=[YF])
            k.op("dve", lambda e: e.tensor_tensor(YO.t[:], YO.t[:], YF.t[:], ALU.add), r=[YO, YF], w=[YO])
            y3 = YO.t[:].rearrange("p (a b) -> p a b", b=64)
            mean, var, rs_ = st8
            k.op("dve", lambda e: e.tensor_reduce(mean.t[:], y3, AX.X, ALU.add), r=[YO], w=[mean])
            k.op("dve", lambda e: e.tensor_scalar(mean.t[:], mean.t[:], 1.0 / 64, None, ALU.mult), r=[mean], w=[mean])
            c3 = CEN.t[:].rearrange("p (a b) -> p a b", b=64)
            k.op("dve", lambda e: e.tensor_tensor(c3, y3, mean.t[:, :].unsqueeze(2).to_broadcast([128, 8, 64]), ALU.subtract), r=[YO, mean], w=[CEN])
            k.op("pool", lambda e: e.tensor_tensor(SQ5.t[:], CEN.t[:], CEN.t[:], ALU.mult), r=[CEN], w=[SQ5])
            k.op("dve", lambda e: e.tensor_reduce(var.t[:], SQ5.t[:].rearrange("p (a b) -> p a b", b=64), AX.X, ALU.add), r=[SQ5], w=[var])
            rstd(rs_.t[:], var.t[:], 1.0 / 64, 64e-5, [var], [rs_])
            k.op("dve", lambda e: e.tensor_tensor(c3, c3, rs_.t[:, :].unsqueeze(2).to_broadcast([128, 8, 64]), ALU.mult), r=[CEN, rs_], w=[CEN])
            k.op("pool", lambda e: e.tensor_tensor(CEN.t[:], CEN.t[:], LNG.t[:], ALU.mult), r=[CEN, LNG], w=[CEN])
            k.op("pool", lambda e: e.tensor_tensor(CEN.t[:], CEN.t[:], LNB.t[:], ALU.add), r=[CEN, LNB], w=[CEN])
            b = nb()
            k.mm(banks[b].t[:, b, :], SGL.t[:], G2.t[:], True, True, r=[SGL, G2], w=[banks[b]])
            k.op("dve", lambda e, b=b: e.tensor_tensor(YG.t[:], CEN.t[:], banks[b].t[:, b, :], ALU.mult), r=[CEN, banks[b]], w=[YG])
            b = nb()
            for j in range(4):
                k.tr(banks[b].t[:, b, j * 128:(j + 1) * 128], YG.t[:, j * 128:(j + 1) * 128], ident, r=[YG], w=[banks[b]])
            k.act(YGT.t[:], banks[b].t[:, b, :].rearrange("p (a b) -> p a b", b=128), AF.Copy, r=[banks[b]], w=[YGT])
            k.dma("sp", OAt.t[:], oa_s[t0:t0 + 128, :], r=[DR("oa")], w=[OAt])
            b = nb()
            for j in range(4):
                k.tr(banks[b].t[:, b, j * 128:(j + 1) * 128], OAt.t[:, j * 128:(j + 1) * 128], ident, r=[OAt], w=[banks[b]])
            k.act(OAT.t[:], banks[b].t[:, b, :].rearrange("p (a b) -> p a b", b=128), AF.Copy, r=[banks[b]], w=[OAT])
            k.dma("sp", OCT.t[:], ocT_s[:, t0:t0 + 128].rearrange("(c p) t -> p c t", p=128), r=[DR("oc")], w=[OCT])
            k.dma("sp", GT.t[:], gate_s[:, t0:t0 + 128].rearrange("(c p) t -> p c t", p=128), r=[DR("gate")], w=[GT])
            k.dma("sp", XT.t[:], xT_s[:, t0:t0 + 128].rearrange("(c p) t -> p c t", p=128), r=[DR("xT")], w=[XT])
            for dc in range(8):
                b = nb()
                for i, (Wt, At) in enumerate(((Woa, OAT), (Wob, YGT), (Woc, OCT))):
                    for kc in range(4):
                        k.mm(banks[b].t[:, b, i * 128:(i + 1) * 128], Wt.t[:, kc, dc * 128:(dc + 1) * 128], At.t[:, kc, :], kc == 0, kc == 3, r=[Wt, At], w=[banks[b]])
                k.op("dve", lambda e, b=b, dc=dc: e.tensor_tensor(PROD.t[:], banks[b].t[:, b, 0:384].rearrange("p (a b) -> p a b", b=128), GT.t[:, dc::8, :], ALU.mult),
                     r=[banks[b], GT], w=[PROD])
                k.op("pool", lambda e: e.tensor_tensor(MS.t[:], PROD.t[:, 0, :], PROD.t[:, 1, :], ALU.add), r=[PROD], w=[MS])
                k.op("pool", lambda e, dc=dc: e.tensor_tensor(MRG.t[:, dc, :], MS.t[:], PROD.t[:, 2, :], ALU.add), r=[MS, PROD], w=[MRG])
            for dc in range(8):
                b = nb()
                for kc in range(8):
                    k.mm(banks[b].t[:, b, 0:128], Wo.t[:, kc, dc * 128:(dc + 1) * 128], MRG.t[:, kc, :], kc == 0, kc == 7, r=[Wo, MRG], w=[banks[b]])
                k.op("dve", lambda e, b=b, dc=dc: e.tensor_tensor(X1.t[:, dc, :], banks[b].t[:, b, 0:128], XT.t[:, dc, :], ALU.add), r=[banks[b], XT], w=[X1])
            k.dma("pool", x1T_s[:, t0:t0 + 128].rearrange("(c p) t -> p c t", p=128), X1.t[:], r=[X1], w=[DR("x1T")])

        tasks = []
        st = {"P": None}
        for u in range(NU):
            ncn = units[u] // 128
            nmix = MIX[u] // 128
            for d in range(2):
                order = list(range(nmix)) if d == 0 else list(range(ncn - 1, -1, -1))
                for i, c in enumerate(order):
                    tasks.append((u, c, d, c >= nmix, i == 0))
        prev = None
        for tk in tasks + [None]:
            cur = rwkv_chunk(tk[0], tk[1], tk[2], st, state_only=tk[3], first=tk[4]) if tk is not None else None
            cur_s1_done = cur is None
            prev_done = prev is None
            while not (cur_s1_done and prev_done):
                if not prev_done:
                    try:
                        next(prev)
                    except StopIteration:
                        prev_done = True
                if not cur_s1_done:
                    if next(cur) == "S2":
                        cur_s1_done = True
            prev = cur

    chk("p45")
    with k.scope():
        Wup = k.sb("Wup", [128, 8, 2 * DFF], BF16)
        Wdn = k.sb("Wdn", [128, NFC, D], BF16)
        stage_scope = k.scope()
        stage_scope.__enter__()
        stage[0] = k.ring("stg", [128, 1024], F32, 2)
        for kc in range(8):
            load_w(Wup, lambda c0, n, kc=kc: Wup.t[:, kc, c0:c0 + n], w_up[kc * 128:(kc + 1) * 128, :], 2 * DFF, gain=V("g_ffn", kc))
        for fc in range(NFC):
            load_w(Wdn, lambda c0, n, fc=fc: Wdn.t[:, fc, c0:c0 + n], w_dn[fc * 128:(fc + 1) * 128, :], D)
        stage_scope.__exit__(None, None, None)
        NBK = 256
        XW = k.sb("XW", [128, 8, NBK + 2], F32)
        SQ7 = k.sb("SQ7", [128, 8, NBK + 2], BF16)
        H2 = k.sb("H2", [128, 8, NBK + 2], BF16)
        RS7 = k.sb("RS7", [128, NBK + 2], F32)
        ACTT = k.sb("ACTT", [128, NFC, NBK], BF16)
        CT = k.ring("CT", [128, NBK], F32, 2)
        GG = k.ring("GG", [128, NBK], F32, 2)
        YOut = k.ring("YOut", [128, D], F32, 2)
        for u in range(NU):
            T = units[u]
            for blk in range(OWN[u] // NBK):
                t0 = tok0[u] + blk * NBK
                o0 = own0[u] + blk * NBK
                lo = 1 if blk == 0 else 0
                hi = NBK + 1 if blk == T // NBK - 1 else NBK + 2
                if lo == 1:
                    k.op("pool", lambda e: e.memset(XW.t[:, :, 0:1], 0.0), w=[XW])
                if hi == NBK + 1:
                    k.op("pool", lambda e: e.memset(XW.t[:, :, NBK + 1:NBK + 2], 0.0), w=[XW])
                k.dma("sp", XW.t[:, :, lo:hi], x1T_s[:, t0 - 1 + lo:t0 - 1 + hi].rearrange("(c p) t -> p c t", p=128), r=[DR("x1T")], w=[XW])
                k.op("pool", lambda e: e.tensor_tensor(SQ7.t[:], XW.t[:], XW.t[:], ALU.mult), r=[XW], w=[SQ7])
                b = nb()
                for kc in range(8):
                    k.mm(banks[b].t[:, b, 0:NBK + 2], onesb.t[:], SQ7.t[:, kc, :], kc == 0, kc == 7, r=[onesb, SQ7], w=[banks[b]])
                rstd(RS7.t[:], banks[b].t[:, b, 0:NBK + 2], 1.0 / D, EPS, [banks[b]], [RS7])
                k.op("dve", lambda e: e.tensor_tensor(H2.t[:], XW.t[:], RS7.t[:, :].unsqueeze(1).to_broadcast([128, 8, NBK + 2]), ALU.mult), r=[XW, RS7], w=[H2])
                for fc in range(NFC):
                    bg = nb()
                    for kc in range(8):
                        k.mm(banks[bg].t[:, bg, 0:NBK + 2], Wup.t[:, kc, fc * 128:(fc + 1) * 128], H2.t[:, kc, :], kc == 0, kc == 7, r=[Wup, H2], w=[banks[bg]])
                    bv = nb()
                    for kc in range(8):
                        k.mm(banks[bv].t[:, bv, 0:NBK], Wup.t[:, kc, DFF + fc * 128:DFF + (fc + 1) * 128], H2.t[:, kc, 1:NBK + 1], kc == 0, kc == 7, r=[Wup, H2], w=[banks[bv]])
                    ct = CT.next()
                    gg = GG.next()
                    k.act(ct.t[:], banks[bg].t[:, bg, 1:NBK + 1], AF.Identity, r=[banks[bg], VEC], w=[ct], scale=V("cw1", fc), bias=V("cb", fc))
                    k.op("dve", lambda e, bg=bg, ct=ct, fc=fc: e.scalar_tensor_tensor(ct.t[:], banks[bg].t[:, bg, 0:NBK], V("cw0", fc), ct.t[:], ALU.mult, ALU.add), r=[banks[bg], ct, VEC], w=[ct])
                    k.op("dve", lambda e, bg=bg, ct=ct, fc=fc: e.scalar_tensor_tensor(ct.t[:], banks[bg].t[:, bg, 2:NBK + 2], V("cw2", fc), ct.t[:], ALU.mult, ALU.add), r=[banks[bg], ct, VEC], w=[ct])
                    k.act(gg.t[:], ct.t[:], AF.Gelu, r=[ct], w=[gg])
                    k.op("dve", lambda e, bv=bv, gg=gg, fc=fc: e.tensor_tensor(ACTT.t[:, fc, :], gg.t[:], banks[bv].t[:, bv, 0:NBK], ALU.mult), r=[gg, banks[bv]], w=[ACTT])
                for dc in range(8):
                    b = nb()
                    for fc in range(NFC):
                        k.mm(banks[b].t[:, b, 0:NBK], Wdn.t[:, fc, dc * 128:(dc + 1) * 128], ACTT.t[:, fc, :], fc == 0, fc == NFC - 1, r=[Wdn, ACTT], w=[banks[b]])
                    k.op("dve", lambda e, b=b, dc=dc: e.tensor_tensor(XW.t[:, dc, 1:NBK + 1], banks[b].t[:, b, 0:NBK], XW.t[:, dc, 1:NBK + 1], ALU.add), r=[banks[b], XW], w=[XW])
                for j in range(NBK // 128):
                    yo = YOut.next()
                    for half in range(2):
                        b = nb()
                        for q in range(4):
                            dc = half * 4 + q
                            k.tr(banks[b].t[:, b, q * 128:(q + 1) * 128], XW.t[:, dc, 1 + j * 128:1 + (j + 1) * 128], ident, r=[XW], w=[banks[b]])
                        if half == 0:
                            k.act(yo.t[:, 0:512], banks[b].t[:, b, :], AF.Copy, r=[banks[b]], w=[yo])
                        else:
                            k.op("dve", lambda e, b=b, yo=yo: e.tensor_copy(yo.t[:, 512:1024], banks[b].t[:, b, :]), r=[banks[b]], w=[yo])
                    k.dma("pool", ys[o0 + j * 128:o0 + (j + 1) * 128, :], yo.t[:], r=[yo], w=[DR("ys")], is_out=True)
    k.finish()
    return nc, k, locals()


def _cols(v, n):
    v = np.asarray(v, np.float32).reshape(-1)
    if v.size < n * 128:
        v = np.concatenate([v, np.zeros(n * 128 - v.size, np.float32)])
    return v.reshape(n, 128).T


_SWAP = {"mu_prev": "mu_next", "mu_next": "mu_prev", "w0_f": "w0_b", "w0_b": "w0_f", "a0_f": "a0_b", "a0_b": "a0_f",
         "w2_f": "w2_b", "w2_b": "w2_f", "a2_f": "a2_b", "a2_b": "a2_f"}


def host_pack(inp, tlens, rev=False):
    if isinstance(tlens, int):
        tlens = [tlens]

    def g(n):
        if rev and n in _SWAP:
            n = _SWAP[n]
        a = np.asarray(inp[n], np.float32)[0]
        if rev and n == "conv_w":
            a = a[::-1]
        return a
    vec = {}
    vec["g_mix"] = _cols(g("norm_mix_g"), 8)
    vec["g_q"] = _cols(g("q_norm_g"), 3)
    vec["g_kv"] = _cols(g("kv_norm_g"), 2)
    qn, kn = g("mla_qn_g"), g("mla_kn_g")
    rf = np.concatenate([np.arange(64, 96), np.arange(0, 64)])
    sw = np.concatenate([np.arange(80, 96), np.arange(64, 80)])
    vec["gq_r"] = _cols(qn[rf], 1)
    vec["gq_sw"] = _cols(qn[sw], 1)
    vec["gk_r"] = _cols(kn[rf], 1)
    vec["gk_sw"] = _cols(kn[sw], 1)
    rwperm = np.arange(1920)
    if rev:
        rwperm = np.concatenate([np.arange(0, 1536), np.arange(1600, 1664), np.arange(1536, 1600),
                                 np.arange(1728, 1792), np.arange(1664, 1728), np.arange(1792, 1920)])
    vec["mu_p"] = _cols(g("mu_prev")[rwperm], 15)
    vec["mu_n"] = _cols(g("mu_next")[rwperm], 15)
    for n in ("w0_f", "w0_b", "a0_f", "a0_b", "k_k", "k_a", "r_k"):
        vec[n] = _cols(g(n), 4)
    vec["g_mem"] = _cols(g("mem_norm_g"), 8)
    vec["g_xq"] = _cols(g("x_qn_g"), 1)
    vec["g_xk"] = _cols(g("x_kn_g"), 1)
    vec["g_ffn"] = _cols(g("norm_ffn_g"), 8)
    cw = g("conv_w")
    vec["cw0"] = _cols(cw[0], NFC)
    vec["cw1"] = _cols(cw[1], NFC)
    vec["cw2"] = _cols(cw[2], NFC)
    vec["cb"] = _cols(g("conv_b"), NFC)
    vecs = np.concatenate([vec[n] for n, _ in VEC_SPEC], axis=1).astype(np.float32)
    assert vecs.shape == (128, NVEC)
    w_in = g("w_in")
    w_in = np.concatenate([w_in[:, :672], w_in[:, 672:2592][:, rwperm], w_in[:, 2592:]], axis=1)
    w_in_ext = np.concatenate([w_in, w_in[:, 656:672], w_in[:, 640:656]], axis=1)
    w_uq = g("w_uq").reshape(384, 8, 96)
    w_uq_p = np.concatenate([w_uq[:, :, rf].reshape(384, 768), w_uq[:, :, sw].reshape(384, 256)], axis=1)
    w_ukv = g("w_ukv").reshape(256, 8, 128)
    w_ukvk = np.concatenate([np.zeros((256, 8, 32), np.float32), w_ukv[:, :, 0:64]], axis=2).reshape(256, 768)
    w_ukvv = w_ukv[:, :, 64:128].reshape(256, 512)
    w_mkv = g("w_mkv").reshape(D, 4, 256)
    half = 16
    inv = np.power(np.float32(10000.0), -np.arange(half, dtype=np.float32) / half).astype(np.float32)
    shared = {}
    for t_ in tlens:
        pos = np.arange(t_, dtype=np.float32)
        if rev:
            pos = pos[::-1]
        ang = pos[None, :] * inv[:, None]
        cos, sin = np.cos(ang).astype(np.float32), np.sin(ang).astype(np.float32)
        shared["ropec%d" % t_] = np.concatenate([cos, cos], 0)
        shared["ropes%d" % t_] = np.concatenate([-sin, sin], 0)
    shared.update({
        "vecs": vecs,
        "rowb": np.stack([np.tile(g("lnx_g")[None, :], (128, 1)), np.tile(g("lnx_b")[None, :], (128, 1))]).astype(np.float32),
        "w_in": w_in_ext, "w_uq": w_uq_p, "w_ukvk": w_ukvk, "w_ukvv": w_ukvv,
        "w_mk": w_mkv[:, :, 0:128].reshape(D, 512), "w_mv": w_mkv[:, :, 128:256].reshape(D, 512),
        "w2": np.concatenate([g("w2_f"), g("w2_b")], 0), "a2": np.concatenate([g("a2_f"), g("a2_b")], 0), "g2": g("g2"),
        "w_oa": g("w_o_a"), "w_ob": g("w_o_b"), "w_oc": g("w_o_c"), "w_out": g("w_out"),
        "w_up": g("w_up"), "w_dn": g("w_down"),
    })
    return {kk: np.ascontiguousarray(v, dtype=np.float32) for kk, v in shared.items()}


N_CORES = 8
UNITS = [2048, 2048, 2048, 2048, (8192, 4096, 4224, 9)]
_PROG = {}


def kernel(**inputs):
    xp = np.asarray(inputs["x_prompt"], np.float32)
    xsm = np.asarray(inputs["x_sample"], np.float32)
    mp = np.asarray(inputs["mem_prompt"], np.float32)
    msm = np.asarray(inputs["mem_sample"], np.float32)
    if "nc" not in _PROG:
        _PROG["nc"] = build_program(UNITS, 8192)[0]
    nc = _PROG["nc"]
    packs = [host_pack(inputs, [2048, 8192], rev=False), host_pack(inputs, [2048, 8192], rev=True)]
    in_maps = []
    for c in range(N_CORES):
        s_, rev = c // 2, c % 2 == 1
        seqs = [xp[4 * c + i] for i in range(4)] + [xsm[s_]]
        if rev:
            seqs = [a[::-1] for a in seqs]
        mems = np.concatenate([mp[4 * c + i] for i in range(4)] + [msm[s_]], axis=0)
        m = dict(packs[1 if rev else 0])
        m["xs"] = np.ascontiguousarray(np.concatenate(seqs, axis=0))
        m["mems"] = np.ascontiguousarray(mems)
        in_maps.append(m)
    res = run_bass_kernel_spmd(nc, in_maps, core_ids=list(range(N_CORES)))
    y_prompt = np.empty_like(xp)
    y_sample = np.empty_like(xsm)
    for c in range(N_CORES):
        s_, rev = c // 2, c % 2 == 1
        ys = np.asarray(res.results[c]["ys"], np.float32)
        for i in range(4):
            blk = ys[i * 2048:(i + 1) * 2048]
            y_prompt[4 * c + i] = blk[::-1] if rev else blk
        half = ys[8192:8192 + 4096]
        if rev:
            y_sample[s_, 4096:] = half[::-1]
        else:
            y_sample[s_, :4096] = half
    return (y_prompt, y_sample)
```
